# Optimizing a Trainium2 kernel written in Bass

```python
import math
import jax
import jax.numpy as jnp
from jax import lax
import numpy as np

D_MODEL = 1024
BATCH = 8
SEQ = 4096
DEPTH = 1
DEC_BATCH = 32
DEC_SEQ = 4
PAST_LEN = 16384
PAGE_SIZE = 128

A_GROUPS = ((128, 1), (512, 4), (2048, 16))
A_HEADS_PER_GROUP = 8
A_HEAD_DIM = 64
A_N_HEADS = len(A_GROUPS) * A_HEADS_PER_GROUP
A_QKV_WIDTH = A_N_HEADS * A_HEAD_DIM
A_OUT_WIDTH = A_HEADS_PER_GROUP * A_HEAD_DIM
A_QBLOCK = 128
N_BUCKETS = 32
BUCKET_MAX_DIST = 2048
B_HEADS = 8
B_HEAD_DIM = 128
B_WIDTH = B_HEADS * B_HEAD_DIM
CONV_W = 4
B_CHUNK = 64
ALPHA = (2 * DEPTH) ** 0.25
BETA_INIT = (8 * DEPTH) ** -0.25
LN_EPS = 1e-5
NORM_EPS = 1e-6
PROJ_SPLITS = (A_QKV_WIDTH, A_QKV_WIDTH, A_QKV_WIDTH, A_OUT_WIDTH, 3 * B_WIDTH, B_WIDTH, B_HEADS, B_HEADS, D_MODEL, D_MODEL)
PROJ_WIDTH = sum(PROJ_SPLITS)

kernel_name = 'hybrid_dilated_gdn_decoder_step'


def t5_causal_buckets(dist):
    max_exact = N_BUCKETS // 2
    dist = np.asarray(dist, dtype=np.int64)
    ratio = np.maximum(dist, max_exact) / max_exact
    large = max_exact + (np.log(ratio) / math.log(BUCKET_MAX_DIST / max_exact) * (N_BUCKETS - max_exact)).astype(np.int64)
    return np.where(dist < max_exact, dist, np.minimum(large, N_BUCKETS - 1)).astype(np.int32)


def group_biases(rel_bias):
    out = []
    for gi, (window, dil) in enumerate(A_GROUPS):
        buckets = t5_causal_buckets(dil * np.arange(window // dil + 1))
        cols = rel_bias[buckets][:, gi * A_HEADS_PER_GROUP:(gi + 1) * A_HEADS_PER_GROUP]
        out.append(cols.T)
    return out


def dilated_attend(q, k_slab, v_slab, pos0, dilation, bias_hj):
    n_q = q.shape[1]
    window = k_slab.shape[1] - n_q
    n_keys = window // dilation + 1
    idx = (window + np.arange(n_q)[:, None] - dilation * np.arange(n_keys)[None, :]).astype(np.int32)
    valid = (pos0 + idx) >= 0
    k_sel = k_slab[:, idx]
    v_sel = v_slab[:, idx]
    logits = jnp.einsum('bqhd,bqjhd->bqhj', q, k_sel, preferred_element_type=jnp.float32) * (A_HEAD_DIM ** -0.5)
    logits = logits + bias_hj.astype(jnp.float32)
    logits = jnp.where(valid[:, None, :], logits, -jnp.inf)
    m = jnp.max(logits, axis=-1, keepdims=True)
    p = jnp.exp(logits - m)
    s = jnp.sum(p, axis=-1, keepdims=True)
    out = jnp.einsum('bqhj,bqjhd->bqhd', p / s, v_sel.astype(jnp.float32))
    lse = (m + jnp.log(s))[..., 0]
    return out, lse


def dilated_prompt(q, k, v, window, dilation, bias_hj):
    bsz, seq, nh, hd = q.shape
    pad = ((0, 0), (window, 0), (0, 0), (0, 0))
    kp = jnp.pad(k, pad)
    vp = jnp.pad(v, pad)

    def block(t0):
        qb = lax.dynamic_slice_in_dim(q, t0, A_QBLOCK, axis=1)
        ks = lax.dynamic_slice_in_dim(kp, t0, window + A_QBLOCK, axis=1)
        vs = lax.dynamic_slice_in_dim(vp, t0, window + A_QBLOCK, axis=1)
        return dilated_attend(qb, ks, vs, t0 - window, dilation, bias_hj)

    starts = jnp.arange(seq // A_QBLOCK, dtype=jnp.int32) * A_QBLOCK
    out, lse = lax.map(block, starts)
    out = jnp.moveaxis(out, 0, 1).reshape(bsz, seq, nh, hd)
    lse = jnp.moveaxis(lse, 0, 1).reshape(bsz, seq, nh)
    return out, lse


def combine_groups(outs, lses):
    weights = jax.nn.softmax(jnp.stack(lses, 0), axis=0)
    return jnp.einsum('gbth,gbthd->bthd', weights, jnp.stack(outs, 0))


def short_conv(ext, w):
    n = ext.shape[1] - (CONV_W - 1)
    return sum(ext[:, i:i + n] * w[i] for i in range(CONV_W))


def l2norm(t):
    return t * lax.rsqrt(jnp.sum(t * t, axis=-1, keepdims=True) + NORM_EPS)


def gated_delta_chunk(state, q, k, v, g, beta):
    n = q.shape[2]
    causal = np.tril(np.ones((n, n), dtype=bool))
    strict = np.tril(np.ones((n, n), dtype=bool), -1)
    cum = jnp.cumsum(g, axis=-1)
    decay = jnp.exp(jnp.where(causal, cum[..., :, None] - cum[..., None, :], -jnp.inf))
    kk = jnp.einsum('bhid,bhjd->bhij', k, k)
    lower = jnp.where(strict, beta[..., :, None] * kk * decay, 0.0) + jnp.eye(n, dtype=q.dtype)
    rhs = jnp.concatenate([beta[..., None] * v, beta[..., None] * k * jnp.exp(cum)[..., None]], axis=-1)
    sol = lax.linalg.triangular_solve(lower, rhs, left_side=True, lower=True, unit_diagonal=True)
    u, w = sol[..., :B_HEAD_DIM], sol[..., B_HEAD_DIM:]
    v_new = u - jnp.einsum('bhid,bhde->bhie', w, state)
    qk = jnp.einsum('bhid,bhjd->bhij', q, k) * decay
    out = jnp.einsum('bhid,bhde->bhie', q * jnp.exp(cum)[..., None], state) + jnp.einsum('bhij,bhje->bhie', qk, v_new)
    tail = jnp.exp(cum[..., -1:] - cum)
    new_state = jnp.exp(cum[..., -1])[..., None, None] * state + jnp.einsum('bhid,bhie->bhde', k * tail[..., None], v_new)
    return new_state, out


def gated_delta(state, q, k, v, g, beta):
    bsz, nh, seq = g.shape
    size = B_CHUNK if seq % B_CHUNK == 0 else seq
    n = seq // size

    def chunks(t):
        return jnp.moveaxis(t.reshape((bsz, nh, n, size) + t.shape[3:]), 2, 0)

    state, out = lax.scan(lambda s, xs: gated_delta_chunk(s, *xs), state,
                          (chunks(q), chunks(k), chunks(v), chunks(g), chunks(beta)))
    out = jnp.moveaxis(out, 0, 2).reshape(bsz, nh, seq, B_HEAD_DIM)
    return state, out


def delta_branch(qkv_ext, a_b, b_b, state0, conv_w, a_log, dt_bias):
    qkv = jax.nn.silu(short_conv(qkv_ext, conv_w)).astype(jnp.float32)
    bsz, seq, _ = qkv.shape
    q, k, v = [t.reshape(bsz, seq, B_HEADS, B_HEAD_DIM).transpose(0, 2, 1, 3) for t in jnp.split(qkv, 3, axis=-1)]
    q = l2norm(q) * (B_HEAD_DIM ** -0.5)
    k = l2norm(k)
    beta = jax.nn.sigmoid(b_b.astype(jnp.float32)).transpose(0, 2, 1)
    g = (-jnp.exp(a_log.astype(jnp.float32)) * jax.nn.softplus(a_b.astype(jnp.float32) + dt_bias.astype(jnp.float32))).transpose(0, 2, 1)
    return gated_delta(state0.astype(jnp.float32), q, k, v, g, beta)


def split_columns(p):
    out, start = [], 0
    for width in PROJ_SPLITS:
        out.append(p[..., start:start + width])
        start += width
    return out


def split_heads_a(t):
    return t.reshape(t.shape[:2] + (len(A_GROUPS), A_HEADS_PER_GROUP, A_HEAD_DIM))


def layer_front(x, c, w_cond, b_cond, w_in):
    shift, scale, gate = jnp.split(jax.nn.silu(c) @ w_cond + b_cond, 3, axis=-1)
    h = x * (1 + scale[:, None]) + shift[:, None]
    return gate, split_columns(h @ w_in)


def layer_back(x, gate, o_a, za, o_b, zb, ga, gb, b_norm_w, w_branch_a, w_branch_b, w_out, ln_g, ln_b):
    dtype = x.dtype
    bsz, seq, _ = x.shape
    ya = o_a.reshape(bsz, seq, A_OUT_WIDTH).astype(dtype) * jax.nn.silu(za)
    ob = jnp.swapaxes(o_b, 1, 2)
    ob = ob * lax.rsqrt(jnp.mean(ob * ob, axis=-1, keepdims=True) + NORM_EPS) * b_norm_w.astype(jnp.float32)
    yb = ob.reshape(bsz, seq, B_WIDTH).astype(dtype) * jax.nn.silu(zb)
    merged = jax.nn.sigmoid(ga) * (ya @ w_branch_a) + jax.nn.sigmoid(gb) * (yb @ w_branch_b)
    r = (ALPHA * x + gate[:, None] * (merged @ w_out)).astype(jnp.float32)
    mu = jnp.mean(r, axis=-1, keepdims=True)
    var = jnp.mean(jnp.square(r - mu), axis=-1, keepdims=True)
    return ((r - mu) * lax.rsqrt(var + LN_EPS)).astype(dtype) * ln_g + ln_b


def prompt_layer(x, c, biases, w_cond, b_cond, w_in, conv_w, a_log, dt_bias, b_norm_w, w_branch_a, w_branch_b, w_out, ln_g, ln_b):
    gate, (qa, ka, va, za, qkv_b, zb, a_b, b_b, ga, gb) = layer_front(x, c, w_cond, b_cond, w_in)
    qa, ka, va = split_heads_a(qa), split_heads_a(ka), split_heads_a(va)
    bsz, seq, _ = x.shape
    outs, lses, kv_new = [], [], []
    for gi, (window, dil) in enumerate(A_GROUPS):
        o, l = dilated_prompt(qa[:, :, gi], ka[:, :, gi], va[:, :, gi], window, dil, biases[gi])
        outs.append(o)
        lses.append(l)
        kv_new.append(jnp.stack([ka[:, :, gi], va[:, :, gi]], axis=2)[:, seq - min(window, seq):])
    o_a = combine_groups(outs, lses)
    qkv_ext = jnp.concatenate([jnp.zeros((bsz, CONV_W - 1, qkv_b.shape[-1]), qkv_b.dtype), qkv_b], axis=1)
    state0 = jnp.zeros((bsz, B_HEADS, B_HEAD_DIM, B_HEAD_DIM), jnp.float32)
    state, o_b = delta_branch(qkv_ext, a_b, b_b, state0, conv_w, a_log, dt_bias)
    y = layer_back(x, gate, o_a, za, o_b, zb, ga, gb, b_norm_w, w_branch_a, w_branch_b, w_out, ln_g, ln_b)
    return y, (kv_new[0], kv_new[1], kv_new[2], qkv_ext[:, -(CONV_W - 1):], state.astype(x.dtype))


def sample_layer(x, c, kv_bufs, conv_state, delta_state, biases, w_cond, b_cond, w_in, conv_w, a_log, dt_bias, b_norm_w, w_branch_a, w_branch_b, w_out, ln_g, ln_b):
    gate, (qa, ka, va, za, qkv_b, zb, a_b, b_b, ga, gb) = layer_front(x, c, w_cond, b_cond, w_in)
    qa, ka, va = split_heads_a(qa), split_heads_a(ka), split_heads_a(va)
    outs, lses, kv_new = [], [], []
    for gi, (window, dil) in enumerate(A_GROUPS):
        buf = kv_bufs[gi].astype(x.dtype)
        n_buf = buf.shape[1]
        ext = jnp.concatenate([buf, jnp.stack([ka[:, :, gi], va[:, :, gi]], axis=2)], axis=1)
        ext_p = jnp.pad(ext, ((0, 0), (window - n_buf, 0), (0, 0), (0, 0), (0, 0)))
        o, l = dilated_attend(qa[:, :, gi], ext_p[:, :, 0], ext_p[:, :, 1], PAST_LEN - window, dil, biases[gi])
        outs.append(o)
        lses.append(l)
        kv_new.append(ext[:, ext.shape[1] - n_buf:].astype(kv_bufs[gi].dtype))
    o_a = combine_groups(outs, lses)
    qkv_ext = jnp.concatenate([conv_state.astype(qkv_b.dtype), qkv_b], axis=1)
    state, o_b = delta_branch(qkv_ext, a_b, b_b, delta_state, conv_w, a_log, dt_bias)
    y = layer_back(x, gate, o_a, za, o_b, zb, ga, gb, b_norm_w, w_branch_a, w_branch_b, w_out, ln_g, ln_b)
    return y, (kv_new[0], kv_new[1], kv_new[2], qkv_ext[:, -(CONV_W - 1):].astype(conv_state.dtype), state.astype(delta_state.dtype))


def setup_inputs(seed: int = 0) -> dict:
    key = jax.random.key(seed)
    ks = jax.random.split(key, 24)
    f32 = jnp.float32
    d = D_MODEL

    def nrm(k, shape, s=1.0):
        return jax.random.normal(k, shape, f32) * s

    kv_shape = lambda w: (DEPTH, DEC_BATCH, min(w, PAST_LEN), 2, A_HEADS_PER_GROUP, A_HEAD_DIM)
    dt = jnp.exp(jax.random.uniform(ks[15], (DEPTH, B_HEADS), f32, math.log(1e-3), math.log(1e-1)))
    return {
        'x_prompt': nrm(ks[0], (BATCH, SEQ, d)),
        'x_sample': nrm(ks[1], (DEC_BATCH, DEC_SEQ, d)),
        'c_prompt': nrm(ks[2], (BATCH, d)),
        'c_sample': nrm(ks[3], (DEC_BATCH, d)),
        'cache_kv_w128': nrm(ks[4], kv_shape(A_GROUPS[0][0])),
        'cache_kv_w512': nrm(ks[5], kv_shape(A_GROUPS[1][0])),
        'cache_kv_w2048': nrm(ks[6], kv_shape(A_GROUPS[2][0])),
        'state_conv': nrm(ks[7], (DEPTH, DEC_BATCH, CONV_W - 1, 3 * B_WIDTH)),
        'state_delta': nrm(ks[8], (DEPTH, DEC_BATCH, B_HEADS, B_HEAD_DIM, B_HEAD_DIM), B_HEAD_DIM ** -0.5),
        'w_cond': nrm(ks[9], (DEPTH, d, 3 * d), 0.5 * d ** -0.5),
        'b_cond': nrm(ks[10], (DEPTH, 3 * d), 0.01),
        'w_in': nrm(ks[11], (DEPTH, d, PROJ_WIDTH), d ** -0.5),
        'rel_bias': nrm(ks[12], (N_BUCKETS, A_N_HEADS), 0.5),
        'conv_w': nrm(ks[13], (DEPTH, CONV_W, 3 * B_WIDTH), CONV_W ** -0.5),
        'a_log': jnp.log(jax.random.uniform(ks[14], (DEPTH, B_HEADS), f32, 1.0, 16.0)),
        'dt_bias': dt + jnp.log(-jnp.expm1(-dt)),
        'b_norm_w': 1.0 + nrm(ks[16], (DEPTH, B_HEAD_DIM), 0.01),
        'w_branch_a': nrm(ks[17], (DEPTH, A_OUT_WIDTH, d), BETA_INIT * A_OUT_WIDTH ** -0.5),
        'w_branch_b': nrm(ks[18], (DEPTH, B_WIDTH, d), BETA_INIT * B_WIDTH ** -0.5),
        'w_out': nrm(ks[19], (DEPTH, d, d), BETA_INIT * d ** -0.5),
        'ln_g': 1.0 + nrm(ks[20], (DEPTH, d), 0.01),
        'ln_b': nrm(ks[21], (DEPTH, d), 0.01),
    }


def reference(x_prompt, x_sample, c_prompt, c_sample, cache_kv_w128, cache_kv_w512, cache_kv_w2048, state_conv, state_delta,
              w_cond, b_cond, w_in, rel_bias, conv_w, a_log, dt_bias, b_norm_w, w_branch_a, w_branch_b, w_out, ln_g, ln_b):
    biases = group_biases(rel_bias)
    xp, xs = x_prompt, x_sample
    p_states, s_states = [], []
    for layer in range(DEPTH):
        lw = (w_cond[layer], b_cond[layer], w_in[layer], conv_w[layer], a_log[layer], dt_bias[layer], b_norm_w[layer],
              w_branch_a[layer], w_branch_b[layer], w_out[layer], ln_g[layer], ln_b[layer])
        xp, sp = prompt_layer(xp, c_prompt, biases, *lw)
        xs, ss = sample_layer(xs, c_sample, (cache_kv_w128[layer], cache_kv_w512[layer], cache_kv_w2048[layer]),
                              state_conv[layer], state_delta[layer], biases, *lw)
        p_states.append(sp)
        s_states.append(ss)
    kv128_p, kv512_p, kv2048_p, conv_p, delta_p = [jnp.stack(t, 0) for t in zip(*p_states)]
    kv128_s, kv512_s, kv2048_s, conv_s, delta_s = [jnp.stack(t, 0) for t in zip(*s_states)]
    return (xp, xs, kv128_p, kv512_p, kv2048_p, conv_p, delta_p, kv128_s, kv512_s, kv2048_s, conv_s, delta_s)
```

```python
import contextlib
import math
import numpy as np
import ml_dtypes
import concourse.bass as bass
import concourse.mybir as mybir
from concourse.bass_utils import run_bass_kernel_spmd

F32 = mybir.dt.float32
BF16 = mybir.dt.bfloat16
ALU = mybir.AluOpType
AF = mybir.ActivationFunctionType
AX = mybir.AxisListType
ENGS = ("pe", "act", "dve", "pool", "sp")

D = 1024
SEQ = 4096
NS = 4
ST = 4
NTOK = SEQ + NS * ST
PROJ = 11280
OFF_QA, OFF_KA, OFF_VA, OFF_ZA, OFF_QKVB, OFF_ZB, OFF_AB, OFF_GA, OFF_GB = 0, 1536, 3072, 4608, 5120, 8192, 9216, 9232, 10256
GROUPS = ((128, 1), (512, 4), (2048, 16))
ALPHA = 2 ** 0.25
NEG = -30000.0


class Sem:
    def __init__(self, h, step):
        self.h = h
        self.n = 0
        self.step = step


class Tok:
    __slots__ = ("w", "r", "name", "excl")

    def __init__(self, name="", excl=False):
        self.w = None
        self.r = {}
        self.name = name
        self.excl = excl


class Sched:
    def __init__(self, nc, es):
        self.nc = nc
        self.es = es
        self.prog = {e: [] for e in ENGS}
        self.esem = {e: Sem(es.enter_context(nc.semaphore("sem_" + e)), 1) for e in ENGS if e != "sp"}
        self.seen = {e: {} for e in ENGS}
        self.dsems = []
        self.ninstr = 0

    def dsem(self, name=None):
        s = Sem(self.es.enter_context(self.nc.semaphore(name or ("dsem%d" % len(self.dsems)))), 16)
        self.dsems.append(s)
        return s

    def _need(self, eng, deps):
        for s, v in deps.items():
            if eng == "pe" and s is self.esem["pe"]:
                continue
            if self.seen[eng].get(s, 0) < v:
                self.prog[eng].append(("w", s, v))
                self.seen[eng][s] = v

    def op(self, eng, fn, reads=(), writes=(), dsem=None):
        ex = [t for t in reads if t.excl]
        if ex:
            reads = [t for t in reads if not t.excl]
            writes = list(writes) + [t for t in ex if t not in writes]
        deps = {}
        for t in reads:
            if t.w is not None and deps.get(t.w[0], 0) < t.w[1]:
                deps[t.w[0]] = t.w[1]
        for t in writes:
            if t.w is not None and deps.get(t.w[0], 0) < t.w[1]:
                deps[t.w[0]] = t.w[1]
            for s, v in t.r.items():
                if deps.get(s, 0) < v:
                    deps[s] = v
        self._need(eng, deps)
        sem = dsem if dsem is not None else self.esem[eng]
        sem.n += sem.step
        rec = (sem, sem.n)
        self.prog[eng].append(("i", fn, sem))
        self.ninstr += 1
        for t in reads:
            if t.r.get(sem, 0) < sem.n:
                t.r[sem] = sem.n
        for t in writes:
            t.w = rec
            t.r = {}
        return rec

    def barrier(self, engs=ENGS):
        allsems = list(self.esem.values()) + self.dsems
        for e in engs:
            self._need(e, {s: s.n for s in allsems if s.n > 0})

    def emit(self):
        nc = self.nc
        self.barrier(("sp",))
        with nc.Block() as block:
            def run(engname, e):
                for item in self.prog[engname]:
                    if item[0] == "w":
                        e.wait_ge(item[1].h, item[2])
                    else:
                        item[1](e).then_inc(item[2].h, item[2].step)

            @block.tensor
            def _(e):
                run("pe", e)

            @block.scalar
            def _(e):
                run("act", e)

            @block.vector
            def _(e):
                run("dve", e)

            @block.gpsimd
            def _(e):
                run("pool", e)

            @block.sync
            def _(e):
                run("sp", e)


class K:
    pass


def build(dbg=None, phases="0CBAF", opts=None):
    nc = bass.Bass("TRN2", target_bir_lowering=False)
    k = K()
    k.nc = nc
    k.dbg = dbg or {}
    k.opts = opts or {}
    di = lambda name, shape, dt=F32: nc.dram_tensor(name, list(shape), dt, kind="ExternalInput").ap()
    do = lambda name, shape, dt=F32: nc.dram_tensor(name, list(shape), dt, kind="ExternalOutput").ap()
    dsc = lambda name, shape, dt=F32: nc.dram_tensor(name, list(shape), dt).ap()
    I = k.I = {}
    O = k.O = {}
    I["x"] = di("x", [SEQ, D])
    I["xs"] = di("xs", [NS * ST, D])
    I["cT"] = di("cT", [128, 8, 1 + NS])
    I["kv128"] = di("kv128", [NS, 128, 1024])
    I["kv512"] = di("kv512", [NS, 512, 1024])
    I["kv2048"] = di("kv2048", [NS, 2048, 1024])
    I["sconv"] = di("sconv", [NS * 3, 3072])
    I["sdelta"] = di("sdelta", [NS * 8, 128, 128])
    I["w_cond"] = di("w_cond", [D, 3 * D])
    I["bcondT"] = di("bcondT", [128, 24])
    I["w_in"] = di("w_in", [D, PROJ])
    I["biasT"] = di("biasT", [24, 128, 256])
    I["biasD"] = di("biasD", [24, ST, ST])
    I["convwT"] = di("convwT", [128, 24, 4])
    I["a_log"] = di("a_log", [1, 8])
    I["dt_bias"] = di("dt_bias", [1, 8])
    I["b_norm_w"] = di("b_norm_w", [1, 128])
    I["w_a"] = di("w_a", [512, D])
    I["w_b"] = di("w_b", [D, D])
    I["w_out"] = di("w_out", [D, D])
    I["ln_g"] = di("ln_g", [1, D])
    I["ln_b"] = di("ln_b", [1, D])
    I["cmats"] = di("cmats", [5, 128, 128])
    O["y"] = do("y", [SEQ, D])
    O["ys"] = do("ys", [NS * ST, D])
    O["kvp128"] = do("kvp128", [128, 1024])
    O["kvp512"] = do("kvp512", [512, 1024])
    O["kvp2048"] = do("kvp2048", [2048, 1024])
    O["convp"] = do("convp", [3, 3072])
    O["deltap"] = do("deltap", [8, 128, 128])
    O["kvs128"] = do("kvs128", [NS, 128, 1024])
    O["kvs512"] = do("kvs512", [NS, 512, 1024])
    O["kvs2048"] = do("kvs2048", [NS, 2048, 1024])
    O["convs"] = do("convs", [NS, 3, 3072])
    O["deltas"] = do("deltas", [NS * 8, 128, 128])
    k.hT_d = dsc("hT_d", [D, NTOK], BF16)
    k.ybT_d = dsc("ybT_d", [D, NTOK], BF16)
    k.yaT_d = dsc("yaT_d", [512, NTOK], BF16)
    for name, (shape, dt) in k.dbg.items():
        O[name] = do(name, shape, dt)

    with contextlib.ExitStack() as es:
        S = k.S = Sched(nc, es)
        k.es = es
        k.ps = [es.enter_context(nc.psum_tensor("ps%d" % i, [128, 512], F32)) for i in range(8)]
        k.pst = [Tok("ps%d" % i, excl=True) for i in range(8)]
        k.psi = 0
        k.out_sem = S.dsem("out_sem")
        phase0(k)
        if "C" in phases:
            cache_copies(k)
        if "B" in phases:
            with contextlib.ExitStack() as pes:
                phaseB(k, pes)
                S.barrier()
        if "A" in phases:
            with contextlib.ExitStack() as pes:
                phaseA(k, pes)
                S.barrier()
        if "F" in phases:
            with contextlib.ExitStack() as pes:
                phaseF(k, pes)
                S.barrier()
        S.emit()
    return nc


def psum(k, pool=None):
    pools = getattr(k, "pspools", None)
    if pool is None or pools is None:
        i = k.psi
        k.psi = (i + 1) % 8
        return k.ps[i], k.pst[i]
    lst, idx = pools[pool]
    i = lst[idx % len(lst)]
    pools[pool][1] = idx + 1
    return k.ps[i], k.pst[i]


def sbt(k, es, name, shape, dt):
    t = es.enter_context(k.nc.sbuf_tensor("s_" + name, list(shape), dt))
    return t, Tok(name)


def _l(x):
    return list(x) if isinstance(x, (list, tuple)) else [x]


def dma_in(k, out_ap, in_ap, toks, sem, eng="sp", reads=()):
    k.S.op(eng, lambda e: e.dma_start(out=out_ap, in_=in_ap), reads=_l(reads), writes=_l(toks), dsem=sem)


def dma_out(k, out_ap, in_ap, reads, sem=None, eng="sp", writes=()):
    k.S.op(eng, lambda e: e.dma_start(out=out_ap, in_=in_ap), reads=_l(reads), writes=_l(writes), dsem=sem or k.out_sem)


def MM(k, out, lhsT, rhs, reads, writes, start=True, stop=True):
    k.S.op("pe", lambda e: e.matmul(out, lhsT=lhsT, rhs=rhs, start=start, stop=stop), _l(reads), _l(writes))


def TR(k, out, in_, ident, reads, writes):
    k.S.op("pe", lambda e: e.transpose(out, in_, ident), _l(reads), _l(writes))


def ACT(k, out, in_, func, reads, writes, scale=None, bias=None):
    kw = {}
    if scale is not None:
        kw["scale"] = scale
    if bias is not None:
        kw["bias"] = bias
    k.S.op("act", lambda e: e.activation(out=out, in_=in_, func=func, **kw), _l(reads), _l(writes))


def TT(k, eng, out, in0, in1, op, reads, writes):
    k.S.op(eng, lambda e: e.tensor_tensor(out=out, in0=in0, in1=in1, op=op), _l(reads), _l(writes))


def TS(k, eng, out, in0, s1, op0, reads, writes, s2=None, op1=None):
    if op1 is None:
        k.S.op(eng, lambda e: e.tensor_scalar(out=out, in0=in0, scalar1=s1, scalar2=None, op0=op0), _l(reads), _l(writes))
    else:
        k.S.op(eng, lambda e: e.tensor_scalar(out=out, in0=in0, scalar1=s1, scalar2=s2, op0=op0, op1=op1), _l(reads), _l(writes))


def STT(k, out, in0, scalar, in1, op0, op1, reads, writes):
    k.S.op("dve", lambda e: e.scalar_tensor_tensor(out=out, in0=in0, scalar=scalar, in1=in1, op0=op0, op1=op1), _l(reads), _l(writes))


def CP(k, eng, out, in_, reads, writes):
    if eng == "act":
        ACT(k, out, in_, AF.Identity, reads, writes)
    else:
        k.S.op(eng, lambda e: e.tensor_copy(out=out, in_=in_), _l(reads), _l(writes))


def EV(k, out, in_, reads, writes, scale=None):
    k.rr = getattr(k, "rr", 0) + 1
    if k.rr % 2 == 0:
        ACT(k, out, in_, AF.Copy if not isinstance(scale, (int, float)) or True else AF.Copy, reads, writes, scale=scale)
    else:
        if scale is None:
            CP(k, "dve", out, in_, reads, writes)
        else:
            TS(k, "dve", out, in_, scale, ALU.mult, reads, writes)


def MS(k, eng, ap, val, writes):
    k.S.op(eng, lambda e: e.memset(ap, val), [], _l(writes))


def phase0(k):
    nc, S, es, I = k.nc, k.S, k.es, k.I
    k.cm, k.cm_t = sbt(k, es, "cmats", [128, 5, 128], F32)
    k.ident, k.triu, k.ldiag, k.uincl, k.loff = (k.cm[:, i, :] for i in range(5))
    s0 = S.dsem()
    dma_in(k, k.cm[:], I["cmats"].rearrange("m p f -> p m f"), k.cm_t, s0)
    k.identb, k.identb_t = sbt(k, es, "identb", [128, 128], BF16)
    CP(k, "act", k.identb[:], k.ident, k.cm_t, k.identb_t)
    k.onesf, k.onesf_t = sbt(k, es, "onesf", [128, 128], F32)
    MS(k, "dve", k.onesf[:], 1.0, k.onesf_t)
    k.onesb, k.onesb_t = sbt(k, es, "onesb", [128, 128], BF16)
    MS(k, "dve", k.onesb[:], 1.0, k.onesb_t)
    k.epsc, k.epsc_t = sbt(k, es, "epsc", [128, 3], F32)
    MS(k, "dve", k.epsc[:, 0:1], 1e-6, k.epsc_t)
    MS(k, "dve", k.epsc[:, 1:2], 1.0, k.epsc_t)
    MS(k, "dve", k.epsc[:, 2:3], 1e-5, k.epsc_t)
    k.small, k.small_t = sbt(k, es, "small", [128, 16 + 128], F32)
    s1 = S.dsem()
    s2, s3, s4 = S.dsem(), S.dsem(), S.dsem()
    dma_in(k, k.small[:, 0:8], I["a_log"].partition_broadcast(128), k.small_t, s1)
    dma_in(k, k.small[:, 8:16], I["dt_bias"].partition_broadcast(128), k.small_t, s1)
    dma_in(k, k.small[:, 16:144], I["b_norm_w"].partition_broadcast(128), k.small_t, s1)
    k.negA, k.negA_t = sbt(k, es, "negA", [128, 8], F32)
    ACT(k, k.negA[:], k.small[:, 0:8], AF.Exp, k.small_t, k.negA_t)
    TS(k, "dve", k.negA[:], k.negA[:], -1.0, ALU.mult, k.negA_t, k.negA_t)
    k.dtb = k.small[:, 8:16]
    k.bnw = k.small[:, 16:144]
    k.convw, k.convw_t = sbt(k, es, "convw", [128, 24, 4], F32)
    dma_in(k, k.convw[:], I["convwT"], k.convw_t, s2)
    k.cond, k.cond_t = sbt(k, es, "cond", [128, 24, 1 + NS], F32)
    with contextlib.ExitStack() as tes:
        cT, cT_t = sbt(k, tes, "cT", [128, 8, 1 + NS], F32)
        bc, bc_t = sbt(k, tes, "bcond", [128, 24], F32)
        dma_in(k, cT[:], I["cT"], cT_t, s3)
        dma_in(k, bc[:], I["bcondT"], bc_t, s4)
        ACT(k, cT[:], cT[:], AF.Silu, cT_t, cT_t)
        wst = [sbt(k, tes, "wcst%d" % i, [128, 8, 512], F32) for i in range(2)]
        wss = [S.dsem() for _ in range(2)]
        wv = I["w_cond"].rearrange("(kc p) n -> p kc n", p=128)
        for j in range(6):
            w, w_t = wst[j % 2]
            dma_in(k, w[:], wv[:, :, j * 512:(j + 1) * 512], w_t, wss[j % 2])
            pt, pt_t = psum(k)
            for fc in range(4):
                for kc in range(8):
                    MM(k, pt[:, fc * 8:fc * 8 + 1 + NS], w[:, kc, fc * 128:(fc + 1) * 128], cT[:, kc, :], [w_t, cT_t], pt_t, start=(kc == 0), stop=(kc == 7))
            for fc in range(4):
                f = j * 4 + fc
                TS(k, "dve", k.cond[:, f, :], pt[:, fc * 8:fc * 8 + 1 + NS], bc[:, f:f + 1], ALU.add, [pt_t, bc_t], k.cond_t)
        TS(k, "dve", k.cond[:, 8:16, :], k.cond[:, 8:16, :], 1.0, ALU.add, k.cond_t, k.cond_t)
        S.barrier()
    if "cond" in k.dbg:
        dma_out(k, k.O["cond"], k.cond[:], [k.cond_t])


def build_hT_tile(k, xt, xt_t, ntok, hTt, hTt_t, seqs, pool=None):
    pts = [psum(k, pool), psum(k, pool)]
    for kc in range(8):
        pt, pt_t = pts[kc // 4]
        TR(k, pt[:, (kc % 4) * 128:(kc % 4) * 128 + ntok], xt[0:ntok, kc * 128:(kc + 1) * 128], k.ident[0:ntok, 0:ntok], [xt_t, k.cm_t], pt_t)
    for kc in range(8):
        pt, pt_t = pts[kc // 4]
        for (c0, c1, si) in seqs:
            ACT(k, hTt[:, kc, c0:c1], pt[:, (kc % 4) * 128 + c0:(kc % 4) * 128 + c1], AF.Identity, [pt_t, k.cond_t], hTt_t,
                scale=k.cond[:, 8 + kc, si:si + 1], bias=k.cond[:, kc, si:si + 1])


def load_w_bf16(k, es_tmp, dst, dst_t, col0, ncols, chunk=256, name="wst", src=None, nkc=8):
    S = k.S
    wv = (src if src is not None else k.I["w_in"]).rearrange("(kc p) n -> p kc n", p=128)
    st = [sbt(k, es_tmp, "%s%d" % (name, i), [128, nkc, chunk], F32) for i in range(2)]
    ss = [S.dsem() for _ in range(2)]
    j = 0
    c = 0
    while c < ncols:
        n = min(chunk, ncols - c)
        w, w_t = st[j % 2]
        dma_in(k, w[:, :, 0:n], wv[:, :, col0 + c:col0 + c + n], w_t, ss[j % 2])
        CP(k, "pool" if j % 2 == 0 else "act", dst[:, :, c:c + n], w[:, :, 0:n], w_t, dst_t)
        c += n
        j += 1


def phaseB(k, es):
    nc, S, I, O = k.nc, k.S, k.I, k.O
    NH = 8
    wB, wB_t = sbt(k, es, "wB", [128, 8, 3072], BF16)
    wab, wab_t = sbt(k, es, "wab", [128, 8, 16], BF16)
    with contextlib.ExitStack() as tes:
        load_w_bf16(k, tes, wB, wB_t, OFF_QKVB, 3072)
        load_w_bf16(k, tes, wab, wab_t, OFF_AB, 16, chunk=16, name="wabst")
        S.barrier()
    xts = [sbt(k, es, "xt%d" % i, [128, D], F32) for i in range(2)]
    xsem = [S.dsem() for _ in range(2)]
    hTs = [sbt(k, es, "hTt%d" % i, [128, 8, 128], BF16) for i in range(2)]
    hsem = [S.dsem() for _ in range(2)]
    pc, pc_t = sbt(k, es, "pc", [128, 24, 131], F32)
    cv, cv_t = sbt(k, es, "cv", [128, 24, 128], F32)
    tmpa, tmpa_t = sbt(k, es, "tmpa", [128, 8, 128], F32)
    tmpc, tmpc_t = sbt(k, es, "tmpc", [128, 8, 128], F32)
    cv_ts = [Tok("cv%d" % i) for i in range(3)]
    sq2, sq2_t = sbt(k, es, "sq2", [128, 16, 128], BF16)
    rn, rn_t = sbt(k, es, "rn", [128, 16, 128], F32)
    qkvTs = [sbt(k, es, "qkvT%d" % i, [128, 24, 128], BF16) for i in range(2)]
    kvtoks = [sbt(k, es, "kvtok%d" % i, [128, 16, 128], BF16) for i in range(2)]
    gbs = [sbt(k, es, "gb%d" % i, [128, 56], F32) for i in range(2)]
    Sst, Sst_t = sbt(k, es, "Sst", [128, 8, 128], F32)
    Sbf = [sbt(k, es, "Sbf%d" % i, [128, 8, 128], BF16)[0] for i in range(2)]
    Sh_t = [Tok("S%d" % h) for h in range(NH)]
    Sbf_t = [[Tok("Sbf%d_%d" % (p, h)) for h in range(NH)] for p in range(2)]
    stsem = S.dsem()
    ybT, ybT_t = sbt(k, es, "ybT", [128, 8, 128], BF16)
    ybsem = S.dsem()
    rowmask, rowmask_t = sbt(k, es, "rowmask", [128, 1], F32)
    MS(k, "dve", rowmask[:], 0.0, rowmask_t)
    MS(k, "dve", rowmask[0:ST, :], 1.0, rowmask_t)
    cvsem = S.dsem()
    hist, hist_t = sbt(k, es, "hist", [72, 128], F32)
    histsem = S.dsem()
    NSLOT = 8
    slots = []
    for s in range(NSLOT):
        d = {}
        for nm in ["gTri", "absG", "E", "ecr", "EU"]:
            d[nm] = sbt(k, es, "%s_%d" % (nm, s), [128, 128], F32)
        d["ELd"] = d["gTri"]
        d["ELo"] = d["absG"]
        d["o2"] = d["E"]
        for nm in ["Ad", "Ao", "X0", "X1", "XT0", "XT1", "DT0", "DT1", "qkm", "kt", "qeT"]:
            d[nm] = sbt(k, es, "%s_%d" % (nm, s), [128, 128], BF16)
        d["yb"] = d["Ad"]
        d["nwT"] = d["X0"]
        d["vnew"] = d["X1"]
        d["nNT"] = d["XT0"]
        for nm in ["R", "Xa", "Xb"]:
            d[nm] = sbt(k, es, "%s_%d" % (nm, s), [128, 256], BF16)
        d["st"] = sbt(k, es, "st_%d" % s, [128, 8], F32)
        slots.append(d)

    def T_(h, nm):
        return slots[h % NSLOT][nm][0]

    def Tt(h, nm):
        return slots[h % NSLOT][nm][1]

    MS(k, "pool", Sst[:], 0.0, [Sst_t] + Sh_t)
    MS(k, "pool", Sbf[0][:], 0.0, Sbf_t[0])
    MS(k, "pool", pc[:, :, 0:3], 0.0, pc_t)
    par = [0] * NH

    xv = I["x"].rearrange("(t p) d -> t p d", p=128)
    hTd = k.hT_d.rearrange("(kc p) t -> p kc t", p=128)
    ybd = k.ybT_d.rearrange("(h p) t -> p h t", p=128)
    k.hTd_t = Tok("hTd")
    k.ybd_t = Tok("ybd")

    def load_x(t):
        dma_in(k, xts[t % 2][0][:], xv[t], xts[t % 2][1], xsem[t % 2])

    ntiles = k.opts.get("ntiles", 32)
    nvirt = k.opts.get("nvirt", NS)
    vt_list = list(range(ntiles)) + [("s", s) for s in range(nvirt)]
    load_x(0)
    def front(ti, t):
        virt = isinstance(t, tuple)
        qkvT, qkvT_t = qkvTs[ti % 2]
        kvtok, kvtok_t = kvtoks[ti % 2]
        gb, gb_t = gbs[ti % 2]
        virt = isinstance(t, tuple)
        hTt, hTt_t = hTs[ti % 2]
        xt, xt_t = xts[ti % 2]
        if not virt:
            if t + 1 < ntiles:
                load_x(t + 1)
            build_hT_tile(k, xt, xt_t, 128, hTt, hTt_t, [(0, 128, 0)], pool="f")
            dma_out(k, hTd[:, :, t * 128:(t + 1) * 128], hTt[:], [hTt_t], sem=hsem[ti % 2], writes=[k.hTd_t])
        else:
            s = t[1]
            dma_in(k, xt[0:ST, :], I["xs"][s * ST:(s + 1) * ST, :], xt_t, xsem[ti % 2])
            MS(k, "pool", hTt[:], 0.0, hTt_t)
            build_hT_tile(k, xt, xt_t, ST, hTt, hTt_t, [(0, ST, 1 + s)], pool="f")
            dma_out(k, hTd[:, :, SEQ + s * ST:SEQ + (s + 1) * ST], hTt[:, :, 0:ST], [hTt_t], sem=hsem[ti % 2], writes=[k.hTd_t])
            dma_in(k, hist[:], I["sconv"][s * 3:(s + 1) * 3, :].rearrange("r (c f) -> (r c) f", f=128), hist_t, histsem)
            pt, pt_t = psum(k, "f")
            TR(k, pt[:, 0:72], hist[:], k.ident[0:72, 0:72], [hist_t, k.cm_t], pt_t)
            CP(k, "dve", pc[:, :, 0:3], pt[:, 0:72].rearrange("p (r c) -> p c r", r=3), pt_t, pc_t)
        if k.opts.get('stopB', 9) <= 1:
            return
        yield
        pt, pt_t = psum(k, "f")
        for kc in range(8):
            MM(k, pt[:, 0:16], hTt[:, kc, :], wab[:, kc, :], [hTt_t, wab_t], pt_t, start=(kc == 0), stop=(kc == 7))
        ACT(k, gb[:, 8:16], pt[:, 8:16], AF.Sigmoid, pt_t, gb_t)
        TT(k, "dve", gb[:, 40:48], pt[:, 0:8], k.dtb, ALU.add, [pt_t, k.small_t], gb_t)
        ACT(k, gb[:, 40:48], gb[:, 40:48], AF.Exp, gb_t, gb_t)
        ACT(k, gb[:, 40:48], gb[:, 40:48], AF.Ln, [gb_t, k.epsc_t], gb_t, bias=k.epsc[:, 1:2])
        TT(k, "dve", gb[:, 0:8], gb[:, 40:48], k.negA[:], ALU.mult, [gb_t, k.negA_t], gb_t)
        if virt:
            TS(k, "dve", gb[:, 0:16], gb[:, 0:16], rowmask[:, 0:1], ALU.mult, [gb_t, rowmask_t], gb_t)
        pt, pt_t = psum(k, "f")
        MM(k, pt[:, 0:8], k.triu, gb[:, 0:8], [gb_t, k.cm_t], pt_t)
        CP(k, "dve", gb[:, 16:24], pt[:, 0:8], pt_t, gb_t)
        ACT(k, gb[:, 24:32], gb[:, 16:24], AF.Exp, gb_t, gb_t)
        TS(k, "dve", gb[:, 48:56], pt[:, 0:8], -1.0, ALU.mult, pt_t, gb_t)
        TT(k, "dve", gb[:, 32:40], gb[:, 24:32], gb[:, 8:16], ALU.mult, gb_t, gb_t)
        if k.opts.get('stopB', 9) <= 2:
            return
        yield
        for cg in range(6):
            pt, pt_t = psum(k, "f")
            for c4 in range(4):
                cc = cg * 4 + c4
                for kc in range(8):
                    MM(k, pt[:, c4 * 128:(c4 + 1) * 128], wB[:, kc, cc * 128:(cc + 1) * 128], hTt[:, kc, :], [hTt_t, wB_t], pt_t, start=(kc == 0), stop=(kc == 7))
            CP(k, "act" if cg % 2 == 0 else "dve", pc[:, cg * 4:cg * 4 + 4, 3:131], pt[:].rearrange("p (c f) -> p c f", f=128), pt_t, pc_t)
            yield
        if virt or t == ntiles - 1:
            dstc = O["convs"][t[1]] if virt else O["convp"]
            srcc = pc[:, :, 4:7] if virt else pc[:, :, 128:131]
            for rr in range(3):
                k.S.op("sp", lambda e, o=dstc[rr, :].rearrange("(c p) -> p c", p=128), i_=srcc[:, :, rr]: e.dma_start(out=o, in_=i_, allow_slow_non_contiguous=True), [pc_t], [], dsem=cvsem)
        yield
        for th in range(3):
            cs = slice(th * 8, th * 8 + 8)
            e_ = "pool" if th < 2 else "dve"
            ta, ta_t = (tmpa, tmpa_t) if th < 2 else (tmpc, tmpc_t)
            wb_ = [k.convw[:, cs, i:i + 1].broadcast_to([128, 8, 128]) for i in range(4)]
            TT(k, e_, cv[:, cs, :], pc[:, cs, 0:128], wb_[0], ALU.mult, [pc_t, k.convw_t], cv_ts[th])
            for i in (1, 2, 3):
                TT(k, e_, ta[:], pc[:, cs, i:i + 128], wb_[i], ALU.mult, [pc_t, k.convw_t], ta_t)
                TT(k, e_, cv[:, cs, :], cv[:, cs, :], ta[:], ALU.add, [ta_t, cv_ts[th]], cv_ts[th])
            yield
        CP(k, "pool", pc[:, :, 0:3], pc[:, :, 128:131], pc_t, pc_t)
        for _ in range(k.opts.get('cwait', 10)):
            yield
        for th in (2, 0, 1):
            ACT(k, cv[:, th * 8:th * 8 + 8, :], cv[:, th * 8:th * 8 + 8, :], AF.Silu, cv_ts[th], cv_ts[th])
            yield
        if k.opts.get('stopB', 9) <= 3:
            return
        yield
        TT(k, "pool", sq2[:], cv[:, 0:16, :], cv[:, 0:16, :], ALU.mult, cv_ts[0:2], sq2_t)
        for _ in range(3):
            yield
        for j in range(4):
            pt, pt_t = psum(k, "f")
            MM(k, pt[:], k.onesb[:], sq2[:, j * 4:(j + 1) * 4, :], [sq2_t, k.onesb_t], pt_t)
            ACT(k, rn[:, j * 4:(j + 1) * 4, :], pt[:].rearrange("p (c f) -> p c f", f=128), AF.Ln, [pt_t, k.epsc_t], rn_t, bias=k.epsc[:, 0:1])
        yield
        ACT(k, rn[:], rn[:], AF.Exp, rn_t, rn_t, scale=-0.5)
        yield
        STT(k, qkvT[:, 0:8, :], cv[:, 0:8, :], 128 ** -0.5, rn[:, 0:8, :], ALU.mult, ALU.mult, [cv_ts[0], rn_t], qkvT_t)
        TT(k, "dve", qkvT[:, 8:16, :], cv[:, 8:16, :], rn[:, 8:16, :], ALU.mult, [cv_ts[1], rn_t], qkvT_t)
        CP(k, "act", qkvT[:, 16:24, :], cv[:, 16:24, :], cv_ts[2], qkvT_t)
        yield
        if "qkvT" in k.dbg and ti == k.opts.get("dbg_tile", 0):
            dma_out(k, O["qkvT"], qkvT[:], [qkvT_t])
            dma_out(k, O["gb"], gb[:], [gb_t])
        yield
        for g4 in range(4):
            pt, pt_t = psum(k, "f")
            ptb = pt[:].bitcast(BF16)
            for c4 in range(4):
                TR(k, ptb[:, c4 * 128:(c4 + 1) * 128], qkvT[:, 8 + g4 * 4 + c4, :], k.identb[:], [qkvT_t, k.identb_t], pt_t)
            CP(k, "act" if g4 % 2 == 0 else "dve", kvtok[:, g4 * 4:(g4 + 1) * 4, :], ptb[:, 0:512].rearrange("p (c f) -> p c f", f=128), pt_t, kvtok_t)
        if k.opts.get('stopB', 9) <= 4:
            return

    def units(ti, t):
        virt = isinstance(t, tuple)
        qkvT, qkvT_t = qkvTs[ti % 2]
        kvtok, kvtok_t = kvtoks[ti % 2]
        gb, gb_t = gbs[ti % 2]
        if virt:
            s = t[1]
            dst = O["deltap"] if s == 0 else O["deltas"][(s - 1) * 8:s * 8]
            dma_out(k, dst.rearrange("h k v -> k h v"), Sst[:], [Sst_t] + Sh_t, sem=stsem)
            dma_in(k, Sst[:], I["sdelta"][s * 8:(s + 1) * 8].rearrange("h k v -> k h v"), [Sst_t] + Sh_t, stsem)
            for h in range(NH):
                CP(k, "act", Sbf[par[h]][:, h, :], Sst[:, h, :], Sh_t[h], Sbf_t[par[h]][h])
        yield
        if k.opts.get('stopB', 9) <= 4:
            return
        heads = list(range(NH))

        def stage(mm_fn, ev_fn, chunk=k.opts.get('chunk', 4)):
            for c0 in range(0, NH, chunk):
                banks = {}
                for h in heads[c0:c0 + chunk]:
                    banks[h] = psum(k, "u")
                    mm_fn(h, banks[h][0], banks[h][1])
                for h in heads[c0:c0 + chunk]:
                    ev_fn(h, banks[h][0], banks[h][1])
                yield

        for h in heads:
            EV(k, T_(h, "gTri")[:], k.triu, [k.cm_t, gb_t], Tt(h, "gTri"), scale=gb[:, h:h + 1])
        yield

        def mm(h, pt, pt_t):
            MM(k, pt[:, 0:128], k.onesf[:], T_(h, "gTri")[:], [Tt(h, "gTri"), k.onesf_t], pt_t)

        def ev(h, pt, pt_t):
            ACT(k, T_(h, "absG")[:], pt[:, 0:128], AF.Abs, [pt_t, gb_t], Tt(h, "absG"), bias=gb[:, 48 + h:49 + h])
            ACT(k, T_(h, "ecr")[:], pt[:, 0:128], AF.Exp, pt_t, Tt(h, "ecr"))
            ACT(k, T_(h, "E")[:], T_(h, "absG")[:], AF.Exp, Tt(h, "absG"), Tt(h, "E"), scale=-1.0)
        yield from stage(mm, ev)
        yield
        for h in heads:
            EV(k, T_(h, "kt")[:], kvtok[:, h, :], [kvtok_t, Tt(h, "E")], Tt(h, "kt"), scale=T_(h, "E")[:, 127:128])
            TT(k, "pool", T_(h, "qeT")[:], qkvT[:, h, :], T_(h, "ecr")[:], ALU.mult, [qkvT_t, Tt(h, "ecr")], Tt(h, "qeT"))
            TS(k, "dve", T_(h, "R")[:, 0:128], kvtok[:, 8 + h, :], gb[:, 8 + h:9 + h], ALU.mult, [kvtok_t, gb_t], Tt(h, "R"))
            TS(k, "dve", T_(h, "R")[:, 128:256], kvtok[:, h, :], gb[:, 32 + h:33 + h], ALU.mult, [kvtok_t, gb_t], Tt(h, "R"))
        yield

        def mm(h, pt, pt_t):
            MM(k, pt[:, 0:128], qkvT[:, 8 + h, :], qkvT[:, 8 + h, :], qkvT_t, pt_t)
            MM(k, pt[:, 128:256], qkvT[:, 8 + h, :], qkvT[:, h, :], qkvT_t, pt_t)

        def ev(h, pt, pt_t):
            TT(k, "dve", T_(h, "ELd")[:], pt[:, 0:128], T_(h, "E")[:], ALU.mult, [pt_t, Tt(h, "E")], Tt(h, "ELd"))
            TT(k, "dve", T_(h, "EU")[:], pt[:, 128:256], T_(h, "E")[:], ALU.mult, [pt_t, Tt(h, "E")], Tt(h, "EU"))
            STT(k, T_(h, "Ad")[:], T_(h, "ELd")[:], gb[:, 8 + h:9 + h], k.ldiag, ALU.mult, ALU.mult, [gb_t, Tt(h, "ELd"), k.cm_t], Tt(h, "Ad"))
            STT(k, T_(h, "Ao")[:], T_(h, "ELd")[:], gb[:, 8 + h:9 + h], k.loff, ALU.mult, ALU.mult, [gb_t, Tt(h, "ELd"), k.cm_t], Tt(h, "Ao"))
            TT(k, "pool", T_(h, "qkm")[:], T_(h, "EU")[:], k.uincl, ALU.mult, [Tt(h, "EU"), k.cm_t], Tt(h, "qkm"))
        yield from stage(mm, ev)
        yield

        def mm(h, pt, pt_t):
            TR(k, pt[:].bitcast(BF16)[:, 0:128], T_(h, "Ad")[:], k.identb[:], [Tt(h, "Ad"), k.identb_t], pt_t)

        def ev(h, pt, pt_t):
            ptb = pt[:].bitcast(BF16)[:, 0:128]
            EV(k, T_(h, "XT0")[:], ptb, pt_t, Tt(h, "XT0"))
            TT(k, "dve", T_(h, "DT0")[:], k.identb[:], ptb, ALU.subtract, [pt_t, k.identb_t], Tt(h, "DT0"))
        yield from stage(mm, ev)
        yield
        names = [("Ad", "XT0")] + [("X%d" % (kk % 2), "XT%d" % ((kk + 1) % 2)) for kk in range(4)]
        dts = ["DT0", "DT1", "DT0", "DT1", "DT0"]

        def emit_sq(kk):
            Xc, XTc = names[kk]
            Xn, XTn = names[kk + 1]
            last = kk == 3

            def mm(h, pt, pt_t):
                MM(k, pt[:, 0:128], T_(h, XTc)[:], T_(h, Xc)[:], [Tt(h, XTc), Tt(h, Xc)], pt_t)
                if not last:
                    MM(k, pt[:, 128:256], T_(h, Xc)[:], T_(h, XTc)[:], [Tt(h, XTc), Tt(h, Xc)], pt_t)

            def ev(h, pt, pt_t):
                if last:
                    EV(k, T_(h, Xn)[:], pt[:, 0:128], pt_t, Tt(h, Xn))
                else:
                    e_ = "act" if h % 2 == 0 else "dve"
                    CP(k, e_, T_(h, Xn)[:], pt[:, 0:128], pt_t, Tt(h, Xn))
                    CP(k, e_, T_(h, XTn)[:], pt[:, 128:256], pt_t, Tt(h, XTn))
            yield from stage(mm, ev)

        def emit_dt(kk):
            Xn = names[kk + 1][0]
            DTc_, DTn = dts[kk], dts[kk + 1]

            def mm(h, pt, pt_t):
                MM(k, pt[:, 0:128], T_(h, Xn)[:], T_(h, DTc_)[:], [Tt(h, Xn), Tt(h, DTc_)], pt_t)

            def ev(h, pt, pt_t):
                TT(k, "dve", T_(h, DTn)[:], T_(h, DTc_)[:], pt[:, 0:128], ALU.add, [pt_t, Tt(h, DTc_)], Tt(h, DTn))
            yield from stage(mm, ev)
        for step in (("sq", 0), ("sq", 1), ("dt", 0), ("sq", 2), ("dt", 1), ("sq", 3), ("dt", 2), ("dt", 3)):
            yield from (emit_sq if step[0] == "sq" else emit_dt)(step[1])
        DTc = dts[4]

        def mm(h, pt, pt_t):
            MM(k, pt[:, 0:128], T_(h, "Ao")[:], T_(h, DTc)[:], [Tt(h, "Ao"), Tt(h, DTc)], pt_t)

        def ev(h, pt, pt_t):
            EV(k, T_(h, "nNT")[:], pt[:, 0:128], pt_t, Tt(h, "nNT"), scale=-1.0)
        yield from stage(mm, ev)
        yield
        Xcur = "Xa"
        for it in range(4):
            prev = "Xb" if Xcur == "Xa" else "Xa"

            def mm(h, pt, pt_t):
                MM(k, pt[:, 0:256], T_(h, DTc)[:], T_(h, "R")[:], [Tt(h, DTc), Tt(h, "R")], pt_t, start=True, stop=(it == 0))
                if it > 0:
                    MM(k, pt[:, 0:256], T_(h, "nNT")[:], T_(h, prev)[:], [Tt(h, "nNT"), Tt(h, prev)], pt_t, start=False, stop=True)

            def ev(h, pt, pt_t):
                CP(k, "act" if h % 2 == 0 else "dve", T_(h, Xcur)[:], pt[:, 0:256], pt_t, Tt(h, Xcur))
            yield from stage(mm, ev)
            Xfin = Xcur
            Xcur = prev
            yield

        def mm(h, pt, pt_t):
            TR(k, pt[:].bitcast(BF16)[:, 0:128], T_(h, Xfin)[:, 128:256], k.identb[:], [Tt(h, Xfin), k.identb_t], pt_t)

        def ev(h, pt, pt_t):
            EV(k, T_(h, "nwT")[:], pt[:].bitcast(BF16)[:, 0:128], pt_t, Tt(h, "nwT"), scale=-1.0)
        yield from stage(mm, ev)
        yield

        def mm(h, pt, pt_t):
            MM(k, pt[:, 0:128], T_(h, "nwT")[:], Sbf[par[h]][:, h, :], [Tt(h, "nwT"), Sbf_t[par[h]][h]], pt_t)

        def ev(h, pt, pt_t):
            TT(k, "dve", T_(h, "vnew")[:], T_(h, Xfin)[:, 0:128], pt[:, 0:128], ALU.add, [pt_t, Tt(h, Xfin)], Tt(h, "vnew"))
        yield from stage(mm, ev)
        yield

        def mm(h, pt, pt_t):
            p = par[h]
            MM(k, pt[:, 0:128], T_(h, "qeT")[:], Sbf[p][:, h, :], [Tt(h, "qeT"), Sbf_t[p][h]], pt_t, start=True, stop=False)
            MM(k, pt[:, 0:128], T_(h, "qkm")[:], T_(h, "vnew")[:], [Tt(h, "qkm"), Tt(h, "vnew")], pt_t, start=False, stop=True)
            MM(k, pt[:, 128:256], T_(h, "kt")[:], T_(h, "vnew")[:], [Tt(h, "kt"), Tt(h, "vnew")], pt_t)

        def ev(h, pt, pt_t):
            p = par[h]
            st, st_t = T_(h, "st"), Tt(h, "st")
            STT(k, Sst[:, h, :], Sst[:, h, :], T_(h, "ecr")[:, 127:128], pt[:, 128:256], ALU.mult, ALU.add, [pt_t, Tt(h, "ecr"), Sh_t[h]], Sh_t[h])
            EV(k, Sbf[1 - p][:, h, :], Sst[:, h, :], Sh_t[h], Sbf_t[1 - p][h])
            par[h] = 1 - p
            ACT(k, T_(h, "o2")[:], pt[:, 0:128], AF.Square, pt_t, Tt(h, "o2"))
            k.S.op("dve", lambda e, o=st[:, 0:1], i=T_(h, "o2")[:]: e.tensor_reduce(out=o, in_=i, axis=AX.X, op=ALU.add), [Tt(h, "o2")], [st_t])
            ACT(k, st[:, 1:2], st[:, 0:1], AF.Ln, [st_t, k.epsc_t], st_t, scale=1.0 / 128, bias=k.epsc[:, 0:1])
            ACT(k, st[:, 2:3], st[:, 1:2], AF.Exp, st_t, st_t, scale=-0.5)
            STT(k, T_(h, "yb")[:], pt[:, 0:128], st[:, 2:3], k.bnw, ALU.mult, ALU.mult, [pt_t, st_t, k.small_t], Tt(h, "yb"))
        yield from stage(mm, ev)
        yield

        def mm(h, pt, pt_t):
            TR(k, pt[:].bitcast(BF16)[:, 0:128], T_(h, "yb")[:], k.identb[:], [Tt(h, "yb"), k.identb_t], pt_t)

        def ev(h, pt, pt_t):
            EV(k, ybT[:, h, :], pt[:].bitcast(BF16)[:, 0:128], pt_t, ybT_t)
        yield from stage(mm, ev)
        if not virt:
            dma_out(k, ybd[:, :, t * 128:(t + 1) * 128], ybT[:], [ybT_t], sem=ybsem, writes=[k.ybd_t])
        else:
            s = t[1]
            dma_out(k, ybd[:, :, SEQ + s * ST:SEQ + (s + 1) * ST], ybT[:, :, 0:ST], [ybT_t], sem=ybsem, writes=[k.ybd_t])

    k.pspools = {"u": [[0, 1, 2, 3, 4, 5], 0], "f": [[6, 7], 0]}
    for _ in front(0, vt_list[0]):
        pass
    for ti, t in enumerate(vt_list):
        gf = front(ti + 1, vt_list[ti + 1]) if ti + 1 < len(vt_list) else None
        for _ in units(ti, t):
            if gf is not None:
                if next(gf, "done") == "done":
                    gf = None
        if gf is not None:
            for _ in gf:
                pass
    k.pspools = None
    dst = O["deltap"] if nvirt == 0 else O["deltas"][(nvirt - 1) * 8:nvirt * 8]
    dma_out(k, dst.rearrange("h k v -> k h v"), Sst[:], [Sst_t] + Sh_t, sem=stsem)
    if "ybT" in k.dbg:
        dma_out(k, O["ybT"], k.ybT_d, [k.ybd_t])
class Stg:
    def __init__(self, k, es, nkc, chunk, name):
        self.k = k
        self.nkc = nkc
        self.chunk = chunk
        self.st = [sbt(k, es, "%s%d" % (name, i), [128, nkc, chunk], F32) for i in range(2)]
        self.ss = [k.S.dsem() for _ in range(2)]
        self.j = 0

    def load(self, dst, dst_t, dcol0, src, col0, ncols):
        k = self.k
        wv = src.rearrange("(kc p) n -> p kc n", p=128)
        c = 0
        while c < ncols:
            n = min(self.chunk, ncols - c)
            w, w_t = self.st[self.j % 2]
            dma_in(k, w[:, :, 0:n], wv[:, :, col0 + c:col0 + c + n], w_t, self.ss[self.j % 2])
            CP(k, "pool" if self.j % 2 == 0 else "act", dst[:, :, dcol0 + c:dcol0 + c + n], w[:, :, 0:n], w_t, dst_t)
            c += n
            self.j += 1


def subseq_chunks(d, maxlen):
    L = SEQ // d
    for r in range(d):
        for u0 in range(0, L, maxlen):
            n = min(maxlen, L - u0)
            yield (r * L + u0, r + d * u0, n)


def tslice(tok0, n, d):
    return slice(tok0, tok0 + d * (n - 1) + 1, d)


def cache_copies(k):
    for (W, d) in GROUPS:
        src, dst = k.I["kv%d" % W], k.O["kvs%d" % W]
        for s in range(NS):
            r = 0
            while r < W - ST:
                n = min(256, W - ST - r)
                dma_out(k, dst[s, r:r + n, :], src[s, r + ST:r + ST + n, :], [], eng="pool")
                r += n


def phaseA(k, es):
    nc, S, I, O = k.nc, k.S, k.I, k.O
    hT, hT_t = sbt(k, es, "hT", [128, 8, NTOK], BF16)
    hsem = S.dsem()
    hTd = k.hT_d.rearrange("(kc p) t -> p kc t", p=128)
    for kc in range(8):
        dma_in(k, hT[:, kc, :], hTd[:, kc, :], hT_t, hsem, reads=[k.hTd_t])
    if not k.opts.get("skip_kv"):
        with contextlib.ExitStack() as tes:
            wkv, wkv_t = sbt(k, tes, "wkv", [128, 8, 1024], BF16)
            okv = [sbt(k, tes, "okv%d" % i, [128, 1024], F32) for i in range(2)]
            oks = [S.dsem() for _ in range(2)]
            stg = Stg(k, tes, 8, 256, "kvst")
            j = 0
            for g, (W, d) in enumerate(GROUPS):
                stg.load(wkv, wkv_t, 0, I["w_in"], OFF_KA + g * 512, 512)
                stg.load(wkv, wkv_t, 512, I["w_in"], OFF_VA + g * 512, 512)
                t0 = 32 - W // 128
                for tt in list(range(t0, 32)) + [-1]:
                    M = 128 if tt >= 0 else NS * ST
                    cs = slice(tt * 128, (tt + 1) * 128) if tt >= 0 else slice(SEQ, NTOK)
                    pa, pb = psum(k), psum(k)
                    for half, pp in ((0, pa), (1, pb)):
                        for kc in range(8):
                            MM(k, pp[0][0:M, :], hT[:, kc, cs], wkv[:, kc, half * 512:(half + 1) * 512], [hT_t, wkv_t], pp[1], start=(kc == 0), stop=(kc == 7))
                    o, o_t = okv[j % 2]
                    CP(k, "act", o[0:M, 0:512], pa[0][0:M, :], pa[1], o_t)
                    CP(k, "dve", o[0:M, 512:1024], pb[0][0:M, :], pb[1], o_t)
                    if tt >= 0:
                        dma_out(k, O["kvp%d" % W][(tt - t0) * 128:(tt - t0 + 1) * 128, :], o[:], [o_t], sem=oks[j % 2])
                    else:
                        for s in range(NS):
                            dma_out(k, O["kvs%d" % W][s, W - ST:W, :], o[s * ST:(s + 1) * ST, :], [o_t], sem=oks[j % 2])
                    j += 1
            S.barrier()
    if k.opts.get("skip_attn"):
        return
    UA, UA_t = sbt(k, es, "UaccA", [128, NTOK], F32)
    UB, UB_t = sbt(k, es, "UaccB", [128, NTOK], F32)
    Us = ((UA, UA_t), (UB, UB_t))
    qT, qT_t = sbt(k, es, "qT", [128, NTOK], BF16)
    kT, kT_t = sbt(k, es, "kT", [128, NTOK], BF16)
    vT, vT_t = sbt(k, es, "vT", [128, SEQ], BF16)
    Vaug, Vaug_t = sbt(k, es, "Vaug", [128, 32, 2, 128], BF16)
    Vsn, Vsn_t = sbt(k, es, "Vsn", [ST, NS, 2, 128], BF16)
    vcs = [sbt(k, es, "vc%d" % i, [128, 2, 128], BF16) for i in range(8)]
    kcTs = [sbt(k, es, "kcT%d" % i, [128, 128], BF16) for i in range(8)]
    cks = [sbt(k, es, "ck%d" % i, [128, 2, 128], F32) for i in range(8)]
    cksem = [S.dsem() for _ in range(8)]
    Ps = [sbt(k, es, "Ps%d" % i, [128, 32], BF16) for i in range(8)]
    NPT = 6
    wq, wq_t = sbt(k, es, "wq", [128, 8, 384], BF16)
    wza, wza_t = sbt(k, es, "wza", [128, 8, 128], BF16)
    stg = Stg(k, es, 8, 128, "ast")
    Bt, Bt_t = sbt(k, es, "Bt", [128, 2, 256], F32)
    BD, BD_t = sbt(k, es, "BD", [ST, 2, ST], F32)
    bsem = S.dsem()
    bdsem = S.dsem()
    PT = [[sbt(k, es, "PT%d_%d" % (hd, p), [128, 256], BF16) for p in range(NPT)] for hd in range(2)]
    tbs = [sbt(k, es, "tb%d" % i, [128, 256], F32) for i in range(8)]
    sz, sz_t = sbt(k, es, "sz", [128, 512], F32)
    rec, rec_t = sbt(k, es, "rec", [128, 512], F32)
    t1, t1_t = sbt(k, es, "t1", [128, 512], F32)
    yats = [sbt(k, es, "yat%d" % i, [128, 512], BF16) for i in range(2)]
    yasem = [S.dsem() for _ in range(2)]
    k.yad_t = Tok("yad")
    MS(k, "pool", Vaug[:, :, 0, 64:128], 1.0, Vaug_t)
    MS(k, "pool", Vaug[:, :, 1, 0:64], 1.0, Vaug_t)
    MS(k, "pool", Vsn[:, :, 0, 64:128], 1.0, Vsn_t)
    MS(k, "pool", Vsn[:, :, 1, 0:64], 1.0, Vsn_t)
    for vc, vc_t in vcs:
        MS(k, "pool", vc[:, 0, 64:128], 1.0, vc_t)
        MS(k, "pool", vc[:, 1, 0:64], 1.0, vc_t)
    tbi = 0
    cki = 0
    yi = 0
    nsp = k.opts.get("nsp", 4)
    for sp in range(nsp):
        for g, (W, d) in enumerate(GROUPS):
            L = SEQ // d
            nb = L // 128
            stg.load(wq, wq_t, 0, I["w_in"], OFF_QA + g * 512 + sp * 128, 128)
            stg.load(wq, wq_t, 128, I["w_in"], OFF_KA + g * 512 + sp * 128, 128)
            stg.load(wq, wq_t, 256, I["w_in"], OFF_VA + g * 512 + sp * 128, 128)
            for hd in range(2):
                dma_in(k, Bt[:, hd, :], I["biasT"][g * 8 + 2 * sp + hd], Bt_t, bsem)
                dma_in(k, BD[:, hd, :], I["biasD"][g * 8 + 2 * sp + hd], BD_t, bdsem)
            for c0 in range(0, SEQ, 512):
                for which, dstT, dst_t in ((0, qT, qT_t), (1, kT, kT_t), (2, vT, vT_t)):
                    pt, pt_t = psum(k)
                    for kc in range(8):
                        MM(k, pt[:, 0:512], wq[:, kc, which * 128:(which + 1) * 128], hT[:, kc, c0:c0 + 512], [wq_t, hT_t], pt_t, start=(kc == 0), stop=(kc == 7))
                    ov = dstT[:, 0:SEQ].rearrange("p (r u) -> p r u", r=d)[:, :, c0 // d:(c0 + 512) // d]
                    iv = pt[:, 0:512].rearrange("p (a r) -> p r a", r=d)
                    if which == 0:
                        ACT(k, ov, iv, AF.Copy, pt_t, dst_t, scale=0.125)
                    elif which == 1:
                        CP(k, "dve", ov, iv, pt_t, dst_t)
                    else:
                        EV(k, ov, iv, pt_t, dst_t)
            for which, dstT, dst_t in ((0, qT, qT_t), (1, kT, kT_t)):
                pt, pt_t = psum(k)
                for kc in range(8):
                    MM(k, pt[:, 0:NS * ST], wq[:, kc, which * 128:(which + 1) * 128], hT[:, kc, SEQ:NTOK], [wq_t, hT_t], pt_t, start=(kc == 0), stop=(kc == 7))
                if which == 0:
                    ACT(k, dstT[:, SEQ:NTOK], pt[:, 0:NS * ST], AF.Copy, pt_t, dst_t, scale=0.125)
                else:
                    CP(k, "dve", dstT[:, SEQ:NTOK], pt[:, 0:NS * ST], pt_t, dst_t)
            for n4 in range(8):
                pt, pt_t = psum(k)
                ptb = pt[:].bitcast(BF16)
                for j in range(4):
                    n = n4 * 4 + j
                    TR(k, ptb[:, j * 128:(j + 1) * 128], vT[:, n * 128:(n + 1) * 128], k.identb[:], [vT_t, k.identb_t], pt_t)
                pv = ptb[:, 0:512].rearrange("p (c f) -> p c f", f=128)
                CP(k, "act", Vaug[:, n4 * 4:(n4 + 1) * 4, 0, 0:64], pv[:, :, 0:64], pt_t, Vaug_t)
                CP(k, "dve", Vaug[:, n4 * 4:(n4 + 1) * 4, 1, 64:128], pv[:, :, 64:128], pt_t, Vaug_t)
            pt, pt_t = psum(k)
            for s in range(NS):
                for kc in range(8):
                    MM(k, pt[0:ST, s * 128:(s + 1) * 128], hT[:, kc, SEQ + s * ST:SEQ + (s + 1) * ST], wq[:, kc, 256:384], [wq_t, hT_t], pt_t, start=(kc == 0), stop=(kc == 7))
            pv = pt[0:ST, :].rearrange("p (c f) -> p c f", f=128)
            CP(k, "act", Vsn[:, :, 0, 0:64], pv[:, :, 0:64], pt_t, Vsn_t)
            CP(k, "dve", Vsn[:, :, 1, 64:128], pv[:, :, 64:128], pt_t, Vsn_t)
            def unit_info(kb, hd):
                first = (kb % nb == 0)
                lastb = (kb % nb == nb - 1)
                r, u0 = (kb * 128) // L, (kb * 128) % L
                return first, (128 if lastb else 256), tslice(r + d * u0, 128, d)

            def emit_qk(grp):
                nonlocal tbi
                banks = {}
                for (kb, hd) in grp:
                    first, N, cols = unit_info(kb, hd)
                    hs = slice(64 * hd, 64 * hd + 64)
                    banks[(kb, hd)] = psum(k)
                    pt, pt_t = banks[(kb, hd)]
                    MM(k, pt[:, 0:N], kT[hs, kb * 128:(kb + 1) * 128], qT[hs, kb * 128:kb * 128 + N], [kT_t, qT_t], pt_t)
                tb_of = {}
                for (kb, hd) in grp:
                    first, N, cols = unit_info(kb, hd)
                    pt, pt_t = banks[(kb, hd)]
                    tb_of[(kb, hd)] = tbs[tbi % len(tbs)]
                    tbi += 1
                    tb, tb_t = tb_of[(kb, hd)]
                    TT(k, "dve", tb[:, 0:N], pt[:, 0:N], Bt[:, hd, 0:N], ALU.add, [pt_t, Bt_t], tb_t)
                for (kb, hd) in grp:
                    first, N, cols = unit_info(kb, hd)
                    tb, tb_t = tb_of[(kb, hd)]
                    P, P_t = PT[hd][kb % NPT]
                    ACT(k, P[:, 0:N], tb[:, 0:N], AF.Exp, tb_t, P_t)

            def emit_pv(grp):
                banks = {}
                for (kb, hd) in grp:
                    first, N, cols = unit_info(kb, hd)
                    P, P_t = PT[hd][kb % NPT]
                    banks[(kb, hd)] = psum(k)
                    pu, pu_t = banks[(kb, hd)]
                    if not first:
                        Pp, Pp_t = PT[hd][(kb - 1) % NPT]
                        MM(k, pu[:, 0:128], Vaug[:, kb - 1, hd, :], Pp[:, 128:256], [Vaug_t, Pp_t], pu_t, start=True, stop=False)
                    MM(k, pu[:, 0:128], Vaug[:, kb, hd, :], P[:, 0:128], [Vaug_t, P_t], pu_t, start=first, stop=True)
                for j, (kb, hd) in enumerate(grp):
                    first, N, cols = unit_info(kb, hd)
                    U, U_t = Us[hd]
                    pu, pu_t = banks[(kb, hd)]
                    if g == 0:
                        CP(k, "act" if j % 2 == 0 else "dve", U[:, cols], pu[:, 0:128], pu_t, U_t)
                    else:
                        TT(k, "dve", U[:, cols], U[:, cols], pu[:, 0:128], ALU.add, [pu_t, U_t], U_t)
            grps = [[(kb, hd) for kb in (2 * n, 2 * n + 1) for hd in range(2)] for n in range(16)]
            emit_qk(grps[0])
            for n in range(16):
                if n + 1 < 16:
                    emit_qk(grps[n + 1])
                emit_pv(grps[n])
            subs = [(s, None) for s in range(NS)] if d == 1 else [(s, i) for s in range(NS) for i in range(ST)]
            if k.opts.get("skip_sattn"):
                subs = []
            NBATCH = 4

            def sub_load(j, s, i):
                ck, ck_t = cks[j % len(cks)]
                rows = I["kv%d" % W][s, (0 if i is None else i):W:d, :].rearrange("r (t c) -> r t c", t=2)[:, :, 2 * sp * 64:2 * sp * 64 + 128]
                dma_in(k, ck[:], rows, ck_t, cksem[j % len(cks)])
            for j0 in range(0, min(NBATCH, len(subs))):
                sub_load(cki + j0, *subs[j0])
            for b0 in range(0, len(subs), NBATCH):
                batch = subs[b0:b0 + NBATCH]
                idx = [cki + b0 + j for j in range(len(batch))]
                for j, (s, i) in enumerate(subs[b0 + NBATCH:b0 + 2 * NBATCH]):
                    sub_load(cki + b0 + NBATCH + j, s, i)
                tr = {}
                for j, (s, i) in zip(idx, batch):
                    ck, ck_t = cks[j % len(cks)]
                    tr[j] = psum(k)
                    TR(k, tr[j][0][:, 0:128], ck[:, 0, :], k.ident, [ck_t, k.cm_t], tr[j][1])
                for j, (s, i) in zip(idx, batch):
                    ck, ck_t = cks[j % len(cks)]
                    vc, vc_t = vcs[j % len(vcs)]
                    kcT, kcT_t = kcTs[j % len(kcTs)]
                    CP(k, "act", kcT[:], tr[j][0][:, 0:128], tr[j][1], kcT_t)
                    CP(k, "dve", vc[:, 0, 0:64], ck[:, 1, 0:64], ck_t, vc_t)
                    CP(k, "dve", vc[:, 1, 64:128], ck[:, 1, 64:128], ck_t, vc_t)
                qk = {}
                for j, (s, i) in zip(idx, batch):
                    kcT, kcT_t = kcTs[j % len(kcTs)]
                    q0 = SEQ + s * ST + (0 if i is None else i)
                    nq = ST if i is None else 1
                    qc = slice(q0, q0 + nq)
                    kc_new = slice(SEQ + s * ST, SEQ + (s + 1) * ST)
                    qk[j] = psum(k)
                    pt, pt_t = qk[j]
                    for hd in range(2):
                        hs = slice(64 * hd, 64 * hd + 64)
                        MM(k, pt[:, 16 * hd:16 * hd + nq], kcT[hs, :], qT[hs, qc], [kcT_t, qT_t], pt_t)
                        MM(k, pt[0:ST, 16 * hd + 8:16 * hd + 8 + nq], kT[hs, kc_new], qT[hs, qc], [kT_t, qT_t], pt_t)
                for j, (s, i) in zip(idx, batch):
                    nq = ST if i is None else 1
                    pt, pt_t = qk[j]
                    tb, tb_t = tbs[j % len(tbs)]
                    ptv = pt[:, 0:32].rearrange("p (h c) -> p h c", h=2)
                    tbv = tb[:, 0:32].rearrange("p (h c) -> p h c", h=2)
                    TT(k, "dve", tbv[:, :, 0:nq], ptv[:, :, 0:nq], Bt[:, :, 128:128 + nq], ALU.add, [pt_t, Bt_t], tb_t)
                    Bc = Bt[0:ST, :, 0:ST] if i is None else BD[:, :, i:i + 1]
                    TT(k, "dve", tbv[0:ST, :, 8:8 + nq], ptv[0:ST, :, 8:8 + nq], Bc, ALU.add, [pt_t, Bt_t, BD_t], tb_t)
                for j, (s, i) in zip(idx, batch):
                    nq = ST if i is None else 1
                    tb, tb_t = tbs[j % len(tbs)]
                    P, P_t = Ps[j % len(Ps)]
                    tbv = tb[:, 0:32].rearrange("p (h c) -> p h c", h=2)
                    Pv = P[:, 0:32].rearrange("p (h c) -> p h c", h=2)
                    ACT(k, Pv[:, :, 0:nq], tbv[:, :, 0:nq], AF.Exp, tb_t, P_t)
                    ACT(k, Pv[0:ST, :, 8:8 + nq], tbv[0:ST, :, 8:8 + nq], AF.Exp, tb_t, P_t)
                pv = {}
                for j, (s, i) in zip(idx, batch):
                    nq = ST if i is None else 1
                    vc, vc_t = vcs[j % len(vcs)]
                    P, P_t = Ps[j % len(Ps)]
                    pv[j] = psum(k)
                    pu, pu_t = pv[j]
                    for hd in range(2):
                        MM(k, pu[:, 16 * hd:16 * hd + nq], vc[:, hd, :], P[:, 16 * hd:16 * hd + nq], [vc_t, P_t], pu_t, start=True, stop=False)
                        MM(k, pu[:, 16 * hd:16 * hd + nq], Vsn[:, s, hd, :], P[0:ST, 16 * hd + 8:16 * hd + 8 + nq], [Vsn_t, P_t], pu_t, start=False, stop=True)
                for j, (s, i) in zip(idx, batch):
                    q0 = SEQ + s * ST + (0 if i is None else i)
                    nq = ST if i is None else 1
                    qc = slice(q0, q0 + nq)
                    pu, pu_t = pv[j]
                    for hd in range(2):
                        U, U_t = Us[hd]
                        if g == 0:
                            CP(k, "act", U[:, qc], pu[:, 16 * hd:16 * hd + nq], pu_t, U_t)
                        else:
                            TT(k, "dve", U[:, qc], U[:, qc], pu[:, 16 * hd:16 * hd + nq], ALU.add, [pu_t, U_t], U_t)
            cki += len(subs)
        stg.load(wza, wza_t, 0, I["w_in"], OFF_ZA + sp * 128, 128)
        for c0 in range(0, NTOK, 512):
            n = min(512, NTOK - c0)
            pt, pt_t = psum(k)
            for kc in range(8):
                MM(k, pt[:, 0:n], wza[:, kc, :], hT[:, kc, c0:c0 + n], [wza_t, hT_t], pt_t, start=(kc == 0), stop=(kc == 7))
            ACT(k, sz[:, 0:n], pt[:, 0:n], AF.Silu, pt_t, sz_t)
            ACT(k, rec[0:64, 0:n], UA[64:128, c0:c0 + n], AF.Ln, UA_t, rec_t)
            ACT(k, rec[64:128, 0:n], UB[0:64, c0:c0 + n], AF.Ln, UB_t, rec_t)
            ACT(k, rec[:, 0:n], rec[:, 0:n], AF.Exp, rec_t, rec_t, scale=-1.0)
            TT(k, "pool", t1[0:64, 0:n], UA[0:64, c0:c0 + n], rec[0:64, 0:n], ALU.mult, [UA_t, rec_t], t1_t)
            TT(k, "pool", t1[64:128, 0:n], UB[64:128, c0:c0 + n], rec[64:128, 0:n], ALU.mult, [UB_t, rec_t], t1_t)
            yat, yat_t = yats[yi % 2]
            TT(k, "dve", yat[:, 0:n], t1[:, 0:n], sz[:, 0:n], ALU.mult, [t1_t, sz_t], yat_t)
            dma_out(k, k.yaT_d[sp * 128:(sp + 1) * 128, c0:c0 + n], yat[:, 0:n], [yat_t], sem=yasem[yi % 2], writes=[k.yad_t])
            yi += 1
    if "yaT" in k.dbg:
        S.barrier(("sp",))
        dma_out(k, O["yaT"], k.yaT_d, [k.yad_t])
def phaseF(k, es):
    nc, S, I, O = k.nc, k.S, k.I, k.O
    TF = 256
    wg, wg_t = sbt(k, es, "wg", [128, 8, 2048], BF16)
    wzb, wzb_t = sbt(k, es, "wzb", [128, 8, 1024], BF16)
    wa, wa_t = sbt(k, es, "wa", [128, 4, 1024], BF16)
    wb, wb_t = sbt(k, es, "wb", [128, 8, 1024], BF16)
    wo, wo_t = sbt(k, es, "wo", [128, 8, 1024], BF16)
    lng, lng_t = sbt(k, es, "lng", [128, 2, D], F32)
    gate_rep, gate_t = sbt(k, es, "gate_rep", [128, D], F32)
    gate_s, gates_t = sbt(k, es, "gate_s", [NS * ST, D], F32)
    s0 = S.dsem()
    dma_in(k, lng[:, 0, :], I["ln_g"].partition_broadcast(128), lng_t, s0)
    dma_in(k, lng[:, 1, :], I["ln_b"].partition_broadcast(128), lng_t, s0)
    with contextlib.ExitStack() as tes:
        stg = Stg(k, tes, 8, 256, "fst")
        stg.load(wg, wg_t, 0, I["w_in"], OFF_GA, 2048)
        stg.load(wzb, wzb_t, 0, I["w_in"], OFF_ZB, 1024)
        stg.load(wb, wb_t, 0, I["w_b"], 0, 1024)
        stg.load(wo, wo_t, 0, I["w_out"], 0, 1024)
        stg4 = Stg(k, tes, 4, 256, "fst4")
        stg4.load(wa, wa_t, 0, I["w_a"], 0, 1024)
        gtmp, gtmp_t = sbt(k, tes, "gtmp", [128, 128], F32)
        for fc in range(8):
            TS(k, "dve", gtmp[:], k.onesf[:], k.cond[:, 16 + fc, 0:1], ALU.mult, [k.onesf_t, k.cond_t], gtmp_t)
            pt, pt_t = psum(k)
            MM(k, pt[:, 0:128], gtmp[:], k.ident, [gtmp_t, k.cm_t], pt_t)
            CP(k, "act", gate_rep[:, fc * 128:(fc + 1) * 128], pt[:, 0:128], pt_t, gate_t)
            for s in range(NS):
                TS(k, "dve", gtmp[:, s * ST:(s + 1) * ST], k.onesf[:, 0:ST], k.cond[:, 16 + fc, 1 + s:2 + s], ALU.mult, [k.onesf_t, k.cond_t], gtmp_t)
            pt, pt_t = psum(k)
            MM(k, pt[0:NS * ST, 0:128], gtmp[:, 0:NS * ST], k.ident, [gtmp_t, k.cm_t], pt_t)
            CP(k, "act", gate_s[:, fc * 128:(fc + 1) * 128], pt[0:NS * ST, 0:128], pt_t, gates_t)
        S.barrier()
    hts = [sbt(k, es, "fhT%d" % i, [128, 8, TF], BF16) for i in range(2)]
    yas = [sbt(k, es, "fya%d" % i, [128, 4, TF], BF16) for i in range(2)]
    ybs = [sbt(k, es, "fyb%d" % i, [128, 8, TF], BF16) for i in range(2)]
    lsem = [[S.dsem() for _ in range(3)] for _ in range(2)]
    xts = [sbt(k, es, "fx%d" % i, [128, D], F32) for i in range(2)]
    xsem = [S.dsem() for _ in range(2)]
    ybg, ybg_t = sbt(k, es, "ybg", [128, 8, TF], BF16)
    mTs = [sbt(k, es, "mT%d" % i, [128, 8, TF], BF16) for i in range(2)]
    nhalf, nhalf_t = sbt(k, es, "nhalf", [128, 1], F32)
    MS(k, "dve", nhalf[:], -0.5, nhalf_t)
    sg = [sbt(k, es, "sg%d" % i, [128, TF], F32) for i in range(4)]
    tt_, tt_t = sbt(k, es, "tt", [128, D], F32)
    ys = [sbt(k, es, "fy%d" % i, [128, D], F32) for i in range(2)]
    ysem = [S.dsem() for _ in range(2)]
    stt, stt_t = sbt(k, es, "fstat", [128, 24], F32)
    hTd = k.hT_d.rearrange("(kc p) t -> p kc t", p=128)
    yad = k.yaT_d.rearrange("(kc p) t -> p kc t", p=128)
    ybd = k.ybT_d.rearrange("(kc p) t -> p kc t", p=128)
    xv = I["x"].rearrange("(t p) d -> t p d", p=128)
    ntile = k.opts.get("nftiles", SEQ // TF)
    tiles = [(i * TF, TF) for i in range(ntile)] + [(SEQ, NS * ST)]

    def loads(ti):
        c0, n = tiles[ti]
        b = ti % 2
        dma_in(k, hts[b][0][:, :, 0:n], hTd[:, :, c0:c0 + n], hts[b][1], lsem[b][0], reads=[k.hTd_t])
        dma_in(k, yas[b][0][:, :, 0:n], yad[:, :, c0:c0 + n], yas[b][1], lsem[b][1], reads=[k.yad_t])
        dma_in(k, ybs[b][0][:, :, 0:n], ybd[:, :, c0:c0 + n], ybs[b][1], lsem[b][2], reads=[k.ybd_t])

    xi = 0
    yi = 0
    sgi = 0
    loads(0)
    def s12(ti):
        nonlocal sgi
        c0, n = tiles[ti]
        mT, mT_t = mTs[ti % 2]
        b = ti % 2
        if ti + 1 < len(tiles):
            loads(ti + 1)
        hTt, hTt_t = hts[b]
        ya, ya_t = yas[b]
        yb, yb_t = ybs[b]
        for hb in range(8):
            pt, pt_t = psum(k)
            for kc in range(8):
                MM(k, pt[:, 0:n], wzb[:, kc, hb * 128:(hb + 1) * 128], hTt[:, kc, 0:n], [wzb_t, hTt_t], pt_t, start=(kc == 0), stop=(kc == 7))
            s_, s_t = sg[sgi % 4]
            sgi += 1
            ACT(k, s_[:, 0:n], pt[:, 0:n], AF.Silu, pt_t, s_t)
            TT(k, "dve", ybg[:, hb, 0:n], yb[:, hb, 0:n], s_[:, 0:n], ALU.mult, [yb_t, s_t], ybg_t)
        yield
        for ncb in range(8):
            if ncb == 4:
                yield
            cs = slice(ncb * 128, (ncb + 1) * 128)
            pa, pb, pga, pgb = psum(k), psum(k), psum(k), psum(k)
            for kc in range(4):
                MM(k, pa[0][:, 0:n], wa[:, kc, cs], ya[:, kc, 0:n], [wa_t, ya_t], pa[1], start=(kc == 0), stop=(kc == 3))
            for kc in range(8):
                MM(k, pb[0][:, 0:n], wb[:, kc, cs], ybg[:, kc, 0:n], [wb_t, ybg_t], pb[1], start=(kc == 0), stop=(kc == 7))
            for kc in range(8):
                MM(k, pga[0][:, 0:n], wg[:, kc, cs], hTt[:, kc, 0:n], [wg_t, hTt_t], pga[1], start=(kc == 0), stop=(kc == 7))
            for kc in range(8):
                MM(k, pgb[0][:, 0:n], wg[:, kc, 1024 + ncb * 128:1024 + (ncb + 1) * 128], hTt[:, kc, 0:n], [wg_t, hTt_t], pgb[1], start=(kc == 0), stop=(kc == 7))
            sa, sa_t = sg[sgi % 4]
            sb_, sb_t = sg[(sgi + 1) % 4]
            sgi += 2
            ACT(k, sa[:, 0:n], pga[0][:, 0:n], AF.Sigmoid, pga[1], sa_t)
            ACT(k, sb_[:, 0:n], pgb[0][:, 0:n], AF.Sigmoid, pgb[1], sb_t)
            TT(k, "dve", sa[:, 0:n], pa[0][:, 0:n], sa[:, 0:n], ALU.mult, [pa[1], sa_t], sa_t)
            TT(k, "dve", sb_[:, 0:n], pb[0][:, 0:n], sb_[:, 0:n], ALU.mult, [pb[1], sb_t], sb_t)
            TT(k, "dve", mT[:, ncb, 0:n], sa[:, 0:n], sb_[:, 0:n], ALU.add, [sa_t, sb_t], mT_t)

    def s3(ti):
        nonlocal xi, yi
        c0, n = tiles[ti]
        mT, mT_t = mTs[ti % 2]
        for j0 in range(0, n, 128):
            M = min(128, n - j0)
            samp = c0 >= SEQ
            xt, xt_t = xts[xi % 2]
            if samp:
                dma_in(k, xt[0:M, :], I["xs"], xt_t, xsem[xi % 2])
            else:
                dma_in(k, xt[:], xv[(c0 + j0) // 128], xt_t, xsem[xi % 2])
            xi += 1
            p0, p1 = psum(k), psum(k)
            for half, pp in ((0, p0), (1, p1)):
                for kc in range(8):
                    MM(k, pp[0][0:M, :], mT[:, kc, j0:j0 + M], wo[:, kc, half * 512:(half + 1) * 512], [mT_t, wo_t], pp[1], start=(kc == 0), stop=(kc == 7))
            gr, gr_t = (gate_s, gates_t) if samp else (gate_rep, gate_t)
            TT(k, "dve", tt_[0:M, 0:512], p0[0][0:M, :], gr[0:M, 0:512], ALU.mult, [p0[1], gr_t], tt_t)
            TT(k, "dve", tt_[0:M, 512:1024], p1[0][0:M, :], gr[0:M, 512:1024], ALU.mult, [p1[1], gr_t], tt_t)
            STT(k, tt_[0:M, :], xt[0:M, :], ALPHA, tt_[0:M, :], ALU.mult, ALU.add, [xt_t, tt_t], tt_t)
            k.S.op("dve", lambda e, o=stt[0:M, 0:6], i_=tt_[0:M, 0:512]: e.bn_stats(out=o, in_=i_), [tt_t], [stt_t])
            k.S.op("dve", lambda e, o=stt[0:M, 6:12], i_=tt_[0:M, 512:1024]: e.bn_stats(out=o, in_=i_), [tt_t], [stt_t])
            k.S.op("dve", lambda e, o=stt[0:M, 12:14], i_=stt[0:M, 0:12]: e.bn_aggr(out=o, in_=i_), [stt_t], [stt_t])
            TS(k, "dve", stt[0:M, 14:15], stt[0:M, 13:14], 1e-5, ALU.add, stt_t, stt_t)
            TT(k, "pool", stt[0:M, 15:16], stt[0:M, 14:15], nhalf[0:M, :], ALU.pow, [stt_t, nhalf_t], stt_t)
            y, y_t = ys[yi % 2]
            STT(k, stt[0:M, 16:17], stt[0:M, 12:13], -1.0, stt[0:M, 15:16], ALU.mult, ALU.mult, [stt_t], stt_t)
            ACT(k, y[0:M, :], tt_[0:M, :], AF.Identity, [tt_t, stt_t], y_t, scale=stt[0:M, 15:16], bias=stt[0:M, 16:17])
            TT(k, "pool", y[0:M, :], y[0:M, :], lng[0:M, 0, :], ALU.mult, [y_t, lng_t], y_t)
            TT(k, "pool", y[0:M, :], y[0:M, :], lng[0:M, 1, :], ALU.add, [y_t, lng_t], y_t)
            if samp:
                dma_out(k, O["ys"], y[0:M, :], [y_t], sem=ysem[yi % 2])
            else:
                dma_out(k, O["y"][c0 + j0:c0 + j0 + 128, :], y[:], [y_t], sem=ysem[yi % 2])
            yi += 1
            yield

    for _ in s12(0):
        pass
    for ti in range(len(tiles)):
        g12 = s12(ti + 1) if ti + 1 < len(tiles) else iter(())
        g3 = s3(ti)
        next(g12, None)
        next(g3, None)
        next(g12, None)
        for _ in g3:
            pass
        for _ in g12:
            pass


def t5_causal_buckets(dist):
    n_buckets, max_dist = 32, 2048
    max_exact = n_buckets // 2
    dist = np.asarray(dist, dtype=np.int64)
    ratio = np.maximum(dist, max_exact) / max_exact
    large = max_exact + (np.log(ratio) / math.log(max_dist / max_exact) * (n_buckets - max_exact)).astype(np.int64)
    return np.where(dist < max_exact, dist, np.minimum(large, n_buckets - 1)).astype(np.int32)


def host_consts():
    p = np.arange(128)[:, None]
    f = np.arange(128)[None, :]
    blk = (p // 32) == (f // 32)
    cm = np.stack([(p == f), (p <= f), (f < p) & blk, (f >= p), (f < p) & ~blk]).astype(np.float32)
    return cm


def bias_layout(rel_bias):
    p = np.arange(128)[:, None]
    f = np.arange(256)[None, :]
    j = np.where(f < 128, f - p, f - p)
    valid = np.where(f < 128, j >= 0, j <= 128)
    idx = np.where(valid, j, 129).astype(np.int64)
    out = np.empty((24, 128, 256), np.float32)
    for gi, (window, dil) in enumerate(GROUPS):
        buckets = t5_causal_buckets(dil * np.arange(window // dil + 1))
        for hh in range(8):
            h = gi * 8 + hh
            ext = np.concatenate([rel_bias[buckets, h], np.array([NEG, NEG], np.float32)]).astype(np.float32)
            out[h] = ext[np.minimum(idx, 129)]
    return out


def bias_diag_layout(rel_bias):
    out = np.empty((24, ST, ST), np.float32)
    eye = np.eye(ST, dtype=bool)
    for gi, (window, dil) in enumerate(GROUPS):
        b0 = t5_causal_buckets(np.zeros(1))[0]
        for hh in range(8):
            h = gi * 8 + hh
            ext = np.array([rel_bias[b0, h], NEG], np.float32)
            out[h] = ext[np.where(eye, 0, 1)]
    return out


def make_in_maps(inp):
    f32 = lambda a: np.ascontiguousarray(np.asarray(a, dtype=np.float32))
    cm = host_consts()
    biasT = bias_layout(np.asarray(inp["rel_bias"], np.float32))
    shared = {
        "w_cond": f32(inp["w_cond"][0]),
        "bcondT": f32(np.asarray(inp["b_cond"][0]).reshape(24, 128).T),
        "w_in": f32(inp["w_in"][0]),
        "biasT": biasT,
        "biasD": bias_diag_layout(np.asarray(inp["rel_bias"], np.float32)),
        "convwT": f32(np.asarray(inp["conv_w"][0]).reshape(4, 24, 128).transpose(2, 1, 0)),
        "a_log": f32(inp["a_log"]).reshape(1, 8),
        "dt_bias": f32(inp["dt_bias"]).reshape(1, 8),
        "b_norm_w": f32(inp["b_norm_w"]).reshape(1, 128),
        "w_a": f32(inp["w_branch_a"][0]),
        "w_b": f32(inp["w_branch_b"][0]),
        "w_out": f32(inp["w_out"][0]),
        "ln_g": f32(inp["ln_g"]).reshape(1, D),
        "ln_b": f32(inp["ln_b"]).reshape(1, D),
        "cmats": cm,
    }
    maps = []
    for b in range(8):
        sl = slice(NS * b, NS * b + NS)
        c5 = np.concatenate([np.asarray(inp["c_prompt"][b:b + 1]), np.asarray(inp["c_sample"][sl])], axis=0)
        m = dict(shared)
        m["x"] = f32(inp["x_prompt"][b])
        m["xs"] = f32(np.asarray(inp["x_sample"][sl]).reshape(NS * ST, D))
        m["cT"] = f32(c5.T.reshape(8, 128, 1 + NS).transpose(1, 0, 2))
        m["kv128"] = f32(np.asarray(inp["cache_kv_w128"][0, sl]).reshape(NS, 128, 1024))
        m["kv512"] = f32(np.asarray(inp["cache_kv_w512"][0, sl]).reshape(NS, 512, 1024))
        m["kv2048"] = f32(np.asarray(inp["cache_kv_w2048"][0, sl]).reshape(NS, 2048, 1024))
        m["sconv"] = f32(np.asarray(inp["state_conv"][0, sl]).reshape(NS * 3, 3072))
        m["sdelta"] = f32(np.asarray(inp["state_delta"][0, sl]).reshape(NS * 8, 128, 128))
        maps.append(m)
    return maps


_NC_CACHE = {}


def kernel(**inputs):
    if "nc" not in _NC_CACHE:
        _NC_CACHE["nc"] = build()
    nc = _NC_CACHE["nc"]
    maps = make_in_maps(inputs)
    res = run_bass_kernel_spmd(nc, maps, core_ids=list(range(8))).results
    g = lambda name: [np.asarray(r[name], dtype=np.float32) for r in res]
    y = np.stack(g("y"))
    ys = np.concatenate([a.reshape(NS, ST, D) for a in g("ys")], axis=0)
    outs = [y, ys]
    for w in (128, 512, 2048):
        outs.append(np.stack([a.reshape(w, 2, 8, 64) for a in g("kvp%d" % w)])[None])
    outs.append(np.stack(g("convp"))[None])
    outs.append(np.stack(g("deltap"))[None])
    for w in (128, 512, 2048):
        outs.append(np.concatenate([a.reshape(NS, w, 2, 8, 64) for a in g("kvs%d" % w)], axis=0)[None])
    outs.append(np.concatenate(g("convs"), axis=0)[None])
    outs.append(np.concatenate([a.reshape(NS, 8, 128, 128) for a in g("deltas")], axis=0)[None])
    return tuple(outs)
```

```python
import contextlib
import math
import numpy as np
import ml_dtypes
import concourse.bass as bass
import concourse.mybir as mybir
from concourse.bass_utils import run_bass_kernel_spmd

F32 = mybir.dt.float32
BF16 = mybir.dt.bfloat16
ALU = mybir.AluOpType
AF = mybir.ActivationFunctionType
AX = mybir.AxisListType
ENGS = ("pe", "act", "dve", "pool", "sp")

D = 1024
SEQ = 4096
NS = 4
ST = 4
NTOK = SEQ + NS * ST
PROJ = 11280
OFF_QA, OFF_KA, OFF_VA, OFF_ZA, OFF_QKVB, OFF_ZB, OFF_AB, OFF_GA, OFF_GB = 0, 1536, 3072, 4608, 5120, 8192, 9216, 9232, 10256
GROUPS = ((128, 1), (512, 4), (2048, 16))
ALPHA = 2 ** 0.25
NEG = -30000.0


class Sem:
    def __init__(self, h, step):
        self.h = h
        self.n = 0
        self.step = step


class Tok:
    __slots__ = ("w", "r", "name", "excl")

    def __init__(self, name="", excl=False):
        self.w = None
        self.r = {}
        self.name = name
        self.excl = excl


class Sched:
    def __init__(self, nc, es):
        self.nc = nc
        self.es = es
        self.prog = {e: [] for e in ENGS}
        self.esem = {e: Sem(es.enter_context(nc.semaphore("sem_" + e)), 1) for e in ENGS if e != "sp"}
        self.seen = {e: {} for e in ENGS}
        self.dsems = []
        self.ninstr = 0

    def dsem(self, name=None):
        s = Sem(self.es.enter_context(self.nc.semaphore(name or ("dsem%d" % len(self.dsems)))), 16)
        self.dsems.append(s)
        return s

    def _need(self, eng, deps):
        for s, v in deps.items():
            if eng == "pe" and s is self.esem["pe"]:
                continue
            if self.seen[eng].get(s, 0) < v:
                self.prog[eng].append(("w", s, v))
                self.seen[eng][s] = v

    def op(self, eng, fn, reads=(), writes=(), dsem=None):
        ex = [t for t in reads if t.excl]
        if ex:
            reads = [t for t in reads if not t.excl]
            writes = list(writes) + [t for t in ex if t not in writes]
        deps = {}
        for t in reads:
            if t.w is not None and deps.get(t.w[0], 0) < t.w[1]:
                deps[t.w[0]] = t.w[1]
        for t in writes:
            if t.w is not None and deps.get(t.w[0], 0) < t.w[1]:
                deps[t.w[0]] = t.w[1]
            for s, v in t.r.items():
                if deps.get(s, 0) < v:
                    deps[s] = v
        self._need(eng, deps)
        sem = dsem if dsem is not None else self.esem[eng]
        sem.n += sem.step
        rec = (sem, sem.n)
        self.prog[eng].append(("i", fn, sem))
        self.ninstr += 1
        for t in reads:
            if t.r.get(sem, 0) < sem.n:
                t.r[sem] = sem.n
        for t in writes:
            t.w = rec
            t.r = {}
        return rec

    def barrier(self, engs=ENGS):
        allsems = list(self.esem.values()) + self.dsems
        for e in engs:
            self._need(e, {s: s.n for s in allsems if s.n > 0})

    def emit(self):
        nc = self.nc
        self.barrier(("sp",))
        with nc.Block() as block:
            def run(engname, e):
                for item in self.prog[engname]:
                    if item[0] == "w":
                        e.wait_ge(item[1].h, item[2])
                    else:
                        item[1](e).then_inc(item[2].h, item[2].step)

            @block.tensor
            def _(e):
                run("pe", e)

            @block.scalar
            def _(e):
                run("act", e)

            @block.vector
            def _(e):
                run("dve", e)

            @block.gpsimd
            def _(e):
                run("pool", e)

            @block.sync
            def _(e):
                run("sp", e)


class K:
    pass


def build(dbg=None, phases="0CBAF", opts=None):
    nc = bass.Bass("TRN2", target_bir_lowering=False)
    k = K()
    k.nc = nc
    k.dbg = dbg or {}
    k.opts = opts or {}
    di = lambda name, shape, dt=F32: nc.dram_tensor(name, list(shape), dt, kind="ExternalInput").ap()
    do = lambda name, shape, dt=F32: nc.dram_tensor(name, list(shape), dt, kind="ExternalOutput").ap()
    dsc = lambda name, shape, dt=F32: nc.dram_tensor(name, list(shape), dt).ap()
    I = k.I = {}
    O = k.O = {}
    I["x"] = di("x", [SEQ, D])
    I["xs"] = di("xs", [NS * ST, D])
    I["cT"] = di("cT", [128, 8, 1 + NS])
    I["kv128"] = di("kv128", [NS, 128, 1024])
    I["kv512"] = di("kv512", [NS, 512, 1024])
    I["kv2048"] = di("kv2048", [NS, 2048, 1024])
    I["sconv"] = di("sconv", [NS * 3, 3072])
    I["sdelta"] = di("sdelta", [NS * 8, 128, 128])
    I["w_cond"] = di("w_cond", [D, 3 * D])
    I["bcondT"] = di("bcondT", [128, 24])
    I["w_in"] = di("w_in", [D, PROJ])
    I["biasT"] = di("biasT", [24, 128, 256])
    I["biasD"] = di("biasD", [24, ST, ST])
    I["convwT"] = di("convwT", [128, 24, 4])
    I["a_log"] = di("a_log", [1, 8])
    I["dt_bias"] = di("dt_bias", [1, 8])
    I["b_norm_w"] = di("b_norm_w", [1, 128])
    I["w_a"] = di("w_a", [512, D])
    I["w_b"] = di("w_b", [D, D])
    I["w_out"] = di("w_out", [D, D])
    I["ln_g"] = di("ln_g", [1, D])
    I["ln_b"] = di("ln_b", [1, D])
    I["cmats"] = di("cmats", [5, 128, 128])
    O["y"] = do("y", [SEQ, D])
    O["ys"] = do("ys", [NS * ST, D])
    O["kvp128"] = do("kvp128", [128, 1024])
    O["kvp512"] = do("kvp512", [512, 1024])
    O["kvp2048"] = do("kvp2048", [2048, 1024])
    O["convp"] = do("convp", [3, 3072])
    O["deltap"] = do("deltap", [8, 128, 128])
    O["kvs128"] = do("kvs128", [NS, 128, 1024])
    O["kvs512"] = do("kvs512", [NS, 512, 1024])
    O["kvs2048"] = do("kvs2048", [NS, 2048, 1024])
    O["convs"] = do("convs", [NS, 3, 3072])
    O["deltas"] = do("deltas", [NS * 8, 128, 128])
    k.hT_d = dsc("hT_d", [D, NTOK], BF16)
    k.ybT_d = dsc("ybT_d", [D, NTOK], BF16)
    k.yaT_d = dsc("yaT_d", [512, NTOK], BF16)
    for name, (shape, dt) in k.dbg.items():
        O[name] = do(name, shape, dt)

    with contextlib.ExitStack() as es:
        S = k.S = Sched(nc, es)
        k.es = es
        k.ps = [es.enter_context(nc.psum_tensor("ps%d" % i, [128, 512], F32)) for i in range(8)]
        k.pst = [Tok("ps%d" % i, excl=True) for i in range(8)]
        k.psi = 0
        k.out_sem = S.dsem("out_sem")
        phase0(k)
        if "C" in phases:
            cache_copies(k)
        if "B" in phases:
            with contextlib.ExitStack() as pes:
                phaseB(k, pes)
                S.barrier()
        if "A" in phases:
            with contextlib.ExitStack() as pes:
                phaseA(k, pes)
                S.barrier()
        if "F" in phases:
            with contextlib.ExitStack() as pes:
                phaseF(k, pes)
                S.barrier()
        S.emit()
    return nc


def psum(k, pool=None):
    pools = getattr(k, "pspools", None)
    if pool is None or pools is None:
        i = k.psi
        k.psi = (i + 1) % 8
        return k.ps[i], k.pst[i]
    lst, idx = pools[pool]
    i = lst[idx % len(lst)]
    pools[pool][1] = idx + 1
    return k.ps[i], k.pst[i]


def sbt(k, es, name, shape, dt):
    t = es.enter_context(k.nc.sbuf_tensor("s_" + name, list(shape), dt))
    return t, Tok(name)


def _l(x):
    return list(x) if isinstance(x, (list, tuple)) else [x]


def dma_in(k, out_ap, in_ap, toks, sem, eng="sp", reads=()):
    k.S.op(eng, lambda e: e.dma_start(out=out_ap, in_=in_ap), reads=_l(reads), writes=_l(toks), dsem=sem)


def dma_out(k, out_ap, in_ap, reads, sem=None, eng="sp", writes=()):
    k.S.op(eng, lambda e: e.dma_start(out=out_ap, in_=in_ap), reads=_l(reads), writes=_l(writes), dsem=sem or k.out_sem)


def MM(k, out, lhsT, rhs, reads, writes, start=True, stop=True):
    k.S.op("pe", lambda e: e.matmul(out, lhsT=lhsT, rhs=rhs, start=start, stop=stop), _l(reads), _l(writes))


def TR(k, out, in_, ident, reads, writes):
    k.S.op("pe", lambda e: e.transpose(out, in_, ident), _l(reads), _l(writes))


def ACT(k, out, in_, func, reads, writes, scale=None, bias=None):
    kw = {}
    if scale is not None:
        kw["scale"] = scale
    if bias is not None:
        kw["bias"] = bias
    k.S.op("act", lambda e: e.activation(out=out, in_=in_, func=func, **kw), _l(reads), _l(writes))


def TT(k, eng, out, in0, in1, op, reads, writes):
    k.S.op(eng, lambda e: e.tensor_tensor(out=out, in0=in0, in1=in1, op=op), _l(reads), _l(writes))


def TS(k, eng, out, in0, s1, op0, reads, writes, s2=None, op1=None):
    if op1 is None:
        k.S.op(eng, lambda e: e.tensor_scalar(out=out, in0=in0, scalar1=s1, scalar2=None, op0=op0), _l(reads), _l(writes))
    else:
        k.S.op(eng, lambda e: e.tensor_scalar(out=out, in0=in0, scalar1=s1, scalar2=s2, op0=op0, op1=op1), _l(reads), _l(writes))


def STT(k, out, in0, scalar, in1, op0, op1, reads, writes):
    k.S.op("dve", lambda e: e.scalar_tensor_tensor(out=out, in0=in0, scalar=scalar, in1=in1, op0=op0, op1=op1), _l(reads), _l(writes))


def CP(k, eng, out, in_, reads, writes):
    if eng == "act":
        ACT(k, out, in_, AF.Identity, reads, writes)
    else:
        k.S.op(eng, lambda e: e.tensor_copy(out=out, in_=in_), _l(reads), _l(writes))


def EV(k, out, in_, reads, writes, scale=None):
    k.rr = getattr(k, "rr", 0) + 1
    if k.rr % 2 == 0:
        ACT(k, out, in_, AF.Copy if not isinstance(scale, (int, float)) or True else AF.Copy, reads, writes, scale=scale)
    else:
        if scale is None:
            CP(k, "dve", out, in_, reads, writes)
        else:
            TS(k, "dve", out, in_, scale, ALU.mult, reads, writes)


def MS(k, eng, ap, val, writes):
    k.S.op(eng, lambda e: e.memset(ap, val), [], _l(writes))


def phase0(k):
    nc, S, es, I = k.nc, k.S, k.es, k.I
    k.cm, k.cm_t = sbt(k, es, "cmats", [128, 5, 128], F32)
    k.ident, k.triu, k.ldiag, k.uincl, k.loff = (k.cm[:, i, :] for i in range(5))
    s0 = S.dsem()
    dma_in(k, k.cm[:], I["cmats"].rearrange("m p f -> p m f"), k.cm_t, s0)
    k.identb, k.identb_t = sbt(k, es, "identb", [128, 128], BF16)
    CP(k, "act", k.identb[:], k.ident, k.cm_t, k.identb_t)
    k.onesf, k.onesf_t = sbt(k, es, "onesf", [128, 128], F32)
    MS(k, "dve", k.onesf[:], 1.0, k.onesf_t)
    k.onesb, k.onesb_t = sbt(k, es, "onesb", [128, 128], BF16)
    MS(k, "dve", k.onesb[:], 1.0, k.onesb_t)
    k.epsc, k.epsc_t = sbt(k, es, "epsc", [128, 3], F32)
    MS(k, "dve", k.epsc[:, 0:1], 1e-6, k.epsc_t)
    MS(k, "dve", k.epsc[:, 1:2], 1.0, k.epsc_t)
    MS(k, "dve", k.epsc[:, 2:3], 1e-5, k.epsc_t)
    k.small, k.small_t = sbt(k, es, "small", [128, 16 + 128], F32)
    s1 = S.dsem()
    s2, s3, s4 = S.dsem(), S.dsem(), S.dsem()
    dma_in(k, k.small[:, 0:8], I["a_log"].partition_broadcast(128), k.small_t, s1)
    dma_in(k, k.small[:, 8:16], I["dt_bias"].partition_broadcast(128), k.small_t, s1)
    dma_in(k, k.small[:, 16:144], I["b_norm_w"].partition_broadcast(128), k.small_t, s1)
    k.negA, k.negA_t = sbt(k, es, "negA", [128, 8], F32)
    ACT(k, k.negA[:], k.small[:, 0:8], AF.Exp, k.small_t, k.negA_t)
    TS(k, "dve", k.negA[:], k.negA[:], -1.0, ALU.mult, k.negA_t, k.negA_t)
    k.dtb = k.small[:, 8:16]
    k.bnw = k.small[:, 16:144]
    k.convw, k.convw_t = sbt(k, es, "convw", [128, 24, 4], F32)
    dma_in(k, k.convw[:], I["convwT"], k.convw_t, s2)
    k.cond, k.cond_t = sbt(k, es, "cond", [128, 24, 1 + NS], F32)
    with contextlib.ExitStack() as tes:
        cT, cT_t = sbt(k, tes, "cT", [128, 8, 1 + NS], F32)
        bc, bc_t = sbt(k, tes, "bcond", [128, 24], F32)
        dma_in(k, cT[:], I["cT"], cT_t, s3)
        dma_in(k, bc[:], I["bcondT"], bc_t, s4)
        ACT(k, cT[:], cT[:], AF.Silu, cT_t, cT_t)
        wst = [sbt(k, tes, "wcst%d" % i, [128, 8, 512], F32) for i in range(2)]
        wss = [S.dsem() for _ in range(2)]
        wv = I["w_cond"].rearrange("(kc p) n -> p kc n", p=128)
        for j in range(6):
            w, w_t = wst[j % 2]
            dma_in(k, w[:], wv[:, :, j * 512:(j + 1) * 512], w_t, wss[j % 2])
            pt, pt_t = psum(k)
            for fc in range(4):
                for kc in range(8):
                    MM(k, pt[:, fc * 8:fc * 8 + 1 + NS], w[:, kc, fc * 128:(fc + 1) * 128], cT[:, kc, :], [w_t, cT_t], pt_t, start=(kc == 0), stop=(kc == 7))
            for fc in range(4):
                f = j * 4 + fc
                TS(k, "dve", k.cond[:, f, :], pt[:, fc * 8:fc * 8 + 1 + NS], bc[:, f:f + 1], ALU.add, [pt_t, bc_t], k.cond_t)
        TS(k, "dve", k.cond[:, 8:16, :], k.cond[:, 8:16, :], 1.0, ALU.add, k.cond_t, k.cond_t)
        S.barrier()
    if "cond" in k.dbg:
        dma_out(k, k.O["cond"], k.cond[:], [k.cond_t])


def build_hT_tile(k, xt, xt_t, ntok, hTt, hTt_t, seqs, pool=None):
    pts = [psum(k, pool), psum(k, pool)]
    for kc in range(8):
        pt, pt_t = pts[kc // 4]
        TR(k, pt[:, (kc % 4) * 128:(kc % 4) * 128 + ntok], xt[0:ntok, kc * 128:(kc + 1) * 128], k.ident[0:ntok, 0:ntok], [xt_t, k.cm_t], pt_t)
    for kc in range(8):
        pt, pt_t = pts[kc // 4]
        for (c0, c1, si) in seqs:
            ACT(k, hTt[:, kc, c0:c1], pt[:, (kc % 4) * 128 + c0:(kc % 4) * 128 + c1], AF.Identity, [pt_t, k.cond_t], hTt_t,
                scale=k.cond[:, 8 + kc, si:si + 1], bias=k.cond[:, kc, si:si + 1])


def load_w_bf16(k, es_tmp, dst, dst_t, col0, ncols, chunk=256, name="wst", src=None, nkc=8):
    S = k.S
    wv = (src if src is not None else k.I["w_in"]).rearrange("(kc p) n -> p kc n", p=128)
    st = [sbt(k, es_tmp, "%s%d" % (name, i), [128, nkc, chunk], F32) for i in range(2)]
    ss = [S.dsem() for _ in range(2)]
    j = 0
    c = 0
    while c < ncols:
        n = min(chunk, ncols - c)
        w, w_t = st[j % 2]
        dma_in(k, w[:, :, 0:n], wv[:, :, col0 + c:col0 + c + n], w_t, ss[j % 2])
        CP(k, "pool" if j % 2 == 0 else "act", dst[:, :, c:c + n], w[:, :, 0:n], w_t, dst_t)
        c += n
        j += 1


def phaseB(k, es):
    nc, S, I, O = k.nc, k.S, k.I, k.O
    NH = 8
    wB, wB_t = sbt(k, es, "wB", [128, 8, 3072], BF16)
    wab, wab_t = sbt(k, es, "wab", [128, 8, 16], BF16)
    with contextlib.ExitStack() as tes:
        load_w_bf16(k, tes, wB, wB_t, OFF_QKVB, 3072)
        load_w_bf16(k, tes, wab, wab_t, OFF_AB, 16, chunk=16, name="wabst")
        S.barrier()
    xts = [sbt(k, es, "xt%d" % i, [128, D], F32) for i in range(2)]
    xsem = [S.dsem() for _ in range(2)]
    hTs = [sbt(k, es, "hTt%d" % i, [128, 8, 128], BF16) for i in range(2)]
    hsem = [S.dsem() for _ in range(2)]
    pc, pc_t = sbt(k, es, "pc", [128, 24, 131], F32)
    cv, cv_t = sbt(k, es, "cv", [128, 24, 128], F32)
    tmpa, tmpa_t = sbt(k, es, "tmpa", [128, 8, 128], F32)
    tmpc, tmpc_t = sbt(k, es, "tmpc", [128, 8, 128], F32)
    cv_ts = [Tok("cv%d" % i) for i in range(3)]
    sq2, sq2_t = sbt(k, es, "sq2", [128, 16, 128], BF16)
    rn, rn_t = sbt(k, es, "rn", [128, 16, 128], BF16)
    qkvTs = [sbt(k, es, "qkvT%d" % i, [128, 24, 128], BF16) for i in range(2)]
    kvtoks = [sbt(k, es, "kvtok%d" % i, [128, 16, 128], BF16) for i in range(2)]
    gbs = [sbt(k, es, "gb%d" % i, [128, 56], F32) for i in range(2)]
    Sst, Sst_t = sbt(k, es, "Sst", [128, 8, 128], F32)
    Sbf = [sbt(k, es, "Sbf%d" % i, [128, 8, 128], BF16)[0] for i in range(2)]
    Sh_t = [Tok("S%d" % h) for h in range(NH)]
    Sbf_t = [[Tok("Sbf%d_%d" % (p, h)) for h in range(NH)] for p in range(2)]
    stsem = S.dsem()
    ybT, ybT_t = sbt(k, es, "ybT", [128, 8, 128], BF16)
    ybsem = S.dsem()
    diagW, diagW_t = sbt(k, es, "diagW", [128, 8, 4, 128], F32)
    for c_ in range(8):
        for i_ in range(4):
            TS(k, "dve", diagW[:, c_, i_, :], k.ident, k.convw[:, 16 + c_, i_:i_ + 1], ALU.mult, [k.cm_t, k.convw_t], diagW_t)
    rowmask, rowmask_t = sbt(k, es, "rowmask", [128, 1], F32)
    MS(k, "dve", rowmask[:], 0.0, rowmask_t)
    MS(k, "dve", rowmask[0:ST, :], 1.0, rowmask_t)
    cvsem = S.dsem()
    hist, hist_t = sbt(k, es, "hist", [72, 128], F32)
    histsem = S.dsem()
    NSLOT = 8
    slots = []
    for s in range(NSLOT):
        d = {}
        for nm in ["gTri", "absG", "E", "ecr", "EU"]:
            d[nm] = sbt(k, es, "%s_%d" % (nm, s), [128, 128], F32)
        d["ELd"] = d["gTri"]
        d["ELo"] = d["absG"]
        d["o2"] = d["E"]
        for nm in ["Ad", "Ao", "X0", "X1", "XT0", "XT1", "DT0", "DT1", "qkm", "kt", "qeT"]:
            d[nm] = sbt(k, es, "%s_%d" % (nm, s), [128, 128], BF16)
        d["yb"] = d["Ad"]
        d["nwT"] = d["X0"]
        d["vnew"] = d["X1"]
        d["nNT"] = d["XT0"]
        for nm in ["R", "Xa", "Xb"]:
            d[nm] = sbt(k, es, "%s_%d" % (nm, s), [128, 256], BF16)
        d["st"] = sbt(k, es, "st_%d" % s, [128, 8], F32)
        slots.append(d)

    def T_(h, nm):
        return slots[h % NSLOT][nm][0]

    def Tt(h, nm):
        return slots[h % NSLOT][nm][1]

    MS(k, "pool", Sst[:], 0.0, [Sst_t] + Sh_t)
    MS(k, "pool", Sbf[0][:], 0.0, Sbf_t[0])
    MS(k, "pool", pc[:, :, 0:3], 0.0, pc_t)
    par = [0] * NH

    xv = I["x"].rearrange("(t p) d -> t p d", p=128)
    hTd = k.hT_d.rearrange("(kc p) t -> p kc t", p=128)
    ybd = k.ybT_d.rearrange("(h p) t -> p h t", p=128)
    k.hTd_t = Tok("hTd")
    k.ybd_t = Tok("ybd")

    def load_x(t):
        dma_in(k, xts[t % 2][0][:], xv[t], xts[t % 2][1], xsem[t % 2])

    ntiles = k.opts.get("ntiles", 32)
    nvirt = k.opts.get("nvirt", NS)
    vt_list = list(range(ntiles)) + [("s", s) for s in range(nvirt)]
    load_x(0)
    def front(ti, t):
        virt = isinstance(t, tuple)
        qkvT, qkvT_t = qkvTs[ti % 2]
        kvtok, kvtok_t = kvtoks[ti % 2]
        gb, gb_t = gbs[ti % 2]
        virt = isinstance(t, tuple)
        hTt, hTt_t = hTs[ti % 2]
        xt, xt_t = xts[ti % 2]
        if not virt:
            if t + 1 < ntiles:
                load_x(t + 1)
            build_hT_tile(k, xt, xt_t, 128, hTt, hTt_t, [(0, 128, 0)], pool="f")
            dma_out(k, hTd[:, :, t * 128:(t + 1) * 128], hTt[:], [hTt_t], sem=hsem[ti % 2], writes=[k.hTd_t])
        else:
            s = t[1]
            dma_in(k, xt[0:ST, :], I["xs"][s * ST:(s + 1) * ST, :], xt_t, xsem[ti % 2])
            MS(k, "pool", hTt[:], 0.0, hTt_t)
            build_hT_tile(k, xt, xt_t, ST, hTt, hTt_t, [(0, ST, 1 + s)], pool="f")
            dma_out(k, hTd[:, :, SEQ + s * ST:SEQ + (s + 1) * ST], hTt[:, :, 0:ST], [hTt_t], sem=hsem[ti % 2], writes=[k.hTd_t])
            dma_in(k, hist[:], I["sconv"][s * 3:(s + 1) * 3, :].rearrange("r (c f) -> (r c) f", f=128), hist_t, histsem)
            pt, pt_t = psum(k, "f")
            TR(k, pt[:, 0:72], hist[:], k.ident[0:72, 0:72], [hist_t, k.cm_t], pt_t)
            CP(k, "dve", pc[:, :, 0:3], pt[:, 0:72].rearrange("p (r c) -> p c r", r=3), pt_t, pc_t)
        if k.opts.get('stopB', 9) <= 1:
            return
        yield
        pt, pt_t = psum(k, "f")
        for kc in range(8):
            MM(k, pt[:, 0:16], hTt[:, kc, :], wab[:, kc, :], [hTt_t, wab_t], pt_t, start=(kc == 0), stop=(kc == 7))
        ACT(k, gb[:, 8:16], pt[:, 8:16], AF.Sigmoid, pt_t, gb_t)
        TT(k, "dve", gb[:, 40:48], pt[:, 0:8], k.dtb, ALU.add, [pt_t, k.small_t], gb_t)
        ACT(k, gb[:, 40:48], gb[:, 40:48], AF.Exp, gb_t, gb_t)
        ACT(k, gb[:, 40:48], gb[:, 40:48], AF.Ln, [gb_t, k.epsc_t], gb_t, bias=k.epsc[:, 1:2])
        TT(k, "dve", gb[:, 0:8], gb[:, 40:48], k.negA[:], ALU.mult, [gb_t, k.negA_t], gb_t)
        if virt:
            TS(k, "dve", gb[:, 0:16], gb[:, 0:16], rowmask[:, 0:1], ALU.mult, [gb_t, rowmask_t], gb_t)
        pt, pt_t = psum(k, "f")
        MM(k, pt[:, 0:8], k.triu, gb[:, 0:8], [gb_t, k.cm_t], pt_t)
        CP(k, "dve", gb[:, 16:24], pt[:, 0:8], pt_t, gb_t)
        ACT(k, gb[:, 24:32], gb[:, 16:24], AF.Exp, gb_t, gb_t)
        TS(k, "dve", gb[:, 48:56], pt[:, 0:8], -1.0, ALU.mult, pt_t, gb_t)
        TT(k, "dve", gb[:, 32:40], gb[:, 24:32], gb[:, 8:16], ALU.mult, gb_t, gb_t)
        if k.opts.get('stopB', 9) <= 2:
            return
        yield
        for cg in range(6):
            pt, pt_t = psum(k, "f")
            for c4 in range(4):
                cc = cg * 4 + c4
                for kc in range(8):
                    MM(k, pt[:, c4 * 128:(c4 + 1) * 128], wB[:, kc, cc * 128:(cc + 1) * 128], hTt[:, kc, :], [hTt_t, wB_t], pt_t, start=(kc == 0), stop=(kc == 7))
            CP(k, "act" if cg % 2 == 0 else "dve", pc[:, cg * 4:cg * 4 + 4, 3:131], pt[:].rearrange("p (c f) -> p c f", f=128), pt_t, pc_t)
            yield
        if virt or t == ntiles - 1:
            dstc = O["convs"][t[1]] if virt else O["convp"]
            srcc = pc[:, :, 4:7] if virt else pc[:, :, 128:131]
            for rr in range(3):
                k.S.op("sp", lambda e, o=dstc[rr, :].rearrange("(c p) -> p c", p=128), i_=srcc[:, :, rr]: e.dma_start(out=o, in_=i_, allow_slow_non_contiguous=True), [pc_t], [], dsem=cvsem)
        yield
        for th, e_, ta, ta_t in ((0, "pool", tmpa, tmpa_t), (1, "dve", tmpc, tmpc_t)):
            cs = slice(th * 8, th * 8 + 8)
            wb_ = [k.convw[:, cs, i:i + 1].broadcast_to([128, 8, 128]) for i in range(4)]
            TT(k, e_, cv[:, cs, :], pc[:, cs, 0:128], wb_[0], ALU.mult, [pc_t, k.convw_t], cv_ts[th])
            for i in (1, 2, 3):
                TT(k, e_, ta[:], pc[:, cs, i:i + 128], wb_[i], ALU.mult, [pc_t, k.convw_t], ta_t)
                TT(k, e_, cv[:, cs, :], cv[:, cs, :], ta[:], ALU.add, [ta_t, cv_ts[th]], cv_ts[th])
            yield
        for half in range(2):
            pt, pt_t = psum(k, "f")
            for j in range(4):
                c_ = half * 4 + j
                for i in range(4):
                    MM(k, pt[:, j * 128:(j + 1) * 128], diagW[:, c_, i, :], pc[:, 16 + c_, i:i + 128], [diagW_t, pc_t], pt_t, start=(i == 0), stop=(i == 3))
            ACT(k, cv[:, 16 + half * 4:16 + half * 4 + 4, :], pt[:].rearrange("p (c f) -> p c f", f=128), AF.Silu, pt_t, cv_ts[2])
            yield
        CP(k, "pool", pc[:, :, 0:3], pc[:, :, 128:131], pc_t, pc_t)
        for _ in range(k.opts.get('cwait', 4)):
            yield
        for th in (1, 0):
            ACT(k, cv[:, th * 8:th * 8 + 8, :], cv[:, th * 8:th * 8 + 8, :], AF.Silu, cv_ts[th], cv_ts[th])
            yield
        yield
        TT(k, "pool", sq2[:], cv[:, 0:16, :], cv[:, 0:16, :], ALU.mult, cv_ts[0:2], sq2_t)
        for _ in range(3):
            yield
        for j in range(4):
            pt, pt_t = psum(k, "f")
            MM(k, pt[:], k.onesb[:], sq2[:, j * 4:(j + 1) * 4, :], [sq2_t, k.onesb_t], pt_t)
            ACT(k, rn[:, j * 4:(j + 1) * 4, :], pt[:].rearrange("p (c f) -> p c f", f=128), AF.Ln, [pt_t, k.epsc_t], rn_t, bias=k.epsc[:, 0:1])
        yield
        ACT(k, rn[:], rn[:], AF.Exp, rn_t, rn_t, scale=-0.5)
        yield
        STT(k, qkvT[:, 0:8, :], cv[:, 0:8, :], 128 ** -0.5, rn[:, 0:8, :], ALU.mult, ALU.mult, [cv_ts[0], rn_t], qkvT_t)
        TT(k, "dve", qkvT[:, 8:16, :], cv[:, 8:16, :], rn[:, 8:16, :], ALU.mult, [cv_ts[1], rn_t], qkvT_t)
        CP(k, "act", qkvT[:, 16:24, :], cv[:, 16:24, :], cv_ts[2], qkvT_t)
        yield
        if "qkvT" in k.dbg and ti == k.opts.get("dbg_tile", 0):
            dma_out(k, O["qkvT"], qkvT[:], [qkvT_t])
            dma_out(k, O["gb"], gb[:], [gb_t])
        yield
        for g4 in range(4):
            pt, pt_t = psum(k, "f")
            ptb = pt[:].bitcast(BF16)
            for c4 in range(4):
                TR(k, ptb[:, c4 * 128:(c4 + 1) * 128], qkvT[:, 8 + g4 * 4 + c4, :], k.identb[:], [qkvT_t, k.identb_t], pt_t)
            CP(k, "act" if g4 % 2 == 0 else "dve", kvtok[:, g4 * 4:(g4 + 1) * 4, :], ptb[:, 0:512].rearrange("p (c f) -> p c f", f=128), pt_t, kvtok_t)
        if k.opts.get('stopB', 9) <= 4:
            return

    def units(ti, t):
        virt = isinstance(t, tuple)
        qkvT, qkvT_t = qkvTs[ti % 2]
        kvtok, kvtok_t = kvtoks[ti % 2]
        gb, gb_t = gbs[ti % 2]
        if virt:
            s = t[1]
            dst = O["deltap"] if s == 0 else O["deltas"][(s - 1) * 8:s * 8]
            dma_out(k, dst.rearrange("h k v -> k h v"), Sst[:], [Sst_t] + Sh_t, sem=stsem)
            dma_in(k, Sst[:], I["sdelta"][s * 8:(s + 1) * 8].rearrange("h k v -> k h v"), [Sst_t] + Sh_t, stsem)
            for h in range(NH):
                CP(k, "act", Sbf[par[h]][:, h, :], Sst[:, h, :], Sh_t[h], Sbf_t[par[h]][h])
        yield
        if k.opts.get('stopB', 9) <= 4:
            return
        heads = list(range(NH))

        def stage(mm_fn, ev_fn, chunk=k.opts.get('chunk', 4)):
            for c0 in range(0, NH, chunk):
                banks = {}
                for h in heads[c0:c0 + chunk]:
                    banks[h] = psum(k, "u")
                    mm_fn(h, banks[h][0], banks[h][1])
                for h in heads[c0:c0 + chunk]:
                    ev_fn(h, banks[h][0], banks[h][1])
                yield

        for h in heads:
            EV(k, T_(h, "gTri")[:], k.triu, [k.cm_t, gb_t], Tt(h, "gTri"), scale=gb[:, h:h + 1])
        yield

        def mm(h, pt, pt_t):
            MM(k, pt[:, 0:128], k.onesf[:], T_(h, "gTri")[:], [Tt(h, "gTri"), k.onesf_t], pt_t)

        def ev(h, pt, pt_t):
            ACT(k, T_(h, "absG")[:], pt[:, 0:128], AF.Abs, [pt_t, gb_t], Tt(h, "absG"), bias=gb[:, 48 + h:49 + h])
            ACT(k, T_(h, "ecr")[:], pt[:, 0:128], AF.Exp, pt_t, Tt(h, "ecr"))
            ACT(k, T_(h, "E")[:], T_(h, "absG")[:], AF.Exp, Tt(h, "absG"), Tt(h, "E"), scale=-1.0)
        yield from stage(mm, ev)
        yield
        for h in heads:
            EV(k, T_(h, "kt")[:], kvtok[:, h, :], [kvtok_t, Tt(h, "E")], Tt(h, "kt"), scale=T_(h, "E")[:, 127:128])
            TT(k, "pool", T_(h, "qeT")[:], qkvT[:, h, :], T_(h, "ecr")[:], ALU.mult, [qkvT_t, Tt(h, "ecr")], Tt(h, "qeT"))
            TS(k, "dve", T_(h, "R")[:, 0:128], kvtok[:, 8 + h, :], gb[:, 8 + h:9 + h], ALU.mult, [kvtok_t, gb_t], Tt(h, "R"))
            TS(k, "dve", T_(h, "R")[:, 128:256], kvtok[:, h, :], gb[:, 32 + h:33 + h], ALU.mult, [kvtok_t, gb_t], Tt(h, "R"))
        yield

        def mm(h, pt, pt_t):
            MM(k, pt[:, 0:128], qkvT[:, 8 + h, :], qkvT[:, 8 + h, :], qkvT_t, pt_t)
            MM(k, pt[:, 128:256], qkvT[:, 8 + h, :], qkvT[:, h, :], qkvT_t, pt_t)

        def ev(h, pt, pt_t):
            TT(k, "dve", T_(h, "ELd")[:], pt[:, 0:128], T_(h, "E")[:], ALU.mult, [pt_t, Tt(h, "E")], Tt(h, "ELd"))
            TT(k, "dve", T_(h, "EU")[:], pt[:, 128:256], T_(h, "E")[:], ALU.mult, [pt_t, Tt(h, "E")], Tt(h, "EU"))
            STT(k, T_(h, "Ad")[:], T_(h, "ELd")[:], gb[:, 8 + h:9 + h], k.ldiag, ALU.mult, ALU.mult, [gb_t, Tt(h, "ELd"), k.cm_t], Tt(h, "Ad"))
            STT(k, T_(h, "Ao")[:], T_(h, "ELd")[:], gb[:, 8 + h:9 + h], k.loff, ALU.mult, ALU.mult, [gb_t, Tt(h, "ELd"), k.cm_t], Tt(h, "Ao"))
            TT(k, "pool", T_(h, "qkm")[:], T_(h, "EU")[:], k.uincl, ALU.mult, [Tt(h, "EU"), k.cm_t], Tt(h, "qkm"))
        yield from stage(mm, ev)
        yield

        def mm(h, pt, pt_t):
            TR(k, pt[:].bitcast(BF16)[:, 0:128], T_(h, "Ad")[:], k.identb[:], [Tt(h, "Ad"), k.identb_t], pt_t)

        def ev(h, pt, pt_t):
            ptb = pt[:].bitcast(BF16)[:, 0:128]
            EV(k, T_(h, "XT0")[:], ptb, pt_t, Tt(h, "XT0"))
            TT(k, "dve", T_(h, "DT0")[:], k.identb[:], ptb, ALU.subtract, [pt_t, k.identb_t], Tt(h, "DT0"))
        yield from stage(mm, ev)
        yield
        names = [("Ad", "XT0")] + [("X%d" % (kk % 2), "XT%d" % ((kk + 1) % 2)) for kk in range(4)]
        dts = ["DT0", "DT1", "DT0", "DT1", "DT0"]

        def emit_sq(kk):
            Xc, XTc = names[kk]
            Xn, XTn = names[kk + 1]
            last = kk == 3

            def mm(h, pt, pt_t):
                MM(k, pt[:, 0:128], T_(h, XTc)[:], T_(h, Xc)[:], [Tt(h, XTc), Tt(h, Xc)], pt_t)
                if not last:
                    MM(k, pt[:, 128:256], T_(h, Xc)[:], T_(h, XTc)[:], [Tt(h, XTc), Tt(h, Xc)], pt_t)

            def ev(h, pt, pt_t):
                if last:
                    EV(k, T_(h, Xn)[:], pt[:, 0:128], pt_t, Tt(h, Xn))
                else:
                    e_ = "act" if h % 2 == 0 else "dve"
                    CP(k, e_, T_(h, Xn)[:], pt[:, 0:128], pt_t, Tt(h, Xn))
                    CP(k, e_, T_(h, XTn)[:], pt[:, 128:256], pt_t, Tt(h, XTn))
            yield from stage(mm, ev)

        def emit_dt(kk):
            Xn = names[kk + 1][0]
            DTc_, DTn = dts[kk], dts[kk + 1]

            def mm(h, pt, pt_t):
                MM(k, pt[:, 0:128], T_(h, Xn)[:], T_(h, DTc_)[:], [Tt(h, Xn), Tt(h, DTc_)], pt_t)

            def ev(h, pt, pt_t):
                TT(k, "dve", T_(h, DTn)[:], T_(h, DTc_)[:], pt[:, 0:128], ALU.add, [pt_t, Tt(h, DTc_)], Tt(h, DTn))
            yield from stage(mm, ev)
        for step in (("sq", 0), ("sq", 1), ("dt", 0), ("sq", 2), ("dt", 1), ("sq", 3), ("dt", 2), ("dt", 3)):
            yield from (emit_sq if step[0] == "sq" else emit_dt)(step[1])
        DTc = dts[4]

        def mm(h, pt, pt_t):
            MM(k, pt[:, 0:128], T_(h, "Ao")[:], T_(h, DTc)[:], [Tt(h, "Ao"), Tt(h, DTc)], pt_t)

        def ev(h, pt, pt_t):
            EV(k, T_(h, "nNT")[:], pt[:, 0:128], pt_t, Tt(h, "nNT"), scale=-1.0)
        yield from stage(mm, ev)
        yield
        Xcur = "Xa"
        for it in range(4):
            prev = "Xb" if Xcur == "Xa" else "Xa"

            def mm(h, pt, pt_t):
                MM(k, pt[:, 0:256], T_(h, DTc)[:], T_(h, "R")[:], [Tt(h, DTc), Tt(h, "R")], pt_t, start=True, stop=(it == 0))
                if it > 0:
                    MM(k, pt[:, 0:256], T_(h, "nNT")[:], T_(h, prev)[:], [Tt(h, "nNT"), Tt(h, prev)], pt_t, start=False, stop=True)

            def ev(h, pt, pt_t):
                CP(k, "act" if h % 2 == 0 else "dve", T_(h, Xcur)[:], pt[:, 0:256], pt_t, Tt(h, Xcur))
            yield from stage(mm, ev)
            Xfin = Xcur
            Xcur = prev
            yield

        def mm(h, pt, pt_t):
            TR(k, pt[:].bitcast(BF16)[:, 0:128], T_(h, Xfin)[:, 128:256], k.identb[:], [Tt(h, Xfin), k.identb_t], pt_t)

        def ev(h, pt, pt_t):
            EV(k, T_(h, "nwT")[:], pt[:].bitcast(BF16)[:, 0:128], pt_t, Tt(h, "nwT"), scale=-1.0)
        yield from stage(mm, ev)
        yield

        def mm(h, pt, pt_t):
            MM(k, pt[:, 0:128], T_(h, "nwT")[:], Sbf[par[h]][:, h, :], [Tt(h, "nwT"), Sbf_t[par[h]][h]], pt_t)

        def ev(h, pt, pt_t):
            TT(k, "dve", T_(h, "vnew")[:], T_(h, Xfin)[:, 0:128], pt[:, 0:128], ALU.add, [pt_t, Tt(h, Xfin)], Tt(h, "vnew"))
        yield from stage(mm, ev)
        yield

        def mm(h, pt, pt_t):
            p = par[h]
            MM(k, pt[:, 0:128], T_(h, "qeT")[:], Sbf[p][:, h, :], [Tt(h, "qeT"), Sbf_t[p][h]], pt_t, start=True, stop=False)
            MM(k, pt[:, 0:128], T_(h, "qkm")[:], T_(h, "vnew")[:], [Tt(h, "qkm"), Tt(h, "vnew")], pt_t, start=False, stop=True)
            MM(k, pt[:, 128:256], T_(h, "kt")[:], T_(h, "vnew")[:], [Tt(h, "kt"), Tt(h, "vnew")], pt_t)

        def ev(h, pt, pt_t):
            p = par[h]
            st, st_t = T_(h, "st"), Tt(h, "st")
            STT(k, Sst[:, h, :], Sst[:, h, :], T_(h, "ecr")[:, 127:128], pt[:, 128:256], ALU.mult, ALU.add, [pt_t, Tt(h, "ecr"), Sh_t[h]], Sh_t[h])
            EV(k, Sbf[1 - p][:, h, :], Sst[:, h, :], Sh_t[h], Sbf_t[1 - p][h])
            par[h] = 1 - p
            ACT(k, T_(h, "o2")[:], pt[:, 0:128], AF.Square, pt_t, Tt(h, "o2"))
            k.S.op("dve", lambda e, o=st[:, 0:1], i=T_(h, "o2")[:]: e.tensor_reduce(out=o, in_=i, axis=AX.X, op=ALU.add), [Tt(h, "o2")], [st_t])
            ACT(k, st[:, 1:2], st[:, 0:1], AF.Ln, [st_t, k.epsc_t], st_t, scale=1.0 / 128, bias=k.epsc[:, 0:1])
            ACT(k, st[:, 2:3], st[:, 1:2], AF.Exp, st_t, st_t, scale=-0.5)
            STT(k, T_(h, "yb")[:], pt[:, 0:128], st[:, 2:3], k.bnw, ALU.mult, ALU.mult, [pt_t, st_t, k.small_t], Tt(h, "yb"))
        yield from stage(mm, ev)
        yield

        def mm(h, pt, pt_t):
            TR(k, pt[:].bitcast(BF16)[:, 0:128], T_(h, "yb")[:], k.identb[:], [Tt(h, "yb"), k.identb_t], pt_t)

        def ev(h, pt, pt_t):
            EV(k, ybT[:, h, :], pt[:].bitcast(BF16)[:, 0:128], pt_t, ybT_t)
        yield from stage(mm, ev)
        if not virt:
            dma_out(k, ybd[:, :, t * 128:(t + 1) * 128], ybT[:], [ybT_t], sem=ybsem, writes=[k.ybd_t])
        else:
            s = t[1]
            dma_out(k, ybd[:, :, SEQ + s * ST:SEQ + (s + 1) * ST], ybT[:, :, 0:ST], [ybT_t], sem=ybsem, writes=[k.ybd_t])

    k.pspools = {"u": [[0, 1, 2, 3, 4, 5], 0], "f": [[6, 7], 0]}
    for _ in front(0, vt_list[0]):
        pass
    for ti, t in enumerate(vt_list):
        gf = front(ti + 1, vt_list[ti + 1]) if ti + 1 < len(vt_list) else None
        for _ in units(ti, t):
            if gf is not None:
                if next(gf, "done") == "done":
                    gf = None
        if gf is not None:
            for _ in gf:
                pass
    k.pspools = None
    dst = O["deltap"] if nvirt == 0 else O["deltas"][(nvirt - 1) * 8:nvirt * 8]
    dma_out(k, dst.rearrange("h k v -> k h v"), Sst[:], [Sst_t] + Sh_t, sem=stsem)
    if "ybT" in k.dbg:
        dma_out(k, O["ybT"], k.ybT_d, [k.ybd_t])
class Stg:
    def __init__(self, k, es, nkc, chunk, name):
        self.k = k
        self.nkc = nkc
        self.chunk = chunk
        self.st = [sbt(k, es, "%s%d" % (name, i), [128, nkc, chunk], F32) for i in range(2)]
        self.ss = [k.S.dsem() for _ in range(2)]
        self.j = 0

    def load(self, dst, dst_t, dcol0, src, col0, ncols):
        k = self.k
        wv = src.rearrange("(kc p) n -> p kc n", p=128)
        c = 0
        while c < ncols:
            n = min(self.chunk, ncols - c)
            w, w_t = self.st[self.j % 2]
            dma_in(k, w[:, :, 0:n], wv[:, :, col0 + c:col0 + c + n], w_t, self.ss[self.j % 2])
            CP(k, "pool" if self.j % 2 == 0 else "act", dst[:, :, dcol0 + c:dcol0 + c + n], w[:, :, 0:n], w_t, dst_t)
            c += n
            self.j += 1


def subseq_chunks(d, maxlen):
    L = SEQ // d
    for r in range(d):
        for u0 in range(0, L, maxlen):
            n = min(maxlen, L - u0)
            yield (r * L + u0, r + d * u0, n)


def tslice(tok0, n, d):
    return slice(tok0, tok0 + d * (n - 1) + 1, d)


def cache_copies(k):
    for (W, d) in GROUPS:
        src, dst = k.I["kv%d" % W], k.O["kvs%d" % W]
        for s in range(NS):
            r = 0
            while r < W - ST:
                n = min(256, W - ST - r)
                dma_out(k, dst[s, r:r + n, :], src[s, r + ST:r + ST + n, :], [], eng="pool")
                r += n


def phaseA(k, es):
    nc, S, I, O = k.nc, k.S, k.I, k.O
    hT, hT_t = sbt(k, es, "hT", [128, 8, NTOK], BF16)
    hsem = S.dsem()
    hTd = k.hT_d.rearrange("(kc p) t -> p kc t", p=128)
    for kc in range(8):
        dma_in(k, hT[:, kc, :], hTd[:, kc, :], hT_t, hsem, reads=[k.hTd_t])
    if not k.opts.get("skip_kv"):
        with contextlib.ExitStack() as tes:
            wkv, wkv_t = sbt(k, tes, "wkv", [128, 8, 1024], BF16)
            okv = [sbt(k, tes, "okv%d" % i, [128, 1024], F32) for i in range(2)]
            oks = [S.dsem() for _ in range(2)]
            stg = Stg(k, tes, 8, 256, "kvst")
            j = 0
            for g, (W, d) in enumerate(GROUPS):
                stg.load(wkv, wkv_t, 0, I["w_in"], OFF_KA + g * 512, 512)
                stg.load(wkv, wkv_t, 512, I["w_in"], OFF_VA + g * 512, 512)
                t0 = 32 - W // 128
                for tt in list(range(t0, 32)) + [-1]:
                    M = 128 if tt >= 0 else NS * ST
                    cs = slice(tt * 128, (tt + 1) * 128) if tt >= 0 else slice(SEQ, NTOK)
                    pa, pb = psum(k), psum(k)
                    for half, pp in ((0, pa), (1, pb)):
                        for kc in range(8):
                            MM(k, pp[0][0:M, :], hT[:, kc, cs], wkv[:, kc, half * 512:(half + 1) * 512], [hT_t, wkv_t], pp[1], start=(kc == 0), stop=(kc == 7))
                    o, o_t = okv[j % 2]
                    CP(k, "act", o[0:M, 0:512], pa[0][0:M, :], pa[1], o_t)
                    CP(k, "dve", o[0:M, 512:1024], pb[0][0:M, :], pb[1], o_t)
                    if tt >= 0:
                        dma_out(k, O["kvp%d" % W][(tt - t0) * 128:(tt - t0 + 1) * 128, :], o[:], [o_t], sem=oks[j % 2])
                    else:
                        for s in range(NS):
                            dma_out(k, O["kvs%d" % W][s, W - ST:W, :], o[s * ST:(s + 1) * ST, :], [o_t], sem=oks[j % 2])
                    j += 1
            S.barrier()
    if k.opts.get("skip_attn"):
        return
    UA, UA_t = sbt(k, es, "UaccA", [128, NTOK], F32)
    UB, UB_t = sbt(k, es, "UaccB", [128, NTOK], F32)
    Us = ((UA, UA_t), (UB, UB_t))
    qT, qT_t = sbt(k, es, "qT", [128, NTOK], BF16)
    kT, kT_t = sbt(k, es, "kT", [128, NTOK], BF16)
    vT, vT_t = sbt(k, es, "vT", [128, SEQ], BF16)
    Vaug, Vaug_t = sbt(k, es, "Vaug", [128, 32, 2, 128], BF16)
    Vsn, Vsn_t = sbt(k, es, "Vsn", [ST, NS, 2, 128], BF16)
    vcs = [sbt(k, es, "vc%d" % i, [128, 2, 128], BF16) for i in range(8)]
    kcTs = [sbt(k, es, "kcT%d" % i, [128, 128], BF16) for i in range(8)]
    cks = [sbt(k, es, "ck%d" % i, [128, 2, 128], F32) for i in range(8)]
    cksem = [S.dsem() for _ in range(8)]
    Ps = [sbt(k, es, "Ps%d" % i, [128, 32], BF16) for i in range(8)]
    NPT = 6
    wq, wq_t = sbt(k, es, "wq", [128, 8, 384], BF16)
    wza, wza_t = sbt(k, es, "wza", [128, 8, 128], BF16)
    stg = Stg(k, es, 8, 128, "ast")
    Bt, Bt_t = sbt(k, es, "Bt", [128, 2, 256], F32)
    BD, BD_t = sbt(k, es, "BD", [ST, 2, ST], F32)
    bsem = S.dsem()
    bdsem = S.dsem()
    PT = [[sbt(k, es, "PT%d_%d" % (hd, p), [128, 256], BF16) for p in range(NPT)] for hd in range(2)]
    tbs = [sbt(k, es, "tb%d" % i, [128, 256], F32) for i in range(8)]
    sz, sz_t = sbt(k, es, "sz", [128, 512], F32)
    rec, rec_t = sbt(k, es, "rec", [128, 512], F32)
    t1, t1_t = sbt(k, es, "t1", [128, 512], F32)
    yats = [sbt(k, es, "yat%d" % i, [128, 512], BF16) for i in range(2)]
    yasem = [S.dsem() for _ in range(2)]
    k.yad_t = Tok("yad")
    MS(k, "pool", Vaug[:, :, 0, 64:128], 1.0, Vaug_t)
    MS(k, "pool", Vaug[:, :, 1, 0:64], 1.0, Vaug_t)
    MS(k, "pool", Vsn[:, :, 0, 64:128], 1.0, Vsn_t)
    MS(k, "pool", Vsn[:, :, 1, 0:64], 1.0, Vsn_t)
    for vc, vc_t in vcs:
        MS(k, "pool", vc[:, 0, 64:128], 1.0, vc_t)
        MS(k, "pool", vc[:, 1, 0:64], 1.0, vc_t)
    tbi = 0
    cki = 0
    yi = 0
    nsp = k.opts.get("nsp", 4)
    for sp in range(nsp):
        for g, (W, d) in enumerate(GROUPS):
            L = SEQ // d
            nb = L // 128
            stg.load(wq, wq_t, 0, I["w_in"], OFF_QA + g * 512 + sp * 128, 128)
            stg.load(wq, wq_t, 128, I["w_in"], OFF_KA + g * 512 + sp * 128, 128)
            stg.load(wq, wq_t, 256, I["w_in"], OFF_VA + g * 512 + sp * 128, 128)
            for hd in range(2):
                dma_in(k, Bt[:, hd, :], I["biasT"][g * 8 + 2 * sp + hd], Bt_t, bsem)
                dma_in(k, BD[:, hd, :], I["biasD"][g * 8 + 2 * sp + hd], BD_t, bdsem)
            for c0 in range(0, SEQ, 512):
                for which, dstT, dst_t in ((0, qT, qT_t), (1, kT, kT_t), (2, vT, vT_t)):
                    pt, pt_t = psum(k)
                    for kc in range(8):
                        MM(k, pt[:, 0:512], wq[:, kc, which * 128:(which + 1) * 128], hT[:, kc, c0:c0 + 512], [wq_t, hT_t], pt_t, start=(kc == 0), stop=(kc == 7))
                    ov = dstT[:, 0:SEQ].rearrange("p (r u) -> p r u", r=d)[:, :, c0 // d:(c0 + 512) // d]
                    iv = pt[:, 0:512].rearrange("p (a r) -> p r a", r=d)
                    if which == 0:
                        ACT(k, ov, iv, AF.Copy, pt_t, dst_t, scale=0.125)
                    elif which == 1:
                        CP(k, "dve", ov, iv, pt_t, dst_t)
                    else:
                        EV(k, ov, iv, pt_t, dst_t)
            for which, dstT, dst_t in ((0, qT, qT_t), (1, kT, kT_t)):
                pt, pt_t = psum(k)
                for kc in range(8):
                    MM(k, pt[:, 0:NS * ST], wq[:, kc, which * 128:(which + 1) * 128], hT[:, kc, SEQ:NTOK], [wq_t, hT_t], pt_t, start=(kc == 0), stop=(kc == 7))
                if which == 0:
                    ACT(k, dstT[:, SEQ:NTOK], pt[:, 0:NS * ST], AF.Copy, pt_t, dst_t, scale=0.125)
                else:
                    CP(k, "dve", dstT[:, SEQ:NTOK], pt[:, 0:NS * ST], pt_t, dst_t)
            for n4 in range(8):
                pt, pt_t = psum(k)
                ptb = pt[:].bitcast(BF16)
                for j in range(4):
                    n = n4 * 4 + j
                    TR(k, ptb[:, j * 128:(j + 1) * 128], vT[:, n * 128:(n + 1) * 128], k.identb[:], [vT_t, k.identb_t], pt_t)
                pv = ptb[:, 0:512].rearrange("p (c f) -> p c f", f=128)
                CP(k, "act", Vaug[:, n4 * 4:(n4 + 1) * 4, 0, 0:64], pv[:, :, 0:64], pt_t, Vaug_t)
                CP(k, "dve", Vaug[:, n4 * 4:(n4 + 1) * 4, 1, 64:128], pv[:, :, 64:128], pt_t, Vaug_t)
            pt, pt_t = psum(k)
            for s in range(NS):
                for kc in range(8):
                    MM(k, pt[0:ST, s * 128:(s + 1) * 128], hT[:, kc, SEQ + s * ST:SEQ + (s + 1) * ST], wq[:, kc, 256:384], [wq_t, hT_t], pt_t, start=(kc == 0), stop=(kc == 7))
            pv = pt[0:ST, :].rearrange("p (c f) -> p c f", f=128)
            CP(k, "act", Vsn[:, :, 0, 0:64], pv[:, :, 0:64], pt_t, Vsn_t)
            CP(k, "dve", Vsn[:, :, 1, 64:128], pv[:, :, 64:128], pt_t, Vsn_t)
            def unit_info(kb, hd):
                first = (kb % nb == 0)
                lastb = (kb % nb == nb - 1)
                r, u0 = (kb * 128) // L, (kb * 128) % L
                return first, (128 if lastb else 256), tslice(r + d * u0, 128, d)

            def emit_qk(grp):
                nonlocal tbi
                banks = {}
                for (kb, hd) in grp:
                    first, N, cols = unit_info(kb, hd)
                    hs = slice(64 * hd, 64 * hd + 64)
                    banks[(kb, hd)] = psum(k)
                    pt, pt_t = banks[(kb, hd)]
                    MM(k, pt[:, 0:N], kT[hs, kb * 128:(kb + 1) * 128], qT[hs, kb * 128:kb * 128 + N], [kT_t, qT_t], pt_t)
                tb_of = {}
                for (kb, hd) in grp:
                    first, N, cols = unit_info(kb, hd)
                    pt, pt_t = banks[(kb, hd)]
                    tb_of[(kb, hd)] = tbs[tbi % len(tbs)]
                    tbi += 1
                    tb, tb_t = tb_of[(kb, hd)]
                    TT(k, "dve", tb[:, 0:N], pt[:, 0:N], Bt[:, hd, 0:N], ALU.add, [pt_t, Bt_t], tb_t)
                for (kb, hd) in grp:
                    first, N, cols = unit_info(kb, hd)
                    tb, tb_t = tb_of[(kb, hd)]
                    P, P_t = PT[hd][kb % NPT]
                    ACT(k, P[:, 0:N], tb[:, 0:N], AF.Exp, tb_t, P_t)

            def emit_pv(grp):
                banks = {}
                for (kb, hd) in grp:
                    first, N, cols = unit_info(kb, hd)
                    P, P_t = PT[hd][kb % NPT]
                    banks[(kb, hd)] = psum(k)
                    pu, pu_t = banks[(kb, hd)]
                    if not first:
                        Pp, Pp_t = PT[hd][(kb - 1) % NPT]
                        MM(k, pu[:, 0:128], Vaug[:, kb - 1, hd, :], Pp[:, 128:256], [Vaug_t, Pp_t], pu_t, start=True, stop=False)
                    MM(k, pu[:, 0:128], Vaug[:, kb, hd, :], P[:, 0:128], [Vaug_t, P_t], pu_t, start=first, stop=True)
                for j, (kb, hd) in enumerate(grp):
                    first, N, cols = unit_info(kb, hd)
                    U, U_t = Us[hd]
                    pu, pu_t = banks[(kb, hd)]
                    if g == 0:
                        CP(k, "act" if j % 2 == 0 else "dve", U[:, cols], pu[:, 0:128], pu_t, U_t)
                    else:
                        TT(k, "dve", U[:, cols], U[:, cols], pu[:, 0:128], ALU.add, [pu_t, U_t], U_t)
            grps = [[(kb, hd) for kb in (2 * n, 2 * n + 1) for hd in range(2)] for n in range(16)]
            emit_qk(grps[0])
            for n in range(16):
                if n + 1 < 16:
                    emit_qk(grps[n + 1])
                emit_pv(grps[n])
            subs = [(s, None) for s in range(NS)] if d == 1 else [(s, i) for s in range(NS) for i in range(ST)]
            if k.opts.get("skip_sattn"):
                subs = []
            NBATCH = 4

            def sub_load(j, s, i):
                ck, ck_t = cks[j % len(cks)]
                rows = I["kv%d" % W][s, (0 if i is None else i):W:d, :].rearrange("r (t c) -> r t c", t=2)[:, :, 2 * sp * 64:2 * sp * 64 + 128]
                dma_in(k, ck[:], rows, ck_t, cksem[j % len(cks)])
            for j0 in range(0, min(NBATCH, len(subs))):
                sub_load(cki + j0, *subs[j0])
            for b0 in range(0, len(subs), NBATCH):
                batch = subs[b0:b0 + NBATCH]
                idx = [cki + b0 + j for j in range(len(batch))]
                for j, (s, i) in enumerate(subs[b0 + NBATCH:b0 + 2 * NBATCH]):
                    sub_load(cki + b0 + NBATCH + j, s, i)
                tr = {}
                for j, (s, i) in zip(idx, batch):
                    ck, ck_t = cks[j % len(cks)]
                    tr[j] = psum(k)
                    TR(k, tr[j][0][:, 0:128], ck[:, 0, :], k.ident, [ck_t, k.cm_t], tr[j][1])
                for j, (s, i) in zip(idx, batch):
                    ck, ck_t = cks[j % len(cks)]
                    vc, vc_t = vcs[j % len(vcs)]
                    kcT, kcT_t = kcTs[j % len(kcTs)]
                    CP(k, "act", kcT[:], tr[j][0][:, 0:128], tr[j][1], kcT_t)
                    CP(k, "dve", vc[:, 0, 0:64], ck[:, 1, 0:64], ck_t, vc_t)
                    CP(k, "dve", vc[:, 1, 64:128], ck[:, 1, 64:128], ck_t, vc_t)
                qk = {}
                for j, (s, i) in zip(idx, batch):
                    kcT, kcT_t = kcTs[j % len(kcTs)]
                    q0 = SEQ + s * ST + (0 if i is None else i)
                    nq = ST if i is None else 1
                    qc = slice(q0, q0 + nq)
                    kc_new = slice(SEQ + s * ST, SEQ + (s + 1) * ST)
                    qk[j] = psum(k)
                    pt, pt_t = qk[j]
                    for hd in range(2):
                        hs = slice(64 * hd, 64 * hd + 64)
                        MM(k, pt[:, 16 * hd:16 * hd + nq], kcT[hs, :], qT[hs, qc], [kcT_t, qT_t], pt_t)
                        MM(k, pt[0:ST, 16 * hd + 8:16 * hd + 8 + nq], kT[hs, kc_new], qT[hs, qc], [kT_t, qT_t], pt_t)
                for j, (s, i) in zip(idx, batch):
                    nq = ST if i is None else 1
                    pt, pt_t = qk[j]
                    tb, tb_t = tbs[j % len(tbs)]
                    ptv = pt[:, 0:32].rearrange("p (h c) -> p h c", h=2)
                    tbv = tb[:, 0:32].rearrange("p (h c) -> p h c", h=2)
                    TT(k, "dve", tbv[:, :, 0:nq], ptv[:, :, 0:nq], Bt[:, :, 128:128 + nq], ALU.add, [pt_t, Bt_t], tb_t)
                    Bc = Bt[0:ST, :, 0:ST] if i is None else BD[:, :, i:i + 1]
                    TT(k, "dve", tbv[0:ST, :, 8:8 + nq], ptv[0:ST, :, 8:8 + nq], Bc, ALU.add, [pt_t, Bt_t, BD_t], tb_t)
                for j, (s, i) in zip(idx, batch):
                    nq = ST if i is None else 1
                    tb, tb_t = tbs[j % len(tbs)]
                    P, P_t = Ps[j % len(Ps)]
                    tbv = tb[:, 0:32].rearrange("p (h c) -> p h c", h=2)
                    Pv = P[:, 0:32].rearrange("p (h c) -> p h c", h=2)
                    ACT(k, Pv[:, :, 0:nq], tbv[:, :, 0:nq], AF.Exp, tb_t, P_t)
                    ACT(k, Pv[0:ST, :, 8:8 + nq], tbv[0:ST, :, 8:8 + nq], AF.Exp, tb_t, P_t)
                pv = {}
                for j, (s, i) in zip(idx, batch):
                    nq = ST if i is None else 1
                    vc, vc_t = vcs[j % len(vcs)]
                    P, P_t = Ps[j % len(Ps)]
                    pv[j] = psum(k)
                    pu, pu_t = pv[j]
                    for hd in range(2):
                        MM(k, pu[:, 16 * hd:16 * hd + nq], vc[:, hd, :], P[:, 16 * hd:16 * hd + nq], [vc_t, P_t], pu_t, start=True, stop=False)
                        MM(k, pu[:, 16 * hd:16 * hd + nq], Vsn[:, s, hd, :], P[0:ST, 16 * hd + 8:16 * hd + 8 + nq], [Vsn_t, P_t], pu_t, start=False, stop=True)
                for j, (s, i) in zip(idx, batch):
                    q0 = SEQ + s * ST + (0 if i is None else i)
                    nq = ST if i is None else 1
                    qc = slice(q0, q0 + nq)
                    pu, pu_t = pv[j]
                    for hd in range(2):
                        U, U_t = Us[hd]
                        if g == 0:
                            CP(k, "act", U[:, qc], pu[:, 16 * hd:16 * hd + nq], pu_t, U_t)
                        else:
                            TT(k, "dve", U[:, qc], U[:, qc], pu[:, 16 * hd:16 * hd + nq], ALU.add, [pu_t, U_t], U_t)
            cki += len(subs)
        stg.load(wza, wza_t, 0, I["w_in"], OFF_ZA + sp * 128, 128)
        for c0 in range(0, NTOK, 512):
            n = min(512, NTOK - c0)
            pt, pt_t = psum(k)
            for kc in range(8):
                MM(k, pt[:, 0:n], wza[:, kc, :], hT[:, kc, c0:c0 + n], [wza_t, hT_t], pt_t, start=(kc == 0), stop=(kc == 7))
            ACT(k, sz[:, 0:n], pt[:, 0:n], AF.Silu, pt_t, sz_t)
            ACT(k, rec[0:64, 0:n], UA[64:128, c0:c0 + n], AF.Ln, UA_t, rec_t)
            ACT(k, rec[64:128, 0:n], UB[0:64, c0:c0 + n], AF.Ln, UB_t, rec_t)
            ACT(k, rec[:, 0:n], rec[:, 0:n], AF.Exp, rec_t, rec_t, scale=-1.0)
            TT(k, "pool", t1[0:64, 0:n], UA[0:64, c0:c0 + n], rec[0:64, 0:n], ALU.mult, [UA_t, rec_t], t1_t)
            TT(k, "pool", t1[64:128, 0:n], UB[64:128, c0:c0 + n], rec[64:128, 0:n], ALU.mult, [UB_t, rec_t], t1_t)
            yat, yat_t = yats[yi % 2]
            TT(k, "dve", yat[:, 0:n], t1[:, 0:n], sz[:, 0:n], ALU.mult, [t1_t, sz_t], yat_t)
            dma_out(k, k.yaT_d[sp * 128:(sp + 1) * 128, c0:c0 + n], yat[:, 0:n], [yat_t], sem=yasem[yi % 2], writes=[k.yad_t])
            yi += 1
    if "yaT" in k.dbg:
        S.barrier(("sp",))
        dma_out(k, O["yaT"], k.yaT_d, [k.yad_t])
def phaseF(k, es):
    nc, S, I, O = k.nc, k.S, k.I, k.O
    TF = 256
    wg, wg_t = sbt(k, es, "wg", [128, 8, 2048], BF16)
    wzb, wzb_t = sbt(k, es, "wzb", [128, 8, 1024], BF16)
    wa, wa_t = sbt(k, es, "wa", [128, 4, 1024], BF16)
    wb, wb_t = sbt(k, es, "wb", [128, 8, 1024], BF16)
    wo, wo_t = sbt(k, es, "wo", [128, 8, 1024], BF16)
    lng, lng_t = sbt(k, es, "lng", [128, 2, D], F32)
    gate_rep, gate_t = sbt(k, es, "gate_rep", [128, D], F32)
    gate_s, gates_t = sbt(k, es, "gate_s", [NS * ST, D], F32)
    s0 = S.dsem()
    dma_in(k, lng[:, 0, :], I["ln_g"].partition_broadcast(128), lng_t, s0)
    dma_in(k, lng[:, 1, :], I["ln_b"].partition_broadcast(128), lng_t, s0)
    with contextlib.ExitStack() as tes:
        stg = Stg(k, tes, 8, 256, "fst")
        stg.load(wg, wg_t, 0, I["w_in"], OFF_GA, 2048)
        stg.load(wzb, wzb_t, 0, I["w_in"], OFF_ZB, 1024)
        stg.load(wb, wb_t, 0, I["w_b"], 0, 1024)
        stg.load(wo, wo_t, 0, I["w_out"], 0, 1024)
        stg4 = Stg(k, tes, 4, 256, "fst4")
        stg4.load(wa, wa_t, 0, I["w_a"], 0, 1024)
        gtmp, gtmp_t = sbt(k, tes, "gtmp", [128, 128], F32)
        for fc in range(8):
            TS(k, "dve", gtmp[:], k.onesf[:], k.cond[:, 16 + fc, 0:1], ALU.mult, [k.onesf_t, k.cond_t], gtmp_t)
            pt, pt_t = psum(k)
            MM(k, pt[:, 0:128], gtmp[:], k.ident, [gtmp_t, k.cm_t], pt_t)
            CP(k, "act", gate_rep[:, fc * 128:(fc + 1) * 128], pt[:, 0:128], pt_t, gate_t)
            for s in range(NS):
                TS(k, "dve", gtmp[:, s * ST:(s + 1) * ST], k.onesf[:, 0:ST], k.cond[:, 16 + fc, 1 + s:2 + s], ALU.mult, [k.onesf_t, k.cond_t], gtmp_t)
            pt, pt_t = psum(k)
            MM(k, pt[0:NS * ST, 0:128], gtmp[:, 0:NS * ST], k.ident, [gtmp_t, k.cm_t], pt_t)
            CP(k, "act", gate_s[:, fc * 128:(fc + 1) * 128], pt[0:NS * ST, 0:128], pt_t, gates_t)
        S.barrier()
    hts = [sbt(k, es, "fhT%d" % i, [128, 8, TF], BF16) for i in range(2)]
    yas = [sbt(k, es, "fya%d" % i, [128, 4, TF], BF16) for i in range(2)]
    ybs = [sbt(k, es, "fyb%d" % i, [128, 8, TF], BF16) for i in range(2)]
    lsem = [[S.dsem() for _ in range(3)] for _ in range(2)]
    xts = [sbt(k, es, "fx%d" % i, [128, D], F32) for i in range(2)]
    xsem = [S.dsem() for _ in range(2)]
    ybg, ybg_t = sbt(k, es, "ybg", [128, 8, TF], BF16)
    mTs = [sbt(k, es, "mT%d" % i, [128, 8, TF], BF16) for i in range(2)]
    nhalf, nhalf_t = sbt(k, es, "nhalf", [128, 1], F32)
    MS(k, "dve", nhalf[:], -0.5, nhalf_t)
    sg = [sbt(k, es, "sg%d" % i, [128, TF], F32) for i in range(4)]
    tt_, tt_t = sbt(k, es, "tt", [128, D], F32)
    ys = [sbt(k, es, "fy%d" % i, [128, D], F32) for i in range(2)]
    ysem = [S.dsem() for _ in range(2)]
    stt, stt_t = sbt(k, es, "fstat", [128, 24], F32)
    hTd = k.hT_d.rearrange("(kc p) t -> p kc t", p=128)
    yad = k.yaT_d.rearrange("(kc p) t -> p kc t", p=128)
    ybd = k.ybT_d.rearrange("(kc p) t -> p kc t", p=128)
    xv = I["x"].rearrange("(t p) d -> t p d", p=128)
    ntile = k.opts.get("nftiles", SEQ // TF)
    tiles = [(i * TF, TF) for i in range(ntile)] + [(SEQ, NS * ST)]

    def loads(ti):
        c0, n = tiles[ti]
        b = ti % 2
        dma_in(k, hts[b][0][:, :, 0:n], hTd[:, :, c0:c0 + n], hts[b][1], lsem[b][0], reads=[k.hTd_t])
        dma_in(k, yas[b][0][:, :, 0:n], yad[:, :, c0:c0 + n], yas[b][1], lsem[b][1], reads=[k.yad_t])
        dma_in(k, ybs[b][0][:, :, 0:n], ybd[:, :, c0:c0 + n], ybs[b][1], lsem[b][2], reads=[k.ybd_t])

    xi = 0
    yi = 0
    sgi = 0
    loads(0)
    def s12(ti):
        nonlocal sgi
        c0, n = tiles[ti]
        mT, mT_t = mTs[ti % 2]
        b = ti % 2
        if ti + 1 < len(tiles):
            loads(ti + 1)
        hTt, hTt_t = hts[b]
        ya, ya_t = yas[b]
        yb, yb_t = ybs[b]
        for hb in range(8):
            pt, pt_t = psum(k)
            for kc in range(8):
                MM(k, pt[:, 0:n], wzb[:, kc, hb * 128:(hb + 1) * 128], hTt[:, kc, 0:n], [wzb_t, hTt_t], pt_t, start=(kc == 0), stop=(kc == 7))
            s_, s_t = sg[sgi % 4]
            sgi += 1
            ACT(k, s_[:, 0:n], pt[:, 0:n], AF.Silu, pt_t, s_t)
            TT(k, "dve", ybg[:, hb, 0:n], yb[:, hb, 0:n], s_[:, 0:n], ALU.mult, [yb_t, s_t], ybg_t)
        yield
        for ncb in range(8):
            if ncb == 4:
                yield
            cs = slice(ncb * 128, (ncb + 1) * 128)
            pa, pb, pga, pgb = psum(k), psum(k), psum(k), psum(k)
            for kc in range(4):
                MM(k, pa[0][:, 0:n], wa[:, kc, cs], ya[:, kc, 0:n], [wa_t, ya_t], pa[1], start=(kc == 0), stop=(kc == 3))
            for kc in range(8):
                MM(k, pb[0][:, 0:n], wb[:, kc, cs], ybg[:, kc, 0:n], [wb_t, ybg_t], pb[1], start=(kc == 0), stop=(kc == 7))
            for kc in range(8):
                MM(k, pga[0][:, 0:n], wg[:, kc, cs], hTt[:, kc, 0:n], [wg_t, hTt_t], pga[1], start=(kc == 0), stop=(kc == 7))
            for kc in range(8):
                MM(k, pgb[0][:, 0:n], wg[:, kc, 1024 + ncb * 128:1024 + (ncb + 1) * 128], hTt[:, kc, 0:n], [wg_t, hTt_t], pgb[1], start=(kc == 0), stop=(kc == 7))
            sa, sa_t = sg[sgi % 4]
            sb_, sb_t = sg[(sgi + 1) % 4]
            sgi += 2
            ACT(k, sa[:, 0:n], pga[0][:, 0:n], AF.Sigmoid, pga[1], sa_t)
            ACT(k, sb_[:, 0:n], pgb[0][:, 0:n], AF.Sigmoid, pgb[1], sb_t)
            TT(k, "dve", sa[:, 0:n], pa[0][:, 0:n], sa[:, 0:n], ALU.mult, [pa[1], sa_t], sa_t)
            TT(k, "dve", sb_[:, 0:n], pb[0][:, 0:n], sb_[:, 0:n], ALU.mult, [pb[1], sb_t], sb_t)
            TT(k, "dve", mT[:, ncb, 0:n], sa[:, 0:n], sb_[:, 0:n], ALU.add, [sa_t, sb_t], mT_t)

    def s3(ti):
        nonlocal xi, yi
        c0, n = tiles[ti]
        mT, mT_t = mTs[ti % 2]
        for j0 in range(0, n, 128):
            M = min(128, n - j0)
            samp = c0 >= SEQ
            xt, xt_t = xts[xi % 2]
            if samp:
                dma_in(k, xt[0:M, :], I["xs"], xt_t, xsem[xi % 2])
            else:
                dma_in(k, xt[:], xv[(c0 + j0) // 128], xt_t, xsem[xi % 2])
            xi += 1
            p0, p1 = psum(k), psum(k)
            for half, pp in ((0, p0), (1, p1)):
                for kc in range(8):
                    MM(k, pp[0][0:M, :], mT[:, kc, j0:j0 + M], wo[:, kc, half * 512:(half + 1) * 512], [mT_t, wo_t], pp[1], start=(kc == 0), stop=(kc == 7))
            gr, gr_t = (gate_s, gates_t) if samp else (gate_rep, gate_t)
            TT(k, "dve", tt_[0:M, 0:512], p0[0][0:M, :], gr[0:M, 0:512], ALU.mult, [p0[1], gr_t], tt_t)
            TT(k, "dve", tt_[0:M, 512:1024], p1[0][0:M, :], gr[0:M, 512:1024], ALU.mult, [p1[1], gr_t], tt_t)
            STT(k, tt_[0:M, :], xt[0:M, :], ALPHA, tt_[0:M, :], ALU.mult, ALU.add, [xt_t, tt_t], tt_t)
            k.S.op("dve", lambda e, o=stt[0:M, 0:6], i_=tt_[0:M, 0:512]: e.bn_stats(out=o, in_=i_), [tt_t], [stt_t])
            k.S.op("dve", lambda e, o=stt[0:M, 6:12], i_=tt_[0:M, 512:1024]: e.bn_stats(out=o, in_=i_), [tt_t], [stt_t])
            k.S.op("dve", lambda e, o=stt[0:M, 12:14], i_=stt[0:M, 0:12]: e.bn_aggr(out=o, in_=i_), [stt_t], [stt_t])
            TS(k, "dve", stt[0:M, 14:15], stt[0:M, 13:14], 1e-5, ALU.add, stt_t, stt_t)
            TT(k, "pool", stt[0:M, 15:16], stt[0:M, 14:15], nhalf[0:M, :], ALU.pow, [stt_t, nhalf_t], stt_t)
            y, y_t = ys[yi % 2]
            STT(k, stt[0:M, 16:17], stt[0:M, 12:13], -1.0, stt[0:M, 15:16], ALU.mult, ALU.mult, [stt_t], stt_t)
            ACT(k, y[0:M, :], tt_[0:M, :], AF.Identity, [tt_t, stt_t], y_t, scale=stt[0:M, 15:16], bias=stt[0:M, 16:17])
            TT(k, "pool", y[0:M, :], y[0:M, :], lng[0:M, 0, :], ALU.mult, [y_t, lng_t], y_t)
            TT(k, "pool", y[0:M, :], y[0:M, :], lng[0:M, 1, :], ALU.add, [y_t, lng_t], y_t)
            if samp:
                dma_out(k, O["ys"], y[0:M, :], [y_t], sem=ysem[yi % 2])
            else:
                dma_out(k, O["y"][c0 + j0:c0 + j0 + 128, :], y[:], [y_t], sem=ysem[yi % 2])
            yi += 1
            yield

    for _ in s12(0):
        pass
    for ti in range(len(tiles)):
        g12 = s12(ti + 1) if ti + 1 < len(tiles) else iter(())
        g3 = s3(ti)
        next(g12, None)
        next(g3, None)
        next(g12, None)
        for _ in g3:
            pass
        for _ in g12:
            pass


def t5_causal_buckets(dist):
    n_buckets, max_dist = 32, 2048
    max_exact = n_buckets // 2
    dist = np.asarray(dist, dtype=np.int64)
    ratio = np.maximum(dist, max_exact) / max_exact
    large = max_exact + (np.log(ratio) / math.log(max_dist / max_exact) * (n_buckets - max_exact)).astype(np.int64)
    return np.where(dist < max_exact, dist, np.minimum(large, n_buckets - 1)).astype(np.int32)


def host_consts():
    p = np.arange(128)[:, None]
    f = np.arange(128)[None, :]
    blk = (p // 32) == (f // 32)
    cm = np.stack([(p == f), (p <= f), (f < p) & blk, (f >= p), (f < p) & ~blk]).astype(np.float32)
    return cm


def bias_layout(rel_bias):
    p = np.arange(128)[:, None]
    f = np.arange(256)[None, :]
    j = np.where(f < 128, f - p, f - p)
    valid = np.where(f < 128, j >= 0, j <= 128)
    idx = np.where(valid, j, 129).astype(np.int64)
    out = np.empty((24, 128, 256), np.float32)
    for gi, (window, dil) in enumerate(GROUPS):
        buckets = t5_causal_buckets(dil * np.arange(window // dil + 1))
        for hh in range(8):
            h = gi * 8 + hh
            ext = np.concatenate([rel_bias[buckets, h], np.array([NEG, NEG], np.float32)]).astype(np.float32)
            out[h] = ext[np.minimum(idx, 129)]
    return out


def bias_diag_layout(rel_bias):
    out = np.empty((24, ST, ST), np.float32)
    eye = np.eye(ST, dtype=bool)
    for gi, (window, dil) in enumerate(GROUPS):
        b0 = t5_causal_buckets(np.zeros(1))[0]
        for hh in range(8):
            h = gi * 8 + hh
            ext = np.array([rel_bias[b0, h], NEG], np.float32)
            out[h] = ext[np.where(eye, 0, 1)]
    return out


def make_in_maps(inp):
    f32 = lambda a: np.ascontiguousarray(np.asarray(a, dtype=np.float32))
    cm = host_consts()
    biasT = bias_layout(np.asarray(inp["rel_bias"], np.float32))
    shared = {
        "w_cond": f32(inp["w_cond"][0]),
        "bcondT": f32(np.asarray(inp["b_cond"][0]).reshape(24, 128).T),
        "w_in": f32(inp["w_in"][0]),
        "biasT": biasT,
        "biasD": bias_diag_layout(np.asarray(inp["rel_bias"], np.float32)),
        "convwT": f32(np.asarray(inp["conv_w"][0]).reshape(4, 24, 128).transpose(2, 1, 0)),
        "a_log": f32(inp["a_log"]).reshape(1, 8),
        "dt_bias": f32(inp["dt_bias"]).reshape(1, 8),
        "b_norm_w": f32(inp["b_norm_w"]).reshape(1, 128),
        "w_a": f32(inp["w_branch_a"][0]),
        "w_b": f32(inp["w_branch_b"][0]),
        "w_out": f32(inp["w_out"][0]),
        "ln_g": f32(inp["ln_g"]).reshape(1, D),
        "ln_b": f32(inp["ln_b"]).reshape(1, D),
        "cmats": cm,
    }
    maps = []
    for b in range(8):
        sl = slice(NS * b, NS * b + NS)
        c5 = np.concatenate([np.asarray(inp["c_prompt"][b:b + 1]), np.asarray(inp["c_sample"][sl])], axis=0)
        m = dict(shared)
        m["x"] = f32(inp["x_prompt"][b])
        m["xs"] = f32(np.asarray(inp["x_sample"][sl]).reshape(NS * ST, D))
        m["cT"] = f32(c5.T.reshape(8, 128, 1 + NS).transpose(1, 0, 2))
        m["kv128"] = f32(np.asarray(inp["cache_kv_w128"][0, sl]).reshape(NS, 128, 1024))
        m["kv512"] = f32(np.asarray(inp["cache_kv_w512"][0, sl]).reshape(NS, 512, 1024))
        m["kv2048"] = f32(np.asarray(inp["cache_kv_w2048"][0, sl]).reshape(NS, 2048, 1024))
        m["sconv"] = f32(np.asarray(inp["state_conv"][0, sl]).reshape(NS * 3, 3072))
        m["sdelta"] = f32(np.asarray(inp["state_delta"][0, sl]).reshape(NS * 8, 128, 128))
        maps.append(m)
    return maps


_NC_CACHE = {}


def kernel(**inputs):
    if "nc" not in _NC_CACHE:
        _NC_CACHE["nc"] = build()
    nc = _NC_CACHE["nc"]
    maps = make_in_maps(inputs)
    res = run_bass_kernel_spmd(nc, maps, core_ids=list(range(8))).results
    g = lambda name: [np.asarray(r[name], dtype=np.float32) for r in res]
    y = np.stack(g("y"))
    ys = np.concatenate([a.reshape(NS, ST, D) for a in g("ys")], axis=0)
    outs = [y, ys]
    for w in (128, 512, 2048):
        outs.append(np.stack([a.reshape(w, 2, 8, 64) for a in g("kvp%d" % w)])[None])
    outs.append(np.stack(g("convp"))[None])
    outs.append(np.stack(g("deltap"))[None])
    for w in (128, 512, 2048):
        outs.append(np.concatenate([a.reshape(NS, w, 2, 8, 64) for a in g("kvs%d" % w)], axis=0)[None])
    outs.append(np.concatenate(g("convs"), axis=0)[None])
    outs.append(np.concatenate([a.reshape(NS, 8, 128, 128) for a in g("deltas")], axis=0)[None])
    return tuple(outs)
```

```python
import contextlib
import math
import numpy as np
import ml_dtypes
import concourse.bass as bass
import concourse.mybir as mybir
from concourse.bass_utils import run_bass_kernel_spmd

F32 = mybir.dt.float32
BF16 = mybir.dt.bfloat16
ALU = mybir.AluOpType
AF = mybir.ActivationFunctionType
AX = mybir.AxisListType
ENGS = ("pe", "act", "dve", "pool", "sp")

D = 1024
SEQ = 4096
NS = 4
ST = 4
NTOK = SEQ + NS * ST
PROJ = 11280
OFF_QA, OFF_KA, OFF_VA, OFF_ZA, OFF_QKVB, OFF_ZB, OFF_AB, OFF_GA, OFF_GB = 0, 1536, 3072, 4608, 5120, 8192, 9216, 9232, 10256
GROUPS = ((128, 1), (512, 4), (2048, 16))
ALPHA = 2 ** 0.25
NEG = -30000.0


class Sem:
    def __init__(self, h, step):
        self.h = h
        self.n = 0
        self.step = step


class Tok:
    __slots__ = ("w", "r", "name", "excl")

    def __init__(self, name="", excl=False):
        self.w = None
        self.r = {}
        self.name = name
        self.excl = excl


class Sched:
    def __init__(self, nc, es):
        self.nc = nc
        self.es = es
        self.prog = {e: [] for e in ENGS}
        self.esem = {e: Sem(es.enter_context(nc.semaphore("sem_" + e)), 1) for e in ENGS if e != "sp"}
        self.seen = {e: {} for e in ENGS}
        self.dsems = []
        self.ninstr = 0

    def dsem(self, name=None):
        s = Sem(self.es.enter_context(self.nc.semaphore(name or ("dsem%d" % len(self.dsems)))), 16)
        self.dsems.append(s)
        return s

    def _need(self, eng, deps):
        for s, v in deps.items():
            if eng == "pe" and s is self.esem["pe"]:
                continue
            if self.seen[eng].get(s, 0) < v:
                self.prog[eng].append(("w", s, v))
                self.seen[eng][s] = v

    def op(self, eng, fn, reads=(), writes=(), dsem=None):
        ex = [t for t in reads if t.excl]
        if ex:
            reads = [t for t in reads if not t.excl]
            writes = list(writes) + [t for t in ex if t not in writes]
        deps = {}
        for t in reads:
            if t.w is not None and deps.get(t.w[0], 0) < t.w[1]:
                deps[t.w[0]] = t.w[1]
        for t in writes:
            if t.w is not None and deps.get(t.w[0], 0) < t.w[1]:
                deps[t.w[0]] = t.w[1]
            for s, v in t.r.items():
                if deps.get(s, 0) < v:
                    deps[s] = v
        self._need(eng, deps)
        sem = dsem if dsem is not None else self.esem[eng]
        sem.n += sem.step
        rec = (sem, sem.n)
        self.prog[eng].append(("i", fn, sem))
        self.ninstr += 1
        for t in reads:
            if t.r.get(sem, 0) < sem.n:
                t.r[sem] = sem.n
        for t in writes:
            t.w = rec
            t.r = {}
        return rec

    def barrier(self, engs=ENGS):
        allsems = list(self.esem.values()) + self.dsems
        for e in engs:
            self._need(e, {s: s.n for s in allsems if s.n > 0})

    def emit(self):
        nc = self.nc
        self.barrier(("sp",))
        with nc.Block() as block:
            def run(engname, e):
                for item in self.prog[engname]:
                    if item[0] == "w":
                        e.wait_ge(item[1].h, item[2])
                    else:
                        item[1](e).then_inc(item[2].h, item[2].step)

            @block.tensor
            def _(e):
                run("pe", e)

            @block.scalar
            def _(e):
                run("act", e)

            @block.vector
            def _(e):
                run("dve", e)

            @block.gpsimd
            def _(e):
                run("pool", e)

            @block.sync
            def _(e):
                run("sp", e)


class K:
    pass


def build(dbg=None, phases="0CBAF", opts=None):
    nc = bass.Bass("TRN2", target_bir_lowering=False)
    k = K()
    k.nc = nc
    k.dbg = dbg or {}
    k.opts = opts or {}
    di = lambda name, shape, dt=F32: nc.dram_tensor(name, list(shape), dt, kind="ExternalInput").ap()
    do = lambda name, shape, dt=F32: nc.dram_tensor(name, list(shape), dt, kind="ExternalOutput").ap()
    dsc = lambda name, shape, dt=F32: nc.dram_tensor(name, list(shape), dt).ap()
    I = k.I = {}
    O = k.O = {}
    I["x"] = di("x", [SEQ, D])
    I["xs"] = di("xs", [NS * ST, D])
    I["cT"] = di("cT", [128, 8, 1 + NS])
    I["kv128"] = di("kv128", [NS, 128, 1024])
    I["kv512"] = di("kv512", [NS, 512, 1024])
    I["kv2048"] = di("kv2048", [NS, 2048, 1024])
    I["sconv"] = di("sconv", [NS * 3, 3072])
    I["sdelta"] = di("sdelta", [NS * 8, 128, 128])
    I["w_cond"] = di("w_cond", [D, 3 * D])
    I["bcondT"] = di("bcondT", [128, 24])
    I["w_in"] = di("w_in", [D, PROJ])
    I["biasT"] = di("biasT", [24, 128, 256])
    I["biasD"] = di("biasD", [24, ST, ST])
    I["convwT"] = di("convwT", [128, 24, 4])
    I["a_log"] = di("a_log", [1, 8])
    I["dt_bias"] = di("dt_bias", [1, 8])
    I["b_norm_w"] = di("b_norm_w", [1, 128])
    I["w_a"] = di("w_a", [512, D])
    I["w_b"] = di("w_b", [D, D])
    I["w_out"] = di("w_out", [D, D])
    I["ln_g"] = di("ln_g", [1, D])
    I["ln_b"] = di("ln_b", [1, D])
    I["cmats"] = di("cmats", [5, 128, 128])
    O["y"] = do("y", [SEQ, D])
    O["ys"] = do("ys", [NS * ST, D])
    O["kvp128"] = do("kvp128", [128, 1024])
    O["kvp512"] = do("kvp512", [512, 1024])
    O["kvp2048"] = do("kvp2048", [2048, 1024])
    O["convp"] = do("convp", [3, 3072])
    O["deltap"] = do("deltap", [8, 128, 128])
    O["kvs128"] = do("kvs128", [NS, 128, 1024])
    O["kvs512"] = do("kvs512", [NS, 512, 1024])
    O["kvs2048"] = do("kvs2048", [NS, 2048, 1024])
    O["convs"] = do("convs", [NS, 3, 3072])
    O["deltas"] = do("deltas", [NS * 8, 128, 128])
    k.hT_d = dsc("hT_d", [D, NTOK], BF16)
    k.ybT_d = dsc("ybT_d", [D, NTOK], BF16)
    k.yaT_d = dsc("yaT_d", [512, NTOK], BF16)
    for name, (shape, dt) in k.dbg.items():
        O[name] = do(name, shape, dt)

    with contextlib.ExitStack() as es:
        S = k.S = Sched(nc, es)
        k.es = es
        k.ps = [es.enter_context(nc.psum_tensor("ps%d" % i, [128, 512], F32)) for i in range(8)]
        k.pst = [Tok("ps%d" % i, excl=True) for i in range(8)]
        k.psi = 0
        k.out_sem = S.dsem("out_sem")
        phase0(k)
        if "C" in phases:
            cache_copies(k)
        if "B" in phases:
            with contextlib.ExitStack() as pes:
                phaseB(k, pes)
                S.barrier()
        if "A" in phases:
            with contextlib.ExitStack() as pes:
                phaseA(k, pes)
                S.barrier()
        if "F" in phases:
            with contextlib.ExitStack() as pes:
                phaseF(k, pes)
                S.barrier()
        S.emit()
    return nc


def psum(k, pool=None):
    pools = getattr(k, "pspools", None)
    if pool is None or pools is None:
        i = k.psi
        k.psi = (i + 1) % 8
        return k.ps[i], k.pst[i]
    lst, idx = pools[pool]
    i = lst[idx % len(lst)]
    pools[pool][1] = idx + 1
    return k.ps[i], k.pst[i]


def sbt(k, es, name, shape, dt):
    t = es.enter_context(k.nc.sbuf_tensor("s_" + name, list(shape), dt))
    return t, Tok(name)


def _l(x):
    return list(x) if isinstance(x, (list, tuple)) else [x]


def dma_in(k, out_ap, in_ap, toks, sem, eng="sp", reads=()):
    k.S.op(eng, lambda e: e.dma_start(out=out_ap, in_=in_ap), reads=_l(reads), writes=_l(toks), dsem=sem)


def dma_out(k, out_ap, in_ap, reads, sem=None, eng="sp", writes=()):
    k.S.op(eng, lambda e: e.dma_start(out=out_ap, in_=in_ap), reads=_l(reads), writes=_l(writes), dsem=sem or k.out_sem)


def MM(k, out, lhsT, rhs, reads, writes, start=True, stop=True):
    k.S.op("pe", lambda e: e.matmul(out, lhsT=lhsT, rhs=rhs, start=start, stop=stop), _l(reads), _l(writes))


def TR(k, out, in_, ident, reads, writes):
    k.S.op("pe", lambda e: e.transpose(out, in_, ident), _l(reads), _l(writes))


def ACT(k, out, in_, func, reads, writes, scale=None, bias=None):
    kw = {}
    if scale is not None:
        kw["scale"] = scale
    if bias is not None:
        kw["bias"] = bias
    k.S.op("act", lambda e: e.activation(out=out, in_=in_, func=func, **kw), _l(reads), _l(writes))


def TT(k, eng, out, in0, in1, op, reads, writes):
    k.S.op(eng, lambda e: e.tensor_tensor(out=out, in0=in0, in1=in1, op=op), _l(reads), _l(writes))


def TS(k, eng, out, in0, s1, op0, reads, writes, s2=None, op1=None):
    if op1 is None:
        k.S.op(eng, lambda e: e.tensor_scalar(out=out, in0=in0, scalar1=s1, scalar2=None, op0=op0), _l(reads), _l(writes))
    else:
        k.S.op(eng, lambda e: e.tensor_scalar(out=out, in0=in0, scalar1=s1, scalar2=s2, op0=op0, op1=op1), _l(reads), _l(writes))


def STT(k, out, in0, scalar, in1, op0, op1, reads, writes):
    k.S.op("dve", lambda e: e.scalar_tensor_tensor(out=out, in0=in0, scalar=scalar, in1=in1, op0=op0, op1=op1), _l(reads), _l(writes))


def CP(k, eng, out, in_, reads, writes):
    if eng == "act":
        ACT(k, out, in_, AF.Identity, reads, writes)
    else:
        k.S.op(eng, lambda e: e.tensor_copy(out=out, in_=in_), _l(reads), _l(writes))


def EV(k, out, in_, reads, writes, scale=None):
    k.rr = getattr(k, "rr", 0) + 1
    if k.rr % 2 == 0:
        ACT(k, out, in_, AF.Copy if not isinstance(scale, (int, float)) or True else AF.Copy, reads, writes, scale=scale)
    else:
        if scale is None:
            CP(k, "dve", out, in_, reads, writes)
        else:
            TS(k, "dve", out, in_, scale, ALU.mult, reads, writes)


def MS(k, eng, ap, val, writes):
    k.S.op(eng, lambda e: e.memset(ap, val), [], _l(writes))


def phase0(k):
    nc, S, es, I = k.nc, k.S, k.es, k.I
    k.cm, k.cm_t = sbt(k, es, "cmats", [128, 5, 128], F32)
    k.ident, k.triu, k.ldiag, k.uincl, k.loff = (k.cm[:, i, :] for i in range(5))
    s0 = S.dsem()
    dma_in(k, k.cm[:], I["cmats"].rearrange("m p f -> p m f"), k.cm_t, s0)
    k.identb, k.identb_t = sbt(k, es, "identb", [128, 128], BF16)
    CP(k, "act", k.identb[:], k.ident, k.cm_t, k.identb_t)
    k.onesf, k.onesf_t = sbt(k, es, "onesf", [128, 128], F32)
    MS(k, "dve", k.onesf[:], 1.0, k.onesf_t)
    k.onesb, k.onesb_t = sbt(k, es, "onesb", [128, 128], BF16)
    MS(k, "dve", k.onesb[:], 1.0, k.onesb_t)
    k.epsc, k.epsc_t = sbt(k, es, "epsc", [128, 3], F32)
    MS(k, "dve", k.epsc[:, 0:1], 1e-6, k.epsc_t)
    MS(k, "dve", k.epsc[:, 1:2], 1.0, k.epsc_t)
    MS(k, "dve", k.epsc[:, 2:3], 1e-5, k.epsc_t)
    k.small, k.small_t = sbt(k, es, "small", [128, 16 + 128], F32)
    s1 = S.dsem()
    s2, s3, s4 = S.dsem(), S.dsem(), S.dsem()
    dma_in(k, k.small[:, 0:8], I["a_log"].partition_broadcast(128), k.small_t, s1)
    dma_in(k, k.small[:, 8:16], I["dt_bias"].partition_broadcast(128), k.small_t, s1)
    dma_in(k, k.small[:, 16:144], I["b_norm_w"].partition_broadcast(128), k.small_t, s1)
    k.negA, k.negA_t = sbt(k, es, "negA", [128, 8], F32)
    ACT(k, k.negA[:], k.small[:, 0:8], AF.Exp, k.small_t, k.negA_t)
    TS(k, "dve", k.negA[:], k.negA[:], -1.0, ALU.mult, k.negA_t, k.negA_t)
    k.dtb = k.small[:, 8:16]
    k.bnw = k.small[:, 16:144]
    k.convw, k.convw_t = sbt(k, es, "convw", [128, 24, 4], F32)
    dma_in(k, k.convw[:], I["convwT"], k.convw_t, s2)
    k.cond, k.cond_t = sbt(k, es, "cond", [128, 24, 1 + NS], F32)
    with contextlib.ExitStack() as tes:
        cT, cT_t = sbt(k, tes, "cT", [128, 8, 1 + NS], F32)
        bc, bc_t = sbt(k, tes, "bcond", [128, 24], F32)
        dma_in(k, cT[:], I["cT"], cT_t, s3)
        dma_in(k, bc[:], I["bcondT"], bc_t, s4)
        ACT(k, cT[:], cT[:], AF.Silu, cT_t, cT_t)
        wst = [sbt(k, tes, "wcst%d" % i, [128, 8, 512], F32) for i in range(2)]
        wss = [S.dsem() for _ in range(2)]
        wv = I["w_cond"].rearrange("(kc p) n -> p kc n", p=128)
        for j in range(6):
            w, w_t = wst[j % 2]
            dma_in(k, w[:], wv[:, :, j * 512:(j + 1) * 512], w_t, wss[j % 2])
            pt, pt_t = psum(k)
            for fc in range(4):
                for kc in range(8):
                    MM(k, pt[:, fc * 8:fc * 8 + 1 + NS], w[:, kc, fc * 128:(fc + 1) * 128], cT[:, kc, :], [w_t, cT_t], pt_t, start=(kc == 0), stop=(kc == 7))
            for fc in range(4):
                f = j * 4 + fc
                TS(k, "dve", k.cond[:, f, :], pt[:, fc * 8:fc * 8 + 1 + NS], bc[:, f:f + 1], ALU.add, [pt_t, bc_t], k.cond_t)
        TS(k, "dve", k.cond[:, 8:16, :], k.cond[:, 8:16, :], 1.0, ALU.add, k.cond_t, k.cond_t)
        S.barrier()
    if "cond" in k.dbg:
        dma_out(k, k.O["cond"], k.cond[:], [k.cond_t])


def build_hT_tile(k, xt, xt_t, ntok, hTt, hTt_t, seqs, pool=None):
    pts = [psum(k, pool), psum(k, pool)]
    for kc in range(8):
        pt, pt_t = pts[kc // 4]
        TR(k, pt[:, (kc % 4) * 128:(kc % 4) * 128 + ntok], xt[0:ntok, kc * 128:(kc + 1) * 128], k.ident[0:ntok, 0:ntok], [xt_t, k.cm_t], pt_t)
    for kc in range(8):
        pt, pt_t = pts[kc // 4]
        for (c0, c1, si) in seqs:
            ACT(k, hTt[:, kc, c0:c1], pt[:, (kc % 4) * 128 + c0:(kc % 4) * 128 + c1], AF.Identity, [pt_t, k.cond_t], hTt_t,
                scale=k.cond[:, 8 + kc, si:si + 1], bias=k.cond[:, kc, si:si + 1])


def load_w_bf16(k, es_tmp, dst, dst_t, col0, ncols, chunk=256, name="wst", src=None, nkc=8):
    S = k.S
    wv = (src if src is not None else k.I["w_in"]).rearrange("(kc p) n -> p kc n", p=128)
    st = [sbt(k, es_tmp, "%s%d" % (name, i), [128, nkc, chunk], F32) for i in range(2)]
    ss = [S.dsem() for _ in range(2)]
    j = 0
    c = 0
    while c < ncols:
        n = min(chunk, ncols - c)
        w, w_t = st[j % 2]
        dma_in(k, w[:, :, 0:n], wv[:, :, col0 + c:col0 + c + n], w_t, ss[j % 2])
        CP(k, "pool" if j % 2 == 0 else "act", dst[:, :, c:c + n], w[:, :, 0:n], w_t, dst_t)
        c += n
        j += 1


def phaseB(k, es):
    nc, S, I, O = k.nc, k.S, k.I, k.O
    NH = 8
    wB, wB_t = sbt(k, es, "wB", [128, 8, 3072], BF16)
    wab, wab_t = sbt(k, es, "wab", [128, 8, 16], BF16)
    with contextlib.ExitStack() as tes:
        load_w_bf16(k, tes, wB, wB_t, OFF_QKVB, 3072)
        load_w_bf16(k, tes, wab, wab_t, OFF_AB, 16, chunk=16, name="wabst")
        S.barrier()
    xts = [sbt(k, es, "xt%d" % i, [128, D], F32) for i in range(2)]
    xsem = [S.dsem() for _ in range(2)]
    hTs = [sbt(k, es, "hTt%d" % i, [128, 8, 128], BF16) for i in range(2)]
    hsem = [S.dsem() for _ in range(2)]
    pc, pc_t = sbt(k, es, "pc", [128, 24, 131], F32)
    cv, cv_t = sbt(k, es, "cv", [128, 24, 128], F32)
    tmpa, tmpa_t = sbt(k, es, "tmpa", [128, 8, 128], F32)
    tmpc, tmpc_t = sbt(k, es, "tmpc", [128, 8, 128], F32)
    cv_ts = [Tok("cv%d" % i) for i in range(3)]
    sq2, sq2_t = sbt(k, es, "sq2", [128, 16, 128], BF16)
    rn, rn_t = sbt(k, es, "rn", [128, 16, 128], BF16)
    qkvTs = [sbt(k, es, "qkvT%d" % i, [128, 24, 128], BF16) for i in range(2)]
    kvtoks = [sbt(k, es, "kvtok%d" % i, [128, 16, 128], BF16) for i in range(2)]
    gbs = [sbt(k, es, "gb%d" % i, [128, 56], F32) for i in range(2)]
    Sst, Sst_t = sbt(k, es, "Sst", [128, 8, 128], F32)
    Sbf = [sbt(k, es, "Sbf%d" % i, [128, 8, 128], BF16)[0] for i in range(2)]
    Sh_t = [Tok("S%d" % h) for h in range(NH)]
    Sbf_t = [[Tok("Sbf%d_%d" % (p, h)) for h in range(NH)] for p in range(2)]
    stsem = S.dsem()
    ybT, ybT_t = sbt(k, es, "ybT", [128, 8, 128], BF16)
    ybsem = S.dsem()
    diagW, diagW_t = sbt(k, es, "diagW", [128, 8, 4, 128], F32)
    for c_ in range(8):
        for i_ in range(4):
            TS(k, "dve", diagW[:, c_, i_, :], k.ident, k.convw[:, 16 + c_, i_:i_ + 1], ALU.mult, [k.cm_t, k.convw_t], diagW_t)
    rowmask, rowmask_t = sbt(k, es, "rowmask", [128, 1], F32)
    MS(k, "dve", rowmask[:], 0.0, rowmask_t)
    MS(k, "dve", rowmask[0:ST, :], 1.0, rowmask_t)
    cvsem = S.dsem()
    hist, hist_t = sbt(k, es, "hist", [72, 128], F32)
    histsem = S.dsem()
    NSLOT = 8
    slots = []
    for s in range(NSLOT):
        d = {}
        for nm in ["gTri", "absG", "E", "ecr", "EU"]:
            d[nm] = sbt(k, es, "%s_%d" % (nm, s), [128, 128], F32)
        d["ELd"] = d["gTri"]
        d["ELo"] = d["absG"]
        d["o2"] = d["E"]
        for nm in ["Ad", "Ao", "X0", "X1", "XT0", "XT1", "DT0", "DT1", "qkm", "kt", "qeT"]:
            d[nm] = sbt(k, es, "%s_%d" % (nm, s), [128, 128], BF16)
        d["yb"] = d["Ad"]
        d["nwT"] = d["X0"]
        d["vnew"] = d["X1"]
        d["nNT"] = d["XT0"]
        for nm in ["R", "Xa", "Xb"]:
            d[nm] = sbt(k, es, "%s_%d" % (nm, s), [128, 256], BF16)
        d["st"] = sbt(k, es, "st_%d" % s, [128, 8], F32)
        slots.append(d)

    def T_(h, nm):
        return slots[h % NSLOT][nm][0]

    def Tt(h, nm):
        return slots[h % NSLOT][nm][1]

    MS(k, "pool", Sst[:], 0.0, [Sst_t] + Sh_t)
    MS(k, "pool", Sbf[0][:], 0.0, Sbf_t[0])
    MS(k, "pool", pc[:, :, 0:3], 0.0, pc_t)
    par = [0] * NH

    xv = I["x"].rearrange("(t p) d -> t p d", p=128)
    hTd = k.hT_d.rearrange("(kc p) t -> p kc t", p=128)
    ybd = k.ybT_d.rearrange("(h p) t -> p h t", p=128)
    k.hTd_t = Tok("hTd")
    k.ybd_t = Tok("ybd")

    def load_x(t):
        dma_in(k, xts[t % 2][0][:], xv[t], xts[t % 2][1], xsem[t % 2])

    ntiles = k.opts.get("ntiles", 32)
    nvirt = k.opts.get("nvirt", NS)
    vt_list = list(range(ntiles)) + [("s", s) for s in range(nvirt)]
    load_x(0)
    def front(ti, t):
        virt = isinstance(t, tuple)
        qkvT, qkvT_t = qkvTs[ti % 2]
        kvtok, kvtok_t = kvtoks[ti % 2]
        gb, gb_t = gbs[ti % 2]
        virt = isinstance(t, tuple)
        hTt, hTt_t = hTs[ti % 2]
        xt, xt_t = xts[ti % 2]
        if not virt:
            if t + 1 < ntiles:
                load_x(t + 1)
            build_hT_tile(k, xt, xt_t, 128, hTt, hTt_t, [(0, 128, 0)], pool="f")
            dma_out(k, hTd[:, :, t * 128:(t + 1) * 128], hTt[:], [hTt_t], sem=hsem[ti % 2], writes=[k.hTd_t])
        else:
            s = t[1]
            dma_in(k, xt[0:ST, :], I["xs"][s * ST:(s + 1) * ST, :], xt_t, xsem[ti % 2])
            MS(k, "pool", hTt[:], 0.0, hTt_t)
            build_hT_tile(k, xt, xt_t, ST, hTt, hTt_t, [(0, ST, 1 + s)], pool="f")
            dma_out(k, hTd[:, :, SEQ + s * ST:SEQ + (s + 1) * ST], hTt[:, :, 0:ST], [hTt_t], sem=hsem[ti % 2], writes=[k.hTd_t])
            dma_in(k, hist[:], I["sconv"][s * 3:(s + 1) * 3, :].rearrange("r (c f) -> (r c) f", f=128), hist_t, histsem)
            pt, pt_t = psum(k, "f")
            TR(k, pt[:, 0:72], hist[:], k.ident[0:72, 0:72], [hist_t, k.cm_t], pt_t)
            CP(k, "dve", pc[:, :, 0:3], pt[:, 0:72].rearrange("p (r c) -> p c r", r=3), pt_t, pc_t)
        if k.opts.get('stopB', 9) <= 1:
            return
        yield
        pt, pt_t = psum(k, "f")
        for kc in range(8):
            MM(k, pt[:, 0:16], hTt[:, kc, :], wab[:, kc, :], [hTt_t, wab_t], pt_t, start=(kc == 0), stop=(kc == 7))
        ACT(k, gb[:, 8:16], pt[:, 8:16], AF.Sigmoid, pt_t, gb_t)
        TT(k, "dve", gb[:, 40:48], pt[:, 0:8], k.dtb, ALU.add, [pt_t, k.small_t], gb_t)
        ACT(k, gb[:, 40:48], gb[:, 40:48], AF.Exp, gb_t, gb_t)
        ACT(k, gb[:, 40:48], gb[:, 40:48], AF.Ln, [gb_t, k.epsc_t], gb_t, bias=k.epsc[:, 1:2])
        TT(k, "dve", gb[:, 0:8], gb[:, 40:48], k.negA[:], ALU.mult, [gb_t, k.negA_t], gb_t)
        if virt:
            TS(k, "dve", gb[:, 0:16], gb[:, 0:16], rowmask[:, 0:1], ALU.mult, [gb_t, rowmask_t], gb_t)
        pt, pt_t = psum(k, "f")
        MM(k, pt[:, 0:8], k.triu, gb[:, 0:8], [gb_t, k.cm_t], pt_t)
        CP(k, "dve", gb[:, 16:24], pt[:, 0:8], pt_t, gb_t)
        ACT(k, gb[:, 24:32], gb[:, 16:24], AF.Exp, gb_t, gb_t)
        TS(k, "dve", gb[:, 48:56], pt[:, 0:8], -1.0, ALU.mult, pt_t, gb_t)
        TT(k, "dve", gb[:, 32:40], gb[:, 24:32], gb[:, 8:16], ALU.mult, gb_t, gb_t)
        if k.opts.get('stopB', 9) <= 2:
            return
        yield
        for cg in range(6):
            pt, pt_t = psum(k, "f")
            for c4 in range(4):
                cc = cg * 4 + c4
                for kc in range(8):
                    MM(k, pt[:, c4 * 128:(c4 + 1) * 128], wB[:, kc, cc * 128:(cc + 1) * 128], hTt[:, kc, :], [hTt_t, wB_t], pt_t, start=(kc == 0), stop=(kc == 7))
            CP(k, "act" if cg % 2 == 0 else "dve", pc[:, cg * 4:cg * 4 + 4, 3:131], pt[:].rearrange("p (c f) -> p c f", f=128), pt_t, pc_t)
            yield
        if virt or t == ntiles - 1:
            dstc = O["convs"][t[1]] if virt else O["convp"]
            srcc = pc[:, :, 4:7] if virt else pc[:, :, 128:131]
            for rr in range(3):
                k.S.op("sp", lambda e, o=dstc[rr, :].rearrange("(c p) -> p c", p=128), i_=srcc[:, :, rr]: e.dma_start(out=o, in_=i_, allow_slow_non_contiguous=True), [pc_t], [], dsem=cvsem)
        yield
        for th, e_, ta, ta_t in ((0, "pool", tmpa, tmpa_t), (1, "dve", tmpc, tmpc_t)):
            cs = slice(th * 8, th * 8 + 8)
            wb_ = [k.convw[:, cs, i:i + 1].broadcast_to([128, 8, 128]) for i in range(4)]
            TT(k, e_, cv[:, cs, :], pc[:, cs, 0:128], wb_[0], ALU.mult, [pc_t, k.convw_t], cv_ts[th])
            for i in (1, 2, 3):
                TT(k, e_, ta[:], pc[:, cs, i:i + 128], wb_[i], ALU.mult, [pc_t, k.convw_t], ta_t)
                TT(k, e_, cv[:, cs, :], cv[:, cs, :], ta[:], ALU.add, [ta_t, cv_ts[th]], cv_ts[th])
            yield
        for half in range(2):
            pt, pt_t = psum(k, "f")
            for j in range(4):
                c_ = half * 4 + j
                for i in range(4):
                    MM(k, pt[:, j * 128:(j + 1) * 128], diagW[:, c_, i, :], pc[:, 16 + c_, i:i + 128], [diagW_t, pc_t], pt_t, start=(i == 0), stop=(i == 3))
            ACT(k, cv[:, 16 + half * 4:16 + half * 4 + 4, :], pt[:].rearrange("p (c f) -> p c f", f=128), AF.Silu, pt_t, cv_ts[2])
            yield
        CP(k, "pool", pc[:, :, 0:3], pc[:, :, 128:131], pc_t, pc_t)
        for _ in range(k.opts.get('cwait', 4)):
            yield
        for th in (1, 0):
            ACT(k, cv[:, th * 8:th * 8 + 8, :], cv[:, th * 8:th * 8 + 8, :], AF.Silu, cv_ts[th], cv_ts[th])
            yield
        yield
        TT(k, "pool", sq2[:], cv[:, 0:16, :], cv[:, 0:16, :], ALU.mult, cv_ts[0:2], sq2_t)
        for _ in range(3):
            yield
        for j in range(4):
            pt, pt_t = psum(k, "f")
            MM(k, pt[:], k.onesb[:], sq2[:, j * 4:(j + 1) * 4, :], [sq2_t, k.onesb_t], pt_t)
            ACT(k, rn[:, j * 4:(j + 1) * 4, :], pt[:].rearrange("p (c f) -> p c f", f=128), AF.Ln, [pt_t, k.epsc_t], rn_t, bias=k.epsc[:, 0:1])
        yield
        ACT(k, rn[:], rn[:], AF.Exp, rn_t, rn_t, scale=-0.5)
        yield
        STT(k, qkvT[:, 0:8, :], cv[:, 0:8, :], 128 ** -0.5, rn[:, 0:8, :], ALU.mult, ALU.mult, [cv_ts[0], rn_t], qkvT_t)
        TT(k, "dve", qkvT[:, 8:16, :], cv[:, 8:16, :], rn[:, 8:16, :], ALU.mult, [cv_ts[1], rn_t], qkvT_t)
        CP(k, "act", qkvT[:, 16:24, :], cv[:, 16:24, :], cv_ts[2], qkvT_t)
        yield
        if "qkvT" in k.dbg and ti == k.opts.get("dbg_tile", 0):
            dma_out(k, O["qkvT"], qkvT[:], [qkvT_t])
            dma_out(k, O["gb"], gb[:], [gb_t])
        yield
        for g4 in range(4):
            pt, pt_t = psum(k, "f")
            ptb = pt[:].bitcast(BF16)
            for c4 in range(4):
                TR(k, ptb[:, c4 * 128:(c4 + 1) * 128], qkvT[:, 8 + g4 * 4 + c4, :], k.identb[:], [qkvT_t, k.identb_t], pt_t)
            CP(k, "act" if g4 % 2 == 0 else "dve", kvtok[:, g4 * 4:(g4 + 1) * 4, :], ptb[:, 0:512].rearrange("p (c f) -> p c f", f=128), pt_t, kvtok_t)
        if k.opts.get('stopB', 9) <= 4:
            return

    def units(ti, t):
        virt = isinstance(t, tuple)
        qkvT, qkvT_t = qkvTs[ti % 2]
        kvtok, kvtok_t = kvtoks[ti % 2]
        gb, gb_t = gbs[ti % 2]
        if virt:
            s = t[1]
            dst = O["deltap"] if s == 0 else O["deltas"][(s - 1) * 8:s * 8]
            dma_out(k, dst.rearrange("h k v -> k h v"), Sst[:], [Sst_t] + Sh_t, sem=stsem)
            dma_in(k, Sst[:], I["sdelta"][s * 8:(s + 1) * 8].rearrange("h k v -> k h v"), [Sst_t] + Sh_t, stsem)
            for h in range(NH):
                CP(k, "act", Sbf[par[h]][:, h, :], Sst[:, h, :], Sh_t[h], Sbf_t[par[h]][h])
        yield
        if k.opts.get('stopB', 9) <= 4:
            return
        heads = list(range(NH))

        def stage(mm_fn, ev_fn, chunk=k.opts.get('chunk', 4)):
            for c0 in range(0, NH, chunk):
                banks = {}
                for h in heads[c0:c0 + chunk]:
                    banks[h] = psum(k, "u")
                    mm_fn(h, banks[h][0], banks[h][1])
                for h in heads[c0:c0 + chunk]:
                    ev_fn(h, banks[h][0], banks[h][1])
                yield

        for h in heads:
            EV(k, T_(h, "gTri")[:], k.triu, [k.cm_t, gb_t], Tt(h, "gTri"), scale=gb[:, h:h + 1])
        yield

        def mm(h, pt, pt_t):
            MM(k, pt[:, 0:128], k.onesf[:], T_(h, "gTri")[:], [Tt(h, "gTri"), k.onesf_t], pt_t)

        def ev(h, pt, pt_t):
            ACT(k, T_(h, "absG")[:], pt[:, 0:128], AF.Abs, [pt_t, gb_t], Tt(h, "absG"), bias=gb[:, 48 + h:49 + h])
            ACT(k, T_(h, "ecr")[:], pt[:, 0:128], AF.Exp, pt_t, Tt(h, "ecr"))
            ACT(k, T_(h, "E")[:], T_(h, "absG")[:], AF.Exp, Tt(h, "absG"), Tt(h, "E"), scale=-1.0)
        yield from stage(mm, ev)
        yield
        for h in heads:
            EV(k, T_(h, "kt")[:], kvtok[:, h, :], [kvtok_t, Tt(h, "E")], Tt(h, "kt"), scale=T_(h, "E")[:, 127:128])
            TT(k, "pool", T_(h, "qeT")[:], qkvT[:, h, :], T_(h, "ecr")[:], ALU.mult, [qkvT_t, Tt(h, "ecr")], Tt(h, "qeT"))
            TS(k, "dve", T_(h, "R")[:, 0:128], kvtok[:, 8 + h, :], gb[:, 8 + h:9 + h], ALU.mult, [kvtok_t, gb_t], Tt(h, "R"))
            TS(k, "dve", T_(h, "R")[:, 128:256], kvtok[:, h, :], gb[:, 32 + h:33 + h], ALU.mult, [kvtok_t, gb_t], Tt(h, "R"))
        yield

        def mm(h, pt, pt_t):
            MM(k, pt[:, 0:128], qkvT[:, 8 + h, :], qkvT[:, 8 + h, :], qkvT_t, pt_t)
            MM(k, pt[:, 128:256], qkvT[:, 8 + h, :], qkvT[:, h, :], qkvT_t, pt_t)

        def ev(h, pt, pt_t):
            TT(k, "dve", T_(h, "ELd")[:], pt[:, 0:128], T_(h, "E")[:], ALU.mult, [pt_t, Tt(h, "E")], Tt(h, "ELd"))
            TT(k, "dve", T_(h, "EU")[:], pt[:, 128:256], T_(h, "E")[:], ALU.mult, [pt_t, Tt(h, "E")], Tt(h, "EU"))
            STT(k, T_(h, "Ad")[:], T_(h, "ELd")[:], gb[:, 8 + h:9 + h], k.ldiag, ALU.mult, ALU.mult, [gb_t, Tt(h, "ELd"), k.cm_t], Tt(h, "Ad"))
            STT(k, T_(h, "Ao")[:], T_(h, "ELd")[:], gb[:, 8 + h:9 + h], k.loff, ALU.mult, ALU.mult, [gb_t, Tt(h, "ELd"), k.cm_t], Tt(h, "Ao"))
            TT(k, "pool", T_(h, "qkm")[:], T_(h, "EU")[:], k.uincl, ALU.mult, [Tt(h, "EU"), k.cm_t], Tt(h, "qkm"))
        yield from stage(mm, ev)
        yield

        def mm(h, pt, pt_t):
            TR(k, pt[:].bitcast(BF16)[:, 0:128], T_(h, "Ad")[:], k.identb[:], [Tt(h, "Ad"), k.identb_t], pt_t)

        def ev(h, pt, pt_t):
            ptb = pt[:].bitcast(BF16)[:, 0:128]
            EV(k, T_(h, "XT0")[:], ptb, pt_t, Tt(h, "XT0"))
            TT(k, "dve", T_(h, "DT0")[:], k.identb[:], ptb, ALU.subtract, [pt_t, k.identb_t], Tt(h, "DT0"))
        yield from stage(mm, ev)
        yield
        names = [("Ad", "XT0")] + [("X%d" % (kk % 2), "XT%d" % ((kk + 1) % 2)) for kk in range(4)]
        dts = ["DT0", "DT1", "DT0", "DT1", "DT0"]

        def emit_sq(kk):
            Xc, XTc = names[kk]
            Xn, XTn = names[kk + 1]
            last = kk == 3

            def mm(h, pt, pt_t):
                MM(k, pt[:, 0:128], T_(h, XTc)[:], T_(h, Xc)[:], [Tt(h, XTc), Tt(h, Xc)], pt_t)
                if not last:
                    MM(k, pt[:, 128:256], T_(h, Xc)[:], T_(h, XTc)[:], [Tt(h, XTc), Tt(h, Xc)], pt_t)

            def ev(h, pt, pt_t):
                if last:
                    EV(k, T_(h, Xn)[:], pt[:, 0:128], pt_t, Tt(h, Xn))
                else:
                    e_ = "act" if h % 2 == 0 else "dve"
                    CP(k, e_, T_(h, Xn)[:], pt[:, 0:128], pt_t, Tt(h, Xn))
                    CP(k, e_, T_(h, XTn)[:], pt[:, 128:256], pt_t, Tt(h, XTn))
            yield from stage(mm, ev)

        def emit_dt(kk):
            Xn = names[kk + 1][0]
            DTc_, DTn = dts[kk], dts[kk + 1]

            def mm(h, pt, pt_t):
                MM(k, pt[:, 0:128], T_(h, Xn)[:], T_(h, DTc_)[:], [Tt(h, Xn), Tt(h, DTc_)], pt_t)

            def ev(h, pt, pt_t):
                TT(k, "dve", T_(h, DTn)[:], T_(h, DTc_)[:], pt[:, 0:128], ALU.add, [pt_t, Tt(h, DTc_)], Tt(h, DTn))
            yield from stage(mm, ev)
        for step in (("sq", 0), ("sq", 1), ("dt", 0), ("sq", 2), ("dt", 1), ("sq", 3), ("dt", 2), ("dt", 3)):
            yield from (emit_sq if step[0] == "sq" else emit_dt)(step[1])
        DTc = dts[4]

        def mm(h, pt, pt_t):
            MM(k, pt[:, 0:128], T_(h, "Ao")[:], T_(h, DTc)[:], [Tt(h, "Ao"), Tt(h, DTc)], pt_t)

        def ev(h, pt, pt_t):
            EV(k, T_(h, "nNT")[:], pt[:, 0:128], pt_t, Tt(h, "nNT"), scale=-1.0)
        yield from stage(mm, ev)
        yield
        Xcur = "Xa"
        for it in range(4):
            prev = "Xb" if Xcur == "Xa" else "Xa"

            def mm(h, pt, pt_t):
                MM(k, pt[:, 0:256], T_(h, DTc)[:], T_(h, "R")[:], [Tt(h, DTc), Tt(h, "R")], pt_t, start=True, stop=(it == 0))
                if it > 0:
                    MM(k, pt[:, 0:256], T_(h, "nNT")[:], T_(h, prev)[:], [Tt(h, "nNT"), Tt(h, prev)], pt_t, start=False, stop=True)

            def ev(h, pt, pt_t):
                CP(k, "act" if h % 2 == 0 else "dve", T_(h, Xcur)[:], pt[:, 0:256], pt_t, Tt(h, Xcur))
            yield from stage(mm, ev)
            Xfin = Xcur
            Xcur = prev
            yield

        def mm(h, pt, pt_t):
            TR(k, pt[:].bitcast(BF16)[:, 0:128], T_(h, Xfin)[:, 128:256], k.identb[:], [Tt(h, Xfin), k.identb_t], pt_t)

        def ev(h, pt, pt_t):
            EV(k, T_(h, "nwT")[:], pt[:].bitcast(BF16)[:, 0:128], pt_t, Tt(h, "nwT"), scale=-1.0)
        yield from stage(mm, ev)
        yield

        def mm(h, pt, pt_t):
            MM(k, pt[:, 0:128], T_(h, "nwT")[:], Sbf[par[h]][:, h, :], [Tt(h, "nwT"), Sbf_t[par[h]][h]], pt_t)

        def ev(h, pt, pt_t):
            TT(k, "dve", T_(h, "vnew")[:], T_(h, Xfin)[:, 0:128], pt[:, 0:128], ALU.add, [pt_t, Tt(h, Xfin)], Tt(h, "vnew"))
        yield from stage(mm, ev)
        yield

        def mm(h, pt, pt_t):
            p = par[h]
            MM(k, pt[:, 0:128], T_(h, "qeT")[:], Sbf[p][:, h, :], [Tt(h, "qeT"), Sbf_t[p][h]], pt_t, start=True, stop=False)
            MM(k, pt[:, 0:128], T_(h, "qkm")[:], T_(h, "vnew")[:], [Tt(h, "qkm"), Tt(h, "vnew")], pt_t, start=False, stop=True)
            MM(k, pt[:, 128:256], T_(h, "kt")[:], T_(h, "vnew")[:], [Tt(h, "kt"), Tt(h, "vnew")], pt_t)

        def ev(h, pt, pt_t):
            p = par[h]
            st, st_t = T_(h, "st"), Tt(h, "st")
            STT(k, Sst[:, h, :], Sst[:, h, :], T_(h, "ecr")[:, 127:128], pt[:, 128:256], ALU.mult, ALU.add, [pt_t, Tt(h, "ecr"), Sh_t[h]], Sh_t[h])
            EV(k, Sbf[1 - p][:, h, :], Sst[:, h, :], Sh_t[h], Sbf_t[1 - p][h])
            par[h] = 1 - p
            ACT(k, T_(h, "o2")[:], pt[:, 0:128], AF.Square, pt_t, Tt(h, "o2"))
            k.S.op("dve", lambda e, o=st[:, 0:1], i=T_(h, "o2")[:]: e.tensor_reduce(out=o, in_=i, axis=AX.X, op=ALU.add), [Tt(h, "o2")], [st_t])
            ACT(k, st[:, 1:2], st[:, 0:1], AF.Ln, [st_t, k.epsc_t], st_t, scale=1.0 / 128, bias=k.epsc[:, 0:1])
            ACT(k, st[:, 2:3], st[:, 1:2], AF.Exp, st_t, st_t, scale=-0.5)
            STT(k, T_(h, "yb")[:], pt[:, 0:128], st[:, 2:3], k.bnw, ALU.mult, ALU.mult, [pt_t, st_t, k.small_t], Tt(h, "yb"))
        yield from stage(mm, ev)
        yield

        def mm(h, pt, pt_t):
            TR(k, pt[:].bitcast(BF16)[:, 0:128], T_(h, "yb")[:], k.identb[:], [Tt(h, "yb"), k.identb_t], pt_t)

        def ev(h, pt, pt_t):
            EV(k, ybT[:, h, :], pt[:].bitcast(BF16)[:, 0:128], pt_t, ybT_t)
        yield from stage(mm, ev)
        if not virt:
            dma_out(k, ybd[:, :, t * 128:(t + 1) * 128], ybT[:], [ybT_t], sem=ybsem, writes=[k.ybd_t])
        else:
            s = t[1]
            dma_out(k, ybd[:, :, SEQ + s * ST:SEQ + (s + 1) * ST], ybT[:, :, 0:ST], [ybT_t], sem=ybsem, writes=[k.ybd_t])

    k.pspools = {"u": [[0, 1, 2, 3, 4, 5], 0], "f": [[6, 7], 0]}
    for _ in front(0, vt_list[0]):
        pass
    for ti, t in enumerate(vt_list):
        gf = front(ti + 1, vt_list[ti + 1]) if ti + 1 < len(vt_list) else None
        for _ in units(ti, t):
            if gf is not None:
                if next(gf, "done") == "done":
                    gf = None
        if gf is not None:
            for _ in gf:
                pass
    k.pspools = None
    dst = O["deltap"] if nvirt == 0 else O["deltas"][(nvirt - 1) * 8:nvirt * 8]
    dma_out(k, dst.rearrange("h k v -> k h v"), Sst[:], [Sst_t] + Sh_t, sem=stsem)
    if "ybT" in k.dbg:
        dma_out(k, O["ybT"], k.ybT_d, [k.ybd_t])
class Stg:
    def __init__(self, k, es, nkc, chunk, name):
        self.k = k
        self.nkc = nkc
        self.chunk = chunk
        self.st = [sbt(k, es, "%s%d" % (name, i), [128, nkc, chunk], F32) for i in range(2)]
        self.ss = [k.S.dsem() for _ in range(2)]
        self.j = 0

    def load(self, dst, dst_t, dcol0, src, col0, ncols):
        k = self.k
        wv = src.rearrange("(kc p) n -> p kc n", p=128)
        c = 0
        while c < ncols:
            n = min(self.chunk, ncols - c)
            w, w_t = self.st[self.j % 2]
            dma_in(k, w[:, :, 0:n], wv[:, :, col0 + c:col0 + c + n], w_t, self.ss[self.j % 2])
            CP(k, "pool" if self.j % 2 == 0 else "act", dst[:, :, dcol0 + c:dcol0 + c + n], w[:, :, 0:n], w_t, dst_t)
            c += n
            self.j += 1


def subseq_chunks(d, maxlen):
    L = SEQ // d
    for r in range(d):
        for u0 in range(0, L, maxlen):
            n = min(maxlen, L - u0)
            yield (r * L + u0, r + d * u0, n)


def tslice(tok0, n, d):
    return slice(tok0, tok0 + d * (n - 1) + 1, d)


def cache_copies(k):
    for (W, d) in GROUPS:
        src, dst = k.I["kv%d" % W], k.O["kvs%d" % W]
        for s in range(NS):
            r = 0
            while r < W - ST:
                n = min(256, W - ST - r)
                dma_out(k, dst[s, r:r + n, :], src[s, r + ST:r + ST + n, :], [], eng="pool")
                r += n


def phaseA(k, es):
    nc, S, I, O = k.nc, k.S, k.I, k.O
    hT, hT_t = sbt(k, es, "hT", [128, 8, NTOK], BF16)
    hsem = S.dsem()
    hTd = k.hT_d.rearrange("(kc p) t -> p kc t", p=128)
    for kc in range(8):
        dma_in(k, hT[:, kc, :], hTd[:, kc, :], hT_t, hsem, reads=[k.hTd_t])
    if not k.opts.get("skip_kv"):
        with contextlib.ExitStack() as tes:
            wkvs = [sbt(k, tes, "wkv%d" % i, [128, 8, 1024], BF16) for i in range(2)]
            okv = [sbt(k, tes, "okv%d" % i, [128, 1024], F32) for i in range(2)]
            oks = [S.dsem() for _ in range(2)]
            stg = Stg(k, tes, 8, 256, "kvst")
            j = 0
            def load_kv(g):
                w_, w_t_ = wkvs[g % 2]
                stg.load(w_, w_t_, 0, I["w_in"], OFF_KA + g * 512, 512)
                stg.load(w_, w_t_, 512, I["w_in"], OFF_VA + g * 512, 512)
            load_kv(0)
            for g, (W, d) in enumerate(GROUPS):
                wkv, wkv_t = wkvs[g % 2]
                if g + 1 < len(GROUPS):
                    load_kv(g + 1)
                t0 = 32 - W // 128
                for tt in list(range(t0, 32)) + [-1]:
                    M = 128 if tt >= 0 else NS * ST
                    cs = slice(tt * 128, (tt + 1) * 128) if tt >= 0 else slice(SEQ, NTOK)
                    pa, pb = psum(k), psum(k)
                    for half, pp in ((0, pa), (1, pb)):
                        for kc in range(8):
                            MM(k, pp[0][0:M, :], hT[:, kc, cs], wkv[:, kc, half * 512:(half + 1) * 512], [hT_t, wkv_t], pp[1], start=(kc == 0), stop=(kc == 7))
                    o, o_t = okv[j % 2]
                    CP(k, "act", o[0:M, 0:512], pa[0][0:M, :], pa[1], o_t)
                    CP(k, "dve", o[0:M, 512:1024], pb[0][0:M, :], pb[1], o_t)
                    if tt >= 0:
                        dma_out(k, O["kvp%d" % W][(tt - t0) * 128:(tt - t0 + 1) * 128, :], o[:], [o_t], sem=oks[j % 2])
                    else:
                        for s in range(NS):
                            dma_out(k, O["kvs%d" % W][s, W - ST:W, :], o[s * ST:(s + 1) * ST, :], [o_t], sem=oks[j % 2])
                    j += 1
            S.barrier()
    if k.opts.get("skip_attn"):
        return
    UA, UA_t = sbt(k, es, "UaccA", [128, NTOK], F32)
    UB, UB_t = sbt(k, es, "UaccB", [128, NTOK], F32)
    Us = ((UA, UA_t), (UB, UB_t))
    qT, qT_t = sbt(k, es, "qT", [128, NTOK], BF16)
    kT, kT_t = sbt(k, es, "kT", [128, NTOK], BF16)
    vT, vT_t = sbt(k, es, "vT", [128, SEQ], BF16)
    Vaug, Vaug_t = sbt(k, es, "Vaug", [128, 32, 2, 128], BF16)
    Vsn, Vsn_t = sbt(k, es, "Vsn", [ST, NS, 2, 128], BF16)
    vcs = [sbt(k, es, "vc%d" % i, [128, 2, 128], BF16) for i in range(8)]
    kcTs = [sbt(k, es, "kcT%d" % i, [128, 128], BF16) for i in range(8)]
    cks = [sbt(k, es, "ck%d" % i, [128, 2, 128], F32) for i in range(8)]
    cksem = [S.dsem() for _ in range(8)]
    Ps = [sbt(k, es, "Ps%d" % i, [128, 32], BF16) for i in range(8)]
    NPT = 6
    wqs = [sbt(k, es, "wq%d" % i, [128, 8, 384], BF16) for i in range(2)]
    wza, wza_t = sbt(k, es, "wza", [128, 8, 128], BF16)
    stg = Stg(k, es, 8, 128, "ast")
    Bt, Bt_t = sbt(k, es, "Bt", [128, 2, 256], F32)
    BD, BD_t = sbt(k, es, "BD", [ST, 2, ST], F32)
    bsem = S.dsem()
    bdsem = S.dsem()
    PT = [[sbt(k, es, "PT%d_%d" % (hd, p), [128, 256], BF16) for p in range(NPT)] for hd in range(2)]
    tbs = [sbt(k, es, "tb%d" % i, [128, 256], F32) for i in range(8)]
    sz, sz_t = sbt(k, es, "sz", [128, 512], F32)
    rec, rec_t = sbt(k, es, "rec", [128, 512], F32)
    t1, t1_t = sbt(k, es, "t1", [128, 512], F32)
    yats = [sbt(k, es, "yat%d" % i, [128, 512], BF16) for i in range(2)]
    yasem = [S.dsem() for _ in range(2)]
    k.yad_t = Tok("yad")
    MS(k, "pool", Vaug[:, :, 0, 64:128], 1.0, Vaug_t)
    MS(k, "pool", Vaug[:, :, 1, 0:64], 1.0, Vaug_t)
    MS(k, "pool", Vsn[:, :, 0, 64:128], 1.0, Vsn_t)
    MS(k, "pool", Vsn[:, :, 1, 0:64], 1.0, Vsn_t)
    for vc, vc_t in vcs:
        MS(k, "pool", vc[:, 0, 64:128], 1.0, vc_t)
        MS(k, "pool", vc[:, 1, 0:64], 1.0, vc_t)
    tbi = 0
    cki = 0
    yi = 0
    nsp = k.opts.get("nsp", 4)
    pairs = [(sp_, g_) for sp_ in range(nsp) for g_ in range(len(GROUPS))]

    def load_wq(idx):
        sp_, g_ = pairs[idx]
        w_, w_t_ = wqs[idx % 2]
        stg.load(w_, w_t_, 0, I["w_in"], OFF_QA + g_ * 512 + sp_ * 128, 128)
        stg.load(w_, w_t_, 128, I["w_in"], OFF_KA + g_ * 512 + sp_ * 128, 128)
        stg.load(w_, w_t_, 256, I["w_in"], OFF_VA + g_ * 512 + sp_ * 128, 128)
    if pairs:
        load_wq(0)
    pidx = -1
    for sp in range(nsp):
        for g, (W, d) in enumerate(GROUPS):
            pidx += 1
            wq, wq_t = wqs[pidx % 2]
            L = SEQ // d
            nb = L // 128
            for hd in range(2):
                dma_in(k, Bt[:, hd, :], I["biasT"][g * 8 + 2 * sp + hd], Bt_t, bsem)
                dma_in(k, BD[:, hd, :], I["biasD"][g * 8 + 2 * sp + hd], BD_t, bdsem)
            for c0 in range(0, SEQ, 512):
                for which, dstT, dst_t in ((0, qT, qT_t), (1, kT, kT_t), (2, vT, vT_t)):
                    pt, pt_t = psum(k)
                    for kc in range(8):
                        MM(k, pt[:, 0:512], wq[:, kc, which * 128:(which + 1) * 128], hT[:, kc, c0:c0 + 512], [wq_t, hT_t], pt_t, start=(kc == 0), stop=(kc == 7))
                    ov = dstT[:, 0:SEQ].rearrange("p (r u) -> p r u", r=d)[:, :, c0 // d:(c0 + 512) // d]
                    iv = pt[:, 0:512].rearrange("p (a r) -> p r a", r=d)
                    if which == 0:
                        ACT(k, ov, iv, AF.Copy, pt_t, dst_t, scale=0.125)
                    elif which == 1:
                        CP(k, "dve", ov, iv, pt_t, dst_t)
                    else:
                        EV(k, ov, iv, pt_t, dst_t)
            for which, dstT, dst_t in ((0, qT, qT_t), (1, kT, kT_t)):
                pt, pt_t = psum(k)
                for kc in range(8):
                    MM(k, pt[:, 0:NS * ST], wq[:, kc, which * 128:(which + 1) * 128], hT[:, kc, SEQ:NTOK], [wq_t, hT_t], pt_t, start=(kc == 0), stop=(kc == 7))
                if which == 0:
                    ACT(k, dstT[:, SEQ:NTOK], pt[:, 0:NS * ST], AF.Copy, pt_t, dst_t, scale=0.125)
                else:
                    CP(k, "dve", dstT[:, SEQ:NTOK], pt[:, 0:NS * ST], pt_t, dst_t)
            for n4 in range(8):
                pt, pt_t = psum(k)
                ptb = pt[:].bitcast(BF16)
                for j in range(4):
                    n = n4 * 4 + j
                    TR(k, ptb[:, j * 128:(j + 1) * 128], vT[:, n * 128:(n + 1) * 128], k.identb[:], [vT_t, k.identb_t], pt_t)
                pv = ptb[:, 0:512].rearrange("p (c f) -> p c f", f=128)
                CP(k, "act", Vaug[:, n4 * 4:(n4 + 1) * 4, 0, 0:64], pv[:, :, 0:64], pt_t, Vaug_t)
                CP(k, "dve", Vaug[:, n4 * 4:(n4 + 1) * 4, 1, 64:128], pv[:, :, 64:128], pt_t, Vaug_t)
            pt, pt_t = psum(k)
            for s in range(NS):
                for kc in range(8):
                    MM(k, pt[0:ST, s * 128:(s + 1) * 128], hT[:, kc, SEQ + s * ST:SEQ + (s + 1) * ST], wq[:, kc, 256:384], [wq_t, hT_t], pt_t, start=(kc == 0), stop=(kc == 7))
            pv = pt[0:ST, :].rearrange("p (c f) -> p c f", f=128)
            CP(k, "act", Vsn[:, :, 0, 0:64], pv[:, :, 0:64], pt_t, Vsn_t)
            CP(k, "dve", Vsn[:, :, 1, 64:128], pv[:, :, 64:128], pt_t, Vsn_t)
            if pidx + 1 < len(pairs):
                load_wq(pidx + 1)
            def unit_info(kb, hd):
                first = (kb % nb == 0)
                lastb = (kb % nb == nb - 1)
                r, u0 = (kb * 128) // L, (kb * 128) % L
                return first, (128 if lastb else 256), tslice(r + d * u0, 128, d)

            def emit_qk(grp):
                nonlocal tbi
                banks = {}
                for (kb, hd) in grp:
                    first, N, cols = unit_info(kb, hd)
                    hs = slice(64 * hd, 64 * hd + 64)
                    banks[(kb, hd)] = psum(k)
                    pt, pt_t = banks[(kb, hd)]
                    MM(k, pt[:, 0:N], kT[hs, kb * 128:(kb + 1) * 128], qT[hs, kb * 128:kb * 128 + N], [kT_t, qT_t], pt_t)
                tb_of = {}
                for (kb, hd) in grp:
                    first, N, cols = unit_info(kb, hd)
                    pt, pt_t = banks[(kb, hd)]
                    tb_of[(kb, hd)] = tbs[tbi % len(tbs)]
                    tbi += 1
                    tb, tb_t = tb_of[(kb, hd)]
                    TT(k, "dve", tb[:, 0:N], pt[:, 0:N], Bt[:, hd, 0:N], ALU.add, [pt_t, Bt_t], tb_t)
                for (kb, hd) in grp:
                    first, N, cols = unit_info(kb, hd)
                    tb, tb_t = tb_of[(kb, hd)]
                    P, P_t = PT[hd][kb % NPT]
                    ACT(k, P[:, 0:N], tb[:, 0:N], AF.Exp, tb_t, P_t)

            def emit_pv(grp):
                banks = {}
                for (kb, hd) in grp:
                    first, N, cols = unit_info(kb, hd)
                    P, P_t = PT[hd][kb % NPT]
                    banks[(kb, hd)] = psum(k)
                    pu, pu_t = banks[(kb, hd)]
                    if not first:
                        Pp, Pp_t = PT[hd][(kb - 1) % NPT]
                        MM(k, pu[:, 0:128], Vaug[:, kb - 1, hd, :], Pp[:, 128:256], [Vaug_t, Pp_t], pu_t, start=True, stop=False)
                    MM(k, pu[:, 0:128], Vaug[:, kb, hd, :], P[:, 0:128], [Vaug_t, P_t], pu_t, start=first, stop=True)
                for j, (kb, hd) in enumerate(grp):
                    first, N, cols = unit_info(kb, hd)
                    U, U_t = Us[hd]
                    pu, pu_t = banks[(kb, hd)]
                    if g == 0:
                        CP(k, "act" if j % 2 == 0 else "dve", U[:, cols], pu[:, 0:128], pu_t, U_t)
                    else:
                        TT(k, "dve", U[:, cols], U[:, cols], pu[:, 0:128], ALU.add, [pu_t, U_t], U_t)
            grps = [[(kb, hd) for kb in (2 * n, 2 * n + 1) for hd in range(2)] for n in range(16)]
            emit_qk(grps[0])
            for n in range(16):
                if n + 1 < 16:
                    emit_qk(grps[n + 1])
                emit_pv(grps[n])
            subs = [(s, None) for s in range(NS)] if d == 1 else [(s, i) for s in range(NS) for i in range(ST)]
            if k.opts.get("skip_sattn"):
                subs = []
            NBATCH = 4

            def sub_load(j, s, i):
                ck, ck_t = cks[j % len(cks)]
                rows = I["kv%d" % W][s, (0 if i is None else i):W:d, :].rearrange("r (t c) -> r t c", t=2)[:, :, 2 * sp * 64:2 * sp * 64 + 128]
                dma_in(k, ck[:], rows, ck_t, cksem[j % len(cks)])
            for j0 in range(0, min(NBATCH, len(subs))):
                sub_load(cki + j0, *subs[j0])
            for b0 in range(0, len(subs), NBATCH):
                batch = subs[b0:b0 + NBATCH]
                idx = [cki + b0 + j for j in range(len(batch))]
                for j, (s, i) in enumerate(subs[b0 + NBATCH:b0 + 2 * NBATCH]):
                    sub_load(cki + b0 + NBATCH + j, s, i)
                tr = {}
                for j, (s, i) in zip(idx, batch):
                    ck, ck_t = cks[j % len(cks)]
                    tr[j] = psum(k)
                    TR(k, tr[j][0][:, 0:128], ck[:, 0, :], k.ident, [ck_t, k.cm_t], tr[j][1])
                for j, (s, i) in zip(idx, batch):
                    ck, ck_t = cks[j % len(cks)]
                    vc, vc_t = vcs[j % len(vcs)]
                    kcT, kcT_t = kcTs[j % len(kcTs)]
                    CP(k, "act", kcT[:], tr[j][0][:, 0:128], tr[j][1], kcT_t)
                    CP(k, "dve", vc[:, 0, 0:64], ck[:, 1, 0:64], ck_t, vc_t)
                    CP(k, "dve", vc[:, 1, 64:128], ck[:, 1, 64:128], ck_t, vc_t)
                qk = {}
                for j, (s, i) in zip(idx, batch):
                    kcT, kcT_t = kcTs[j % len(kcTs)]
                    q0 = SEQ + s * ST + (0 if i is None else i)
                    nq = ST if i is None else 1
                    qc = slice(q0, q0 + nq)
                    kc_new = slice(SEQ + s * ST, SEQ + (s + 1) * ST)
                    qk[j] = psum(k)
                    pt, pt_t = qk[j]
                    for hd in range(2):
                        hs = slice(64 * hd, 64 * hd + 64)
                        MM(k, pt[:, 16 * hd:16 * hd + nq], kcT[hs, :], qT[hs, qc], [kcT_t, qT_t], pt_t)
                        MM(k, pt[0:ST, 16 * hd + 8:16 * hd + 8 + nq], kT[hs, kc_new], qT[hs, qc], [kT_t, qT_t], pt_t)
                for j, (s, i) in zip(idx, batch):
                    nq = ST if i is None else 1
                    pt, pt_t = qk[j]
                    tb, tb_t = tbs[j % len(tbs)]
                    ptv = pt[:, 0:32].rearrange("p (h c) -> p h c", h=2)
                    tbv = tb[:, 0:32].rearrange("p (h c) -> p h c", h=2)
                    TT(k, "dve", tbv[:, :, 0:nq], ptv[:, :, 0:nq], Bt[:, :, 128:128 + nq], ALU.add, [pt_t, Bt_t], tb_t)
                    Bc = Bt[0:ST, :, 0:ST] if i is None else BD[:, :, i:i + 1]
                    TT(k, "dve", tbv[0:ST, :, 8:8 + nq], ptv[0:ST, :, 8:8 + nq], Bc, ALU.add, [pt_t, Bt_t, BD_t], tb_t)
                for j, (s, i) in zip(idx, batch):
                    nq = ST if i is None else 1
                    tb, tb_t = tbs[j % len(tbs)]
                    P, P_t = Ps[j % len(Ps)]
                    tbv = tb[:, 0:32].rearrange("p (h c) -> p h c", h=2)
                    Pv = P[:, 0:32].rearrange("p (h c) -> p h c", h=2)
                    ACT(k, Pv[:, :, 0:nq], tbv[:, :, 0:nq], AF.Exp, tb_t, P_t)
                    ACT(k, Pv[0:ST, :, 8:8 + nq], tbv[0:ST, :, 8:8 + nq], AF.Exp, tb_t, P_t)
                pv = {}
                for j, (s, i) in zip(idx, batch):
                    nq = ST if i is None else 1
                    vc, vc_t = vcs[j % len(vcs)]
                    P, P_t = Ps[j % len(Ps)]
                    pv[j] = psum(k)
                    pu, pu_t = pv[j]
                    for hd in range(2):
                        MM(k, pu[:, 16 * hd:16 * hd + nq], vc[:, hd, :], P[:, 16 * hd:16 * hd + nq], [vc_t, P_t], pu_t, start=True, stop=False)
                        MM(k, pu[:, 16 * hd:16 * hd + nq], Vsn[:, s, hd, :], P[0:ST, 16 * hd + 8:16 * hd + 8 + nq], [Vsn_t, P_t], pu_t, start=False, stop=True)
                for j, (s, i) in zip(idx, batch):
                    q0 = SEQ + s * ST + (0 if i is None else i)
                    nq = ST if i is None else 1
                    qc = slice(q0, q0 + nq)
                    pu, pu_t = pv[j]
                    for hd in range(2):
                        U, U_t = Us[hd]
                        if g == 0:
                            CP(k, "act", U[:, qc], pu[:, 16 * hd:16 * hd + nq], pu_t, U_t)
                        else:
                            TT(k, "dve", U[:, qc], U[:, qc], pu[:, 16 * hd:16 * hd + nq], ALU.add, [pu_t, U_t], U_t)
            cki += len(subs)
        stg.load(wza, wza_t, 0, I["w_in"], OFF_ZA + sp * 128, 128)
        for c0 in range(0, NTOK, 512):
            n = min(512, NTOK - c0)
            pt, pt_t = psum(k)
            for kc in range(8):
                MM(k, pt[:, 0:n], wza[:, kc, :], hT[:, kc, c0:c0 + n], [wza_t, hT_t], pt_t, start=(kc == 0), stop=(kc == 7))
            ACT(k, sz[:, 0:n], pt[:, 0:n], AF.Silu, pt_t, sz_t)
            ACT(k, rec[0:64, 0:n], UA[64:128, c0:c0 + n], AF.Ln, UA_t, rec_t)
            ACT(k, rec[64:128, 0:n], UB[0:64, c0:c0 + n], AF.Ln, UB_t, rec_t)
            ACT(k, rec[:, 0:n], rec[:, 0:n], AF.Exp, rec_t, rec_t, scale=-1.0)
            TT(k, "pool", t1[0:64, 0:n], UA[0:64, c0:c0 + n], rec[0:64, 0:n], ALU.mult, [UA_t, rec_t], t1_t)
            TT(k, "pool", t1[64:128, 0:n], UB[64:128, c0:c0 + n], rec[64:128, 0:n], ALU.mult, [UB_t, rec_t], t1_t)
            yat, yat_t = yats[yi % 2]
            TT(k, "dve", yat[:, 0:n], t1[:, 0:n], sz[:, 0:n], ALU.mult, [t1_t, sz_t], yat_t)
            dma_out(k, k.yaT_d[sp * 128:(sp + 1) * 128, c0:c0 + n], yat[:, 0:n], [yat_t], sem=yasem[yi % 2], writes=[k.yad_t])
            yi += 1
    if "yaT" in k.dbg:
        S.barrier(("sp",))
        dma_out(k, O["yaT"], k.yaT_d, [k.yad_t])
def phaseF(k, es):
    nc, S, I, O = k.nc, k.S, k.I, k.O
    TF = 256
    wg, wg_t = sbt(k, es, "wg", [128, 8, 2048], BF16)
    wzb, wzb_t = sbt(k, es, "wzb", [128, 8, 1024], BF16)
    wa, wa_t = sbt(k, es, "wa", [128, 4, 1024], BF16)
    wb, wb_t = sbt(k, es, "wb", [128, 8, 1024], BF16)
    wo, wo_t = sbt(k, es, "wo", [128, 8, 1024], BF16)
    lng, lng_t = sbt(k, es, "lng", [128, 2, D], F32)
    gate_rep, gate_t = sbt(k, es, "gate_rep", [128, D], F32)
    gate_s, gates_t = sbt(k, es, "gate_s", [NS * ST, D], F32)
    s0 = S.dsem()
    dma_in(k, lng[:, 0, :], I["ln_g"].partition_broadcast(128), lng_t, s0)
    dma_in(k, lng[:, 1, :], I["ln_b"].partition_broadcast(128), lng_t, s0)
    with contextlib.ExitStack() as tes:
        stg = Stg(k, tes, 8, 256, "fst")
        stg.load(wg, wg_t, 0, I["w_in"], OFF_GA, 2048)
        stg.load(wzb, wzb_t, 0, I["w_in"], OFF_ZB, 1024)
        stg.load(wb, wb_t, 0, I["w_b"], 0, 1024)
        stg.load(wo, wo_t, 0, I["w_out"], 0, 1024)
        stg4 = Stg(k, tes, 4, 256, "fst4")
        stg4.load(wa, wa_t, 0, I["w_a"], 0, 1024)
        gtmp, gtmp_t = sbt(k, tes, "gtmp", [128, 128], F32)
        for fc in range(8):
            TS(k, "dve", gtmp[:], k.onesf[:], k.cond[:, 16 + fc, 0:1], ALU.mult, [k.onesf_t, k.cond_t], gtmp_t)
            pt, pt_t = psum(k)
            MM(k, pt[:, 0:128], gtmp[:], k.ident, [gtmp_t, k.cm_t], pt_t)
            CP(k, "act", gate_rep[:, fc * 128:(fc + 1) * 128], pt[:, 0:128], pt_t, gate_t)
            for s in range(NS):
                TS(k, "dve", gtmp[:, s * ST:(s + 1) * ST], k.onesf[:, 0:ST], k.cond[:, 16 + fc, 1 + s:2 + s], ALU.mult, [k.onesf_t, k.cond_t], gtmp_t)
            pt, pt_t = psum(k)
            MM(k, pt[0:NS * ST, 0:128], gtmp[:, 0:NS * ST], k.ident, [gtmp_t, k.cm_t], pt_t)
            CP(k, "act", gate_s[:, fc * 128:(fc + 1) * 128], pt[0:NS * ST, 0:128], pt_t, gates_t)
        S.barrier()
    hts = [sbt(k, es, "fhT%d" % i, [128, 8, TF], BF16) for i in range(2)]
    yas = [sbt(k, es, "fya%d" % i, [128, 4, TF], BF16) for i in range(2)]
    ybs = [sbt(k, es, "fyb%d" % i, [128, 8, TF], BF16) for i in range(2)]
    lsem = [[S.dsem() for _ in range(3)] for _ in range(2)]
    xts = [sbt(k, es, "fx%d" % i, [128, D], F32) for i in range(2)]
    xsem = [S.dsem() for _ in range(2)]
    ybg, ybg_t = sbt(k, es, "ybg", [128, 8, TF], BF16)
    mTs = [sbt(k, es, "mT%d" % i, [128, 8, TF], BF16) for i in range(2)]
    nhalf, nhalf_t = sbt(k, es, "nhalf", [128, 1], F32)
    MS(k, "dve", nhalf[:], -0.5, nhalf_t)
    sg = [sbt(k, es, "sg%d" % i, [128, TF], F32) for i in range(4)]
    tt_, tt_t = sbt(k, es, "tt", [128, D], F32)
    ys = [sbt(k, es, "fy%d" % i, [128, D], F32) for i in range(2)]
    ysem = [S.dsem() for _ in range(2)]
    stt, stt_t = sbt(k, es, "fstat", [128, 24], F32)
    hTd = k.hT_d.rearrange("(kc p) t -> p kc t", p=128)
    yad = k.yaT_d.rearrange("(kc p) t -> p kc t", p=128)
    ybd = k.ybT_d.rearrange("(kc p) t -> p kc t", p=128)
    xv = I["x"].rearrange("(t p) d -> t p d", p=128)
    ntile = k.opts.get("nftiles", SEQ // TF)
    tiles = [(i * TF, TF) for i in range(ntile)] + [(SEQ, NS * ST)]

    def loads(ti):
        c0, n = tiles[ti]
        b = ti % 2
        dma_in(k, hts[b][0][:, :, 0:n], hTd[:, :, c0:c0 + n], hts[b][1], lsem[b][0], reads=[k.hTd_t])
        dma_in(k, yas[b][0][:, :, 0:n], yad[:, :, c0:c0 + n], yas[b][1], lsem[b][1], reads=[k.yad_t])
        dma_in(k, ybs[b][0][:, :, 0:n], ybd[:, :, c0:c0 + n], ybs[b][1], lsem[b][2], reads=[k.ybd_t])

    xi = 0
    yi = 0
    sgi = 0
    loads(0)
    def s12(ti):
        nonlocal sgi
        c0, n = tiles[ti]
        mT, mT_t = mTs[ti % 2]
        b = ti % 2
        if ti + 1 < len(tiles):
            loads(ti + 1)
        hTt, hTt_t = hts[b]
        ya, ya_t = yas[b]
        yb, yb_t = ybs[b]
        for hb in range(8):
            pt, pt_t = psum(k)
            for kc in range(8):
                MM(k, pt[:, 0:n], wzb[:, kc, hb * 128:(hb + 1) * 128], hTt[:, kc, 0:n], [wzb_t, hTt_t], pt_t, start=(kc == 0), stop=(kc == 7))
            s_, s_t = sg[sgi % 4]
            sgi += 1
            ACT(k, s_[:, 0:n], pt[:, 0:n], AF.Silu, pt_t, s_t)
            TT(k, "dve", ybg[:, hb, 0:n], yb[:, hb, 0:n], s_[:, 0:n], ALU.mult, [yb_t, s_t], ybg_t)
        yield
        for ncb in range(8):
            if ncb == 4:
                yield
            cs = slice(ncb * 128, (ncb + 1) * 128)
            pa, pb, pga, pgb = psum(k), psum(k), psum(k), psum(k)
            for kc in range(4):
                MM(k, pa[0][:, 0:n], wa[:, kc, cs], ya[:, kc, 0:n], [wa_t, ya_t], pa[1], start=(kc == 0), stop=(kc == 3))
            for kc in range(8):
                MM(k, pb[0][:, 0:n], wb[:, kc, cs], ybg[:, kc, 0:n], [wb_t, ybg_t], pb[1], start=(kc == 0), stop=(kc == 7))
            for kc in range(8):
                MM(k, pga[0][:, 0:n], wg[:, kc, cs], hTt[:, kc, 0:n], [wg_t, hTt_t], pga[1], start=(kc == 0), stop=(kc == 7))
            for kc in range(8):
                MM(k, pgb[0][:, 0:n], wg[:, kc, 1024 + ncb * 128:1024 + (ncb + 1) * 128], hTt[:, kc, 0:n], [wg_t, hTt_t], pgb[1], start=(kc == 0), stop=(kc == 7))
            sa, sa_t = sg[sgi % 4]
            sb_, sb_t = sg[(sgi + 1) % 4]
            sgi += 2
            ACT(k, sa[:, 0:n], pga[0][:, 0:n], AF.Sigmoid, pga[1], sa_t)
            ACT(k, sb_[:, 0:n], pgb[0][:, 0:n], AF.Sigmoid, pgb[1], sb_t)
            TT(k, "dve", sa[:, 0:n], pa[0][:, 0:n], sa[:, 0:n], ALU.mult, [pa[1], sa_t], sa_t)
            TT(k, "dve", sb_[:, 0:n], pb[0][:, 0:n], sb_[:, 0:n], ALU.mult, [pb[1], sb_t], sb_t)
            TT(k, "dve", mT[:, ncb, 0:n], sa[:, 0:n], sb_[:, 0:n], ALU.add, [sa_t, sb_t], mT_t)

    def s3(ti):
        nonlocal xi, yi
        c0, n = tiles[ti]
        mT, mT_t = mTs[ti % 2]
        for j0 in range(0, n, 128):
            M = min(128, n - j0)
            samp = c0 >= SEQ
            xt, xt_t = xts[xi % 2]
            if samp:
                dma_in(k, xt[0:M, :], I["xs"], xt_t, xsem[xi % 2])
            else:
                dma_in(k, xt[:], xv[(c0 + j0) // 128], xt_t, xsem[xi % 2])
            xi += 1
            p0, p1 = psum(k), psum(k)
            for half, pp in ((0, p0), (1, p1)):
                for kc in range(8):
                    MM(k, pp[0][0:M, :], mT[:, kc, j0:j0 + M], wo[:, kc, half * 512:(half + 1) * 512], [mT_t, wo_t], pp[1], start=(kc == 0), stop=(kc == 7))
            gr, gr_t = (gate_s, gates_t) if samp else (gate_rep, gate_t)
            TT(k, "dve", tt_[0:M, 0:512], p0[0][0:M, :], gr[0:M, 0:512], ALU.mult, [p0[1], gr_t], tt_t)
            TT(k, "dve", tt_[0:M, 512:1024], p1[0][0:M, :], gr[0:M, 512:1024], ALU.mult, [p1[1], gr_t], tt_t)
            STT(k, tt_[0:M, :], xt[0:M, :], ALPHA, tt_[0:M, :], ALU.mult, ALU.add, [xt_t, tt_t], tt_t)
            k.S.op("dve", lambda e, o=stt[0:M, 0:6], i_=tt_[0:M, 0:512]: e.bn_stats(out=o, in_=i_), [tt_t], [stt_t])
            k.S.op("dve", lambda e, o=stt[0:M, 6:12], i_=tt_[0:M, 512:1024]: e.bn_stats(out=o, in_=i_), [tt_t], [stt_t])
            k.S.op("dve", lambda e, o=stt[0:M, 12:14], i_=stt[0:M, 0:12]: e.bn_aggr(out=o, in_=i_), [stt_t], [stt_t])
            TS(k, "dve", stt[0:M, 14:15], stt[0:M, 13:14], 1e-5, ALU.add, stt_t, stt_t)
            TT(k, "pool", stt[0:M, 15:16], stt[0:M, 14:15], nhalf[0:M, :], ALU.pow, [stt_t, nhalf_t], stt_t)
            y, y_t = ys[yi % 2]
            STT(k, stt[0:M, 16:17], stt[0:M, 12:13], -1.0, stt[0:M, 15:16], ALU.mult, ALU.mult, [stt_t], stt_t)
            ACT(k, y[0:M, :], tt_[0:M, :], AF.Identity, [tt_t, stt_t], y_t, scale=stt[0:M, 15:16], bias=stt[0:M, 16:17])
            TT(k, "pool", y[0:M, :], y[0:M, :], lng[0:M, 0, :], ALU.mult, [y_t, lng_t], y_t)
            TT(k, "pool", y[0:M, :], y[0:M, :], lng[0:M, 1, :], ALU.add, [y_t, lng_t], y_t)
            if samp:
                dma_out(k, O["ys"], y[0:M, :], [y_t], sem=ysem[yi % 2])
            else:
                dma_out(k, O["y"][c0 + j0:c0 + j0 + 128, :], y[:], [y_t], sem=ysem[yi % 2])
            yi += 1
            yield

    for _ in s12(0):
        pass
    for ti in range(len(tiles)):
        g12 = s12(ti + 1) if ti + 1 < len(tiles) else iter(())
        g3 = s3(ti)
        next(g12, None)
        next(g3, None)
        next(g12, None)
        for _ in g3:
            pass
        for _ in g12:
            pass


def t5_causal_buckets(dist):
    n_buckets, max_dist = 32, 2048
    max_exact = n_buckets // 2
    dist = np.asarray(dist, dtype=np.int64)
    ratio = np.maximum(dist, max_exact) / max_exact
    large = max_exact + (np.log(ratio) / math.log(max_dist / max_exact) * (n_buckets - max_exact)).astype(np.int64)
    return np.where(dist < max_exact, dist, np.minimum(large, n_buckets - 1)).astype(np.int32)


def host_consts():
    p = np.arange(128)[:, None]
    f = np.arange(128)[None, :]
    blk = (p // 32) == (f // 32)
    cm = np.stack([(p == f), (p <= f), (f < p) & blk, (f >= p), (f < p) & ~blk]).astype(np.float32)
    return cm


def bias_layout(rel_bias):
    p = np.arange(128)[:, None]
    f = np.arange(256)[None, :]
    j = np.where(f < 128, f - p, f - p)
    valid = np.where(f < 128, j >= 0, j <= 128)
    idx = np.where(valid, j, 129).astype(np.int64)
    out = np.empty((24, 128, 256), np.float32)
    for gi, (window, dil) in enumerate(GROUPS):
        buckets = t5_causal_buckets(dil * np.arange(window // dil + 1))
        for hh in range(8):
            h = gi * 8 + hh
            ext = np.concatenate([rel_bias[buckets, h], np.array([NEG, NEG], np.float32)]).astype(np.float32)
            out[h] = ext[np.minimum(idx, 129)]
    return out


def bias_diag_layout(rel_bias):
    out = np.empty((24, ST, ST), np.float32)
    eye = np.eye(ST, dtype=bool)
    for gi, (window, dil) in enumerate(GROUPS):
        b0 = t5_causal_buckets(np.zeros(1))[0]
        for hh in range(8):
            h = gi * 8 + hh
            ext = np.array([rel_bias[b0, h], NEG], np.float32)
            out[h] = ext[np.where(eye, 0, 1)]
    return out


def make_in_maps(inp):
    f32 = lambda a: np.ascontiguousarray(np.asarray(a, dtype=np.float32))
    cm = host_consts()
    biasT = bias_layout(np.asarray(inp["rel_bias"], np.float32))
    shared = {
        "w_cond": f32(inp["w_cond"][0]),
        "bcondT": f32(np.asarray(inp["b_cond"][0]).reshape(24, 128).T),
        "w_in": f32(inp["w_in"][0]),
        "biasT": biasT,
        "biasD": bias_diag_layout(np.asarray(inp["rel_bias"], np.float32)),
        "convwT": f32(np.asarray(inp["conv_w"][0]).reshape(4, 24, 128).transpose(2, 1, 0)),
        "a_log": f32(inp["a_log"]).reshape(1, 8),
        "dt_bias": f32(inp["dt_bias"]).reshape(1, 8),
        "b_norm_w": f32(inp["b_norm_w"]).reshape(1, 128),
        "w_a": f32(inp["w_branch_a"][0]),
        "w_b": f32(inp["w_branch_b"][0]),
        "w_out": f32(inp["w_out"][0]),
        "ln_g": f32(inp["ln_g"]).reshape(1, D),
        "ln_b": f32(inp["ln_b"]).reshape(1, D),
        "cmats": cm,
    }
    maps = []
    for b in range(8):
        sl = slice(NS * b, NS * b + NS)
        c5 = np.concatenate([np.asarray(inp["c_prompt"][b:b + 1]), np.asarray(inp["c_sample"][sl])], axis=0)
        m = dict(shared)
        m["x"] = f32(inp["x_prompt"][b])
        m["xs"] = f32(np.asarray(inp["x_sample"][sl]).reshape(NS * ST, D))
        m["cT"] = f32(c5.T.reshape(8, 128, 1 + NS).transpose(1, 0, 2))
        m["kv128"] = f32(np.asarray(inp["cache_kv_w128"][0, sl]).reshape(NS, 128, 1024))
        m["kv512"] = f32(np.asarray(inp["cache_kv_w512"][0, sl]).reshape(NS, 512, 1024))
        m["kv2048"] = f32(np.asarray(inp["cache_kv_w2048"][0, sl]).reshape(NS, 2048, 1024))
        m["sconv"] = f32(np.asarray(inp["state_conv"][0, sl]).reshape(NS * 3, 3072))
        m["sdelta"] = f32(np.asarray(inp["state_delta"][0, sl]).reshape(NS * 8, 128, 128))
        maps.append(m)
    return maps


_NC_CACHE = {}


def kernel(**inputs):
    if "nc" not in _NC_CACHE:
        _NC_CACHE["nc"] = build()
    nc = _NC_CACHE["nc"]
    maps = make_in_maps(inputs)
    res = run_bass_kernel_spmd(nc, maps, core_ids=list(range(8))).results
    g = lambda name: [np.asarray(r[name], dtype=np.float32) for r in res]
    y = np.stack(g("y"))
    ys = np.concatenate([a.reshape(NS, ST, D) for a in g("ys")], axis=0)
    outs = [y, ys]
    for w in (128, 512, 2048):
        outs.append(np.stack([a.reshape(w, 2, 8, 64) for a in g("kvp%d" % w)])[None])
    outs.append(np.stack(g("convp"))[None])
    outs.append(np.stack(g("deltap"))[None])
    for w in (128, 512, 2048):
        outs.append(np.concatenate([a.reshape(NS, w, 2, 8, 64) for a in g("kvs%d" % w)], axis=0)[None])
    outs.append(np.concatenate(g("convs"), axis=0)[None])
    outs.append(np.concatenate([a.reshape(NS, 8, 128, 128) for a in g("deltas")], axis=0)[None])
    return tuple(outs)
```

```python
import contextlib
import math
import numpy as np
import ml_dtypes
import concourse.bass as bass
import concourse.mybir as mybir
from concourse.bass_utils import run_bass_kernel_spmd

F32 = mybir.dt.float32
BF16 = mybir.dt.bfloat16
ALU = mybir.AluOpType
AF = mybir.ActivationFunctionType
AX = mybir.AxisListType
ENGS = ("pe", "act", "dve", "pool", "sp")

D = 1024
SEQ = 4096
NS = 4
ST = 4
NTOK = SEQ + NS * ST
PROJ = 11280
OFF_QA, OFF_KA, OFF_VA, OFF_ZA, OFF_QKVB, OFF_ZB, OFF_AB, OFF_GA, OFF_GB = 0, 1536, 3072, 4608, 5120, 8192, 9216, 9232, 10256
GROUPS = ((128, 1), (512, 4), (2048, 16))
ALPHA = 2 ** 0.25
NEG = -30000.0


class Sem:
    def __init__(self, h, step):
        self.h = h
        self.n = 0
        self.step = step


class Tok:
    __slots__ = ("w", "r", "name", "excl")

    def __init__(self, name="", excl=False):
        self.w = None
        self.r = {}
        self.name = name
        self.excl = excl


class Sched:
    def __init__(self, nc, es):
        self.nc = nc
        self.es = es
        self.prog = {e: [] for e in ENGS}
        self.esem = {e: Sem(es.enter_context(nc.semaphore("sem_" + e)), 1) for e in ENGS if e != "sp"}
        self.seen = {e: {} for e in ENGS}
        self.dsems = []
        self.ninstr = 0

    def dsem(self, name=None):
        s = Sem(self.es.enter_context(self.nc.semaphore(name or ("dsem%d" % len(self.dsems)))), 16)
        self.dsems.append(s)
        return s

    def _need(self, eng, deps):
        for s, v in deps.items():
            if eng == "pe" and s is self.esem["pe"]:
                continue
            if self.seen[eng].get(s, 0) < v:
                self.prog[eng].append(("w", s, v))
                self.seen[eng][s] = v

    def op(self, eng, fn, reads=(), writes=(), dsem=None):
        ex = [t for t in reads if t.excl]
        if ex:
            reads = [t for t in reads if not t.excl]
            writes = list(writes) + [t for t in ex if t not in writes]
        deps = {}
        for t in reads:
            if t.w is not None and deps.get(t.w[0], 0) < t.w[1]:
                deps[t.w[0]] = t.w[1]
        for t in writes:
            if t.w is not None and deps.get(t.w[0], 0) < t.w[1]:
                deps[t.w[0]] = t.w[1]
            for s, v in t.r.items():
                if deps.get(s, 0) < v:
                    deps[s] = v
        self._need(eng, deps)
        sem = dsem if dsem is not None else self.esem[eng]
        sem.n += sem.step
        rec = (sem, sem.n)
        self.prog[eng].append(("i", fn, sem))
        self.ninstr += 1
        for t in reads:
            if t.r.get(sem, 0) < sem.n:
                t.r[sem] = sem.n
        for t in writes:
            t.w = rec
            t.r = {}
        return rec

    def barrier(self, engs=ENGS):
        allsems = list(self.esem.values()) + self.dsems
        for e in engs:
            self._need(e, {s: s.n for s in allsems if s.n > 0})

    def emit(self):
        nc = self.nc
        self.barrier(("sp",))
        with nc.Block() as block:
            def run(engname, e):
                for item in self.prog[engname]:
                    if item[0] == "w":
                        e.wait_ge(item[1].h, item[2])
                    else:
                        item[1](e).then_inc(item[2].h, item[2].step)

            @block.tensor
            def _(e):
                run("pe", e)

            @block.scalar
            def _(e):
                run("act", e)

            @block.vector
            def _(e):
                run("dve", e)

            @block.gpsimd
            def _(e):
                run("pool", e)

            @block.sync
            def _(e):
                run("sp", e)


class K:
    pass


def build(dbg=None, phases="0CBAF", opts=None):
    nc = bass.Bass("TRN2", target_bir_lowering=False)
    k = K()
    k.nc = nc
    k.dbg = dbg or {}
    k.opts = opts or {}
    di = lambda name, shape, dt=F32: nc.dram_tensor(name, list(shape), dt, kind="ExternalInput").ap()
    do = lambda name, shape, dt=F32: nc.dram_tensor(name, list(shape), dt, kind="ExternalOutput").ap()
    dsc = lambda name, shape, dt=F32: nc.dram_tensor(name, list(shape), dt).ap()
    I = k.I = {}
    O = k.O = {}
    I["x"] = di("x", [SEQ, D])
    I["xs"] = di("xs", [NS * ST, D])
    I["cT"] = di("cT", [128, 8, 1 + NS])
    I["kv128"] = di("kv128", [NS, 128, 1024])
    I["kv512"] = di("kv512", [NS, 512, 1024])
    I["kv2048"] = di("kv2048", [NS, 2048, 1024])
    I["sconv"] = di("sconv", [NS * 3, 3072])
    I["sdelta"] = di("sdelta", [NS * 8, 128, 128])
    I["w_cond"] = di("w_cond", [D, 3 * D])
    I["bcondT"] = di("bcondT", [128, 24])
    I["w_in"] = di("w_in", [D, PROJ])
    I["biasT"] = di("biasT", [24, 128, 256])
    I["biasD"] = di("biasD", [24, ST, ST])
    I["convwT"] = di("convwT", [128, 24, 4])
    I["a_log"] = di("a_log", [1, 8])
    I["dt_bias"] = di("dt_bias", [1, 8])
    I["b_norm_w"] = di("b_norm_w", [1, 128])
    I["w_a"] = di("w_a", [512, D])
    I["w_b"] = di("w_b", [D, D])
    I["w_out"] = di("w_out", [D, D])
    I["ln_g"] = di("ln_g", [1, D])
    I["ln_b"] = di("ln_b", [1, D])
    I["cmats"] = di("cmats", [5, 128, 128])
    O["y"] = do("y", [SEQ, D])
    O["ys"] = do("ys", [NS * ST, D])
    O["kvp128"] = do("kvp128", [128, 1024])
    O["kvp512"] = do("kvp512", [512, 1024])
    O["kvp2048"] = do("kvp2048", [2048, 1024])
    O["convp"] = do("convp", [3, 3072])
    O["deltap"] = do("deltap", [8, 128, 128])
    O["kvs128"] = do("kvs128", [NS, 128, 1024])
    O["kvs512"] = do("kvs512", [NS, 512, 1024])
    O["kvs2048"] = do("kvs2048", [NS, 2048, 1024])
    O["convs"] = do("convs", [NS, 3, 3072])
    O["deltas"] = do("deltas", [NS * 8, 128, 128])
    k.hT_d = dsc("hT_d", [D, NTOK], BF16)
    k.ybT_d = dsc("ybT_d", [D, NTOK], BF16)
    k.yaT_d = dsc("yaT_d", [512, NTOK], BF16)
    for name, (shape, dt) in k.dbg.items():
        O[name] = do(name, shape, dt)

    with contextlib.ExitStack() as es:
        S = k.S = Sched(nc, es)
        k.es = es
        k.ps = [es.enter_context(nc.psum_tensor("ps%d" % i, [128, 512], F32)) for i in range(8)]
        k.pst = [Tok("ps%d" % i, excl=True) for i in range(8)]
        k.psi = 0
        k.out_sem = S.dsem("out_sem")
        phase0(k)
        if "C" in phases:
            cache_copies(k)
        if "B" in phases:
            with contextlib.ExitStack() as pes:
                phaseB(k, pes)
                S.barrier()
        if "A" in phases:
            with contextlib.ExitStack() as pes:
                phaseA(k, pes)
                S.barrier()
        if "F" in phases:
            with contextlib.ExitStack() as pes:
                phaseF(k, pes)
                S.barrier()
        S.emit()
    return nc


def psum(k, pool=None):
    pools = getattr(k, "pspools", None)
    if pool is None or pools is None:
        i = k.psi
        k.psi = (i + 1) % 8
        return k.ps[i], k.pst[i]
    lst, idx = pools[pool]
    i = lst[idx % len(lst)]
    pools[pool][1] = idx + 1
    return k.ps[i], k.pst[i]


def sbt(k, es, name, shape, dt):
    t = es.enter_context(k.nc.sbuf_tensor("s_" + name, list(shape), dt))
    return t, Tok(name)


def _l(x):
    return list(x) if isinstance(x, (list, tuple)) else [x]


def dma_in(k, out_ap, in_ap, toks, sem, eng="sp", reads=()):
    k.S.op(eng, lambda e: e.dma_start(out=out_ap, in_=in_ap), reads=_l(reads), writes=_l(toks), dsem=sem)


def dma_out(k, out_ap, in_ap, reads, sem=None, eng="sp", writes=()):
    k.S.op(eng, lambda e: e.dma_start(out=out_ap, in_=in_ap), reads=_l(reads), writes=_l(writes), dsem=sem or k.out_sem)


def MM(k, out, lhsT, rhs, reads, writes, start=True, stop=True):
    k.S.op("pe", lambda e: e.matmul(out, lhsT=lhsT, rhs=rhs, start=start, stop=stop), _l(reads), _l(writes))


def TR(k, out, in_, ident, reads, writes):
    k.S.op("pe", lambda e: e.transpose(out, in_, ident), _l(reads), _l(writes))


def ACT(k, out, in_, func, reads, writes, scale=None, bias=None):
    kw = {}
    if scale is not None:
        kw["scale"] = scale
    if bias is not None:
        kw["bias"] = bias
    k.S.op("act", lambda e: e.activation(out=out, in_=in_, func=func, **kw), _l(reads), _l(writes))


def TT(k, eng, out, in0, in1, op, reads, writes):
    k.S.op(eng, lambda e: e.tensor_tensor(out=out, in0=in0, in1=in1, op=op), _l(reads), _l(writes))


def TS(k, eng, out, in0, s1, op0, reads, writes, s2=None, op1=None):
    if op1 is None:
        k.S.op(eng, lambda e: e.tensor_scalar(out=out, in0=in0, scalar1=s1, scalar2=None, op0=op0), _l(reads), _l(writes))
    else:
        k.S.op(eng, lambda e: e.tensor_scalar(out=out, in0=in0, scalar1=s1, scalar2=s2, op0=op0, op1=op1), _l(reads), _l(writes))


def STT(k, out, in0, scalar, in1, op0, op1, reads, writes):
    k.S.op("dve", lambda e: e.scalar_tensor_tensor(out=out, in0=in0, scalar=scalar, in1=in1, op0=op0, op1=op1), _l(reads), _l(writes))


def CP(k, eng, out, in_, reads, writes):
    if eng == "act":
        ACT(k, out, in_, AF.Identity, reads, writes)
    else:
        k.S.op(eng, lambda e: e.tensor_copy(out=out, in_=in_), _l(reads), _l(writes))


def EV(k, out, in_, reads, writes, scale=None):
    k.rr = getattr(k, "rr", 0) + 1
    if k.rr % 2 == 0:
        ACT(k, out, in_, AF.Copy if not isinstance(scale, (int, float)) or True else AF.Copy, reads, writes, scale=scale)
    else:
        if scale is None:
            CP(k, "dve", out, in_, reads, writes)
        else:
            TS(k, "dve", out, in_, scale, ALU.mult, reads, writes)


def MS(k, eng, ap, val, writes):
    k.S.op(eng, lambda e: e.memset(ap, val), [], _l(writes))


def phase0(k):
    nc, S, es, I = k.nc, k.S, k.es, k.I
    k.cm, k.cm_t = sbt(k, es, "cmats", [128, 5, 128], F32)
    k.ident, k.triu, k.ldiag, k.uincl, k.loff = (k.cm[:, i, :] for i in range(5))
    s0 = S.dsem()
    dma_in(k, k.cm[:], I["cmats"].rearrange("m p f -> p m f"), k.cm_t, s0)
    k.identb, k.identb_t = sbt(k, es, "identb", [128, 128], BF16)
    CP(k, "act", k.identb[:], k.ident, k.cm_t, k.identb_t)
    k.onesf, k.onesf_t = sbt(k, es, "onesf", [128, 128], F32)
    MS(k, "dve", k.onesf[:], 1.0, k.onesf_t)
    k.onesb, k.onesb_t = sbt(k, es, "onesb", [128, 128], BF16)
    MS(k, "dve", k.onesb[:], 1.0, k.onesb_t)
    k.epsc, k.epsc_t = sbt(k, es, "epsc", [128, 3], F32)
    MS(k, "dve", k.epsc[:, 0:1], 1e-6, k.epsc_t)
    MS(k, "dve", k.epsc[:, 1:2], 1.0, k.epsc_t)
    MS(k, "dve", k.epsc[:, 2:3], 1e-5, k.epsc_t)
    k.small, k.small_t = sbt(k, es, "small", [128, 16 + 128], F32)
    s1 = S.dsem()
    s2, s3, s4 = S.dsem(), S.dsem(), S.dsem()
    dma_in(k, k.small[:, 0:8], I["a_log"].partition_broadcast(128), k.small_t, s1)
    dma_in(k, k.small[:, 8:16], I["dt_bias"].partition_broadcast(128), k.small_t, s1)
    dma_in(k, k.small[:, 16:144], I["b_norm_w"].partition_broadcast(128), k.small_t, s1)
    k.negA, k.negA_t = sbt(k, es, "negA", [128, 8], F32)
    ACT(k, k.negA[:], k.small[:, 0:8], AF.Exp, k.small_t, k.negA_t)
    TS(k, "dve", k.negA[:], k.negA[:], -1.0, ALU.mult, k.negA_t, k.negA_t)
    k.dtb = k.small[:, 8:16]
    k.bnw = k.small[:, 16:144]
    k.convw, k.convw_t = sbt(k, es, "convw", [128, 24, 4], F32)
    dma_in(k, k.convw[:], I["convwT"], k.convw_t, s2)
    k.cond, k.cond_t = sbt(k, es, "cond", [128, 24, 1 + NS], F32)
    with contextlib.ExitStack() as tes:
        cT, cT_t = sbt(k, tes, "cT", [128, 8, 1 + NS], F32)
        bc, bc_t = sbt(k, tes, "bcond", [128, 24], F32)
        dma_in(k, cT[:], I["cT"], cT_t, s3)
        dma_in(k, bc[:], I["bcondT"], bc_t, s4)
        ACT(k, cT[:], cT[:], AF.Silu, cT_t, cT_t)
        wst = [sbt(k, tes, "wcst%d" % i, [128, 8, 512], F32) for i in range(2)]
        wss = [S.dsem() for _ in range(2)]
        wv = I["w_cond"].rearrange("(kc p) n -> p kc n", p=128)
        for j in range(6):
            w, w_t = wst[j % 2]
            dma_in(k, w[:], wv[:, :, j * 512:(j + 1) * 512], w_t, wss[j % 2])
            pt, pt_t = psum(k)
            for fc in range(4):
                for kc in range(8):
                    MM(k, pt[:, fc * 8:fc * 8 + 1 + NS], w[:, kc, fc * 128:(fc + 1) * 128], cT[:, kc, :], [w_t, cT_t], pt_t, start=(kc == 0), stop=(kc == 7))
            for fc in range(4):
                f = j * 4 + fc
                TS(k, "dve", k.cond[:, f, :], pt[:, fc * 8:fc * 8 + 1 + NS], bc[:, f:f + 1], ALU.add, [pt_t, bc_t], k.cond_t)
        TS(k, "dve", k.cond[:, 8:16, :], k.cond[:, 8:16, :], 1.0, ALU.add, k.cond_t, k.cond_t)
        S.barrier()
    if "cond" in k.dbg:
        dma_out(k, k.O["cond"], k.cond[:], [k.cond_t])


def build_hT_tile(k, xt, xt_t, ntok, hTt, hTt_t, seqs, pool=None):
    pts = [psum(k, pool), psum(k, pool)]
    for kc in range(8):
        pt, pt_t = pts[kc // 4]
        TR(k, pt[:, (kc % 4) * 128:(kc % 4) * 128 + ntok], xt[0:ntok, kc * 128:(kc + 1) * 128], k.ident[0:ntok, 0:ntok], [xt_t, k.cm_t], pt_t)
    for kc in range(8):
        pt, pt_t = pts[kc // 4]
        for (c0, c1, si) in seqs:
            ACT(k, hTt[:, kc, c0:c1], pt[:, (kc % 4) * 128 + c0:(kc % 4) * 128 + c1], AF.Identity, [pt_t, k.cond_t], hTt_t,
                scale=k.cond[:, 8 + kc, si:si + 1], bias=k.cond[:, kc, si:si + 1])


def load_w_bf16(k, es_tmp, dst, dst_t, col0, ncols, chunk=256, name="wst", src=None, nkc=8):
    S = k.S
    wv = (src if src is not None else k.I["w_in"]).rearrange("(kc p) n -> p kc n", p=128)
    st = [sbt(k, es_tmp, "%s%d" % (name, i), [128, nkc, chunk], F32) for i in range(2)]
    ss = [S.dsem() for _ in range(2)]
    j = 0
    c = 0
    while c < ncols:
        n = min(chunk, ncols - c)
        w, w_t = st[j % 2]
        dma_in(k, w[:, :, 0:n], wv[:, :, col0 + c:col0 + c + n], w_t, ss[j % 2])
        CP(k, "pool" if j % 2 == 0 else "act", dst[:, :, c:c + n], w[:, :, 0:n], w_t, dst_t)
        c += n
        j += 1


def phaseB(k, es):
    nc, S, I, O = k.nc, k.S, k.I, k.O
    NH = 8
    wB, wB_t = sbt(k, es, "wB", [128, 8, 3072], BF16)
    wab, wab_t = sbt(k, es, "wab", [128, 8, 16], BF16)
    with contextlib.ExitStack() as tes:
        load_w_bf16(k, tes, wB, wB_t, OFF_QKVB, 3072)
        load_w_bf16(k, tes, wab, wab_t, OFF_AB, 16, chunk=16, name="wabst")
        S.barrier()
    xts = [sbt(k, es, "xt%d" % i, [128, D], F32) for i in range(2)]
    xsem = [S.dsem() for _ in range(2)]
    hTs = [sbt(k, es, "hTt%d" % i, [128, 8, 128], BF16) for i in range(2)]
    hsem = [S.dsem() for _ in range(2)]
    pc, pc_t = sbt(k, es, "pc", [128, 24, 131], F32)
    cv, cv_t = sbt(k, es, "cv", [128, 24, 128], F32)
    tmpa, tmpa_t = sbt(k, es, "tmpa", [128, 8, 128], F32)
    tmpc, tmpc_t = sbt(k, es, "tmpc", [128, 8, 128], F32)
    cv_ts = [Tok("cv%d" % i) for i in range(3)]
    sq2, sq2_t = sbt(k, es, "sq2", [128, 16, 128], BF16)
    rn, rn_t = sbt(k, es, "rn", [128, 16, 128], BF16)
    qkvTs = [sbt(k, es, "qkvT%d" % i, [128, 24, 128], BF16) for i in range(2)]
    kvtoks = [sbt(k, es, "kvtok%d" % i, [128, 16, 128], BF16) for i in range(2)]
    gbs = [sbt(k, es, "gb%d" % i, [128, 56], F32) for i in range(2)]
    Sst, Sst_t = sbt(k, es, "Sst", [128, 8, 128], F32)
    Sbf = [sbt(k, es, "Sbf%d" % i, [128, 8, 128], BF16)[0] for i in range(2)]
    Sh_t = [Tok("S%d" % h) for h in range(NH)]
    Sbf_t = [[Tok("Sbf%d_%d" % (p, h)) for h in range(NH)] for p in range(2)]
    stsem = S.dsem()
    ybT, ybT_t = sbt(k, es, "ybT", [128, 8, 128], BF16)
    ybsem = S.dsem()
    diagW, diagW_t = sbt(k, es, "diagW", [128, 8, 4, 128], F32)
    for c_ in range(8):
        for i_ in range(4):
            TS(k, "dve", diagW[:, c_, i_, :], k.ident, k.convw[:, 16 + c_, i_:i_ + 1], ALU.mult, [k.cm_t, k.convw_t], diagW_t)
    rowmask, rowmask_t = sbt(k, es, "rowmask", [128, 1], F32)
    MS(k, "dve", rowmask[:], 0.0, rowmask_t)
    MS(k, "dve", rowmask[0:ST, :], 1.0, rowmask_t)
    cvsem = S.dsem()
    hist, hist_t = sbt(k, es, "hist", [72, 128], F32)
    histsem = S.dsem()
    NSLOT = 8
    slots = []
    for s in range(NSLOT):
        d = {}
        for nm in ["gTri", "absG", "E", "ecr", "EU"]:
            d[nm] = sbt(k, es, "%s_%d" % (nm, s), [128, 128], F32)
        d["ELd"] = d["gTri"]
        d["ELo"] = d["absG"]
        d["o2"] = d["E"]
        for nm in ["Ad", "Ao", "X0", "X1", "XT0", "XT1", "DT0", "DT1", "qkm", "kt", "qeT"]:
            d[nm] = sbt(k, es, "%s_%d" % (nm, s), [128, 128], BF16)
        d["yb"] = d["Ad"]
        d["nwT"] = d["X0"]
        d["vnew"] = d["X1"]
        d["nNT"] = d["XT0"]
        for nm in ["R", "Xa", "Xb"]:
            d[nm] = sbt(k, es, "%s_%d" % (nm, s), [128, 256], BF16)
        d["st"] = sbt(k, es, "st_%d" % s, [128, 8], F32)
        slots.append(d)

    def T_(h, nm):
        return slots[h % NSLOT][nm][0]

    def Tt(h, nm):
        return slots[h % NSLOT][nm][1]

    MS(k, "pool", Sst[:], 0.0, [Sst_t] + Sh_t)
    MS(k, "pool", Sbf[0][:], 0.0, Sbf_t[0])
    MS(k, "pool", pc[:, :, 0:3], 0.0, pc_t)
    par = [0] * NH

    xv = I["x"].rearrange("(t p) d -> t p d", p=128)
    hTd = k.hT_d.rearrange("(kc p) t -> p kc t", p=128)
    ybd = k.ybT_d.rearrange("(h p) t -> p h t", p=128)
    k.hTd_t = Tok("hTd")
    k.ybd_t = Tok("ybd")

    def load_x(t):
        dma_in(k, xts[t % 2][0][:], xv[t], xts[t % 2][1], xsem[t % 2])

    ntiles = k.opts.get("ntiles", 32)
    nvirt = k.opts.get("nvirt", NS)
    vt_list = list(range(ntiles)) + [("s", s) for s in range(nvirt)]
    load_x(0)
    def front(ti, t):
        virt = isinstance(t, tuple)
        qkvT, qkvT_t = qkvTs[ti % 2]
        kvtok, kvtok_t = kvtoks[ti % 2]
        gb, gb_t = gbs[ti % 2]
        virt = isinstance(t, tuple)
        hTt, hTt_t = hTs[ti % 2]
        xt, xt_t = xts[ti % 2]
        if not virt:
            if t + 1 < ntiles:
                load_x(t + 1)
            build_hT_tile(k, xt, xt_t, 128, hTt, hTt_t, [(0, 128, 0)], pool="f")
            dma_out(k, hTd[:, :, t * 128:(t + 1) * 128], hTt[:], [hTt_t], sem=hsem[ti % 2], writes=[k.hTd_t])
        else:
            s = t[1]
            dma_in(k, xt[0:ST, :], I["xs"][s * ST:(s + 1) * ST, :], xt_t, xsem[ti % 2])
            MS(k, "pool", hTt[:], 0.0, hTt_t)
            build_hT_tile(k, xt, xt_t, ST, hTt, hTt_t, [(0, ST, 1 + s)], pool="f")
            dma_out(k, hTd[:, :, SEQ + s * ST:SEQ + (s + 1) * ST], hTt[:, :, 0:ST], [hTt_t], sem=hsem[ti % 2], writes=[k.hTd_t])
            dma_in(k, hist[:], I["sconv"][s * 3:(s + 1) * 3, :].rearrange("r (c f) -> (r c) f", f=128), hist_t, histsem)
            pt, pt_t = psum(k, "f")
            TR(k, pt[:, 0:72], hist[:], k.ident[0:72, 0:72], [hist_t, k.cm_t], pt_t)
            CP(k, "dve", pc[:, :, 0:3], pt[:, 0:72].rearrange("p (r c) -> p c r", r=3), pt_t, pc_t)
        if k.opts.get('stopB', 9) <= 1:
            return
        yield
        pt, pt_t = psum(k, "f")
        for kc in range(8):
            MM(k, pt[:, 0:16], hTt[:, kc, :], wab[:, kc, :], [hTt_t, wab_t], pt_t, start=(kc == 0), stop=(kc == 7))
        ACT(k, gb[:, 8:16], pt[:, 8:16], AF.Sigmoid, pt_t, gb_t)
        TT(k, "dve", gb[:, 40:48], pt[:, 0:8], k.dtb, ALU.add, [pt_t, k.small_t], gb_t)
        ACT(k, gb[:, 40:48], gb[:, 40:48], AF.Exp, gb_t, gb_t)
        ACT(k, gb[:, 40:48], gb[:, 40:48], AF.Ln, [gb_t, k.epsc_t], gb_t, bias=k.epsc[:, 1:2])
        TT(k, "dve", gb[:, 0:8], gb[:, 40:48], k.negA[:], ALU.mult, [gb_t, k.negA_t], gb_t)
        if virt:
            TS(k, "dve", gb[:, 0:16], gb[:, 0:16], rowmask[:, 0:1], ALU.mult, [gb_t, rowmask_t], gb_t)
        pt, pt_t = psum(k, "f")
        MM(k, pt[:, 0:8], k.triu, gb[:, 0:8], [gb_t, k.cm_t], pt_t)
        CP(k, "dve", gb[:, 16:24], pt[:, 0:8], pt_t, gb_t)
        ACT(k, gb[:, 24:32], gb[:, 16:24], AF.Exp, gb_t, gb_t)
        TS(k, "dve", gb[:, 48:56], pt[:, 0:8], -1.0, ALU.mult, pt_t, gb_t)
        TT(k, "dve", gb[:, 32:40], gb[:, 24:32], gb[:, 8:16], ALU.mult, gb_t, gb_t)
        if k.opts.get('stopB', 9) <= 2:
            return
        yield
        for cg in range(6):
            pt, pt_t = psum(k, "f")
            for c4 in range(4):
                cc = cg * 4 + c4
                for kc in range(8):
                    MM(k, pt[:, c4 * 128:(c4 + 1) * 128], wB[:, kc, cc * 128:(cc + 1) * 128], hTt[:, kc, :], [hTt_t, wB_t], pt_t, start=(kc == 0), stop=(kc == 7))
            CP(k, "act" if cg % 2 == 0 else "dve", pc[:, cg * 4:cg * 4 + 4, 3:131], pt[:].rearrange("p (c f) -> p c f", f=128), pt_t, pc_t)
            yield
        if virt or t == ntiles - 1:
            dstc = O["convs"][t[1]] if virt else O["convp"]
            srcc = pc[:, :, 4:7] if virt else pc[:, :, 128:131]
            for rr in range(3):
                k.S.op("sp", lambda e, o=dstc[rr, :].rearrange("(c p) -> p c", p=128), i_=srcc[:, :, rr]: e.dma_start(out=o, in_=i_, allow_slow_non_contiguous=True), [pc_t], [], dsem=cvsem)
        yield
        for th, e_, ta, ta_t in ((0, "pool", tmpa, tmpa_t), (1, "dve", tmpc, tmpc_t)):
            cs = slice(th * 8, th * 8 + 8)
            wb_ = [k.convw[:, cs, i:i + 1].broadcast_to([128, 8, 128]) for i in range(4)]
            TT(k, e_, cv[:, cs, :], pc[:, cs, 0:128], wb_[0], ALU.mult, [pc_t, k.convw_t], cv_ts[th])
            for i in (1, 2, 3):
                TT(k, e_, ta[:], pc[:, cs, i:i + 128], wb_[i], ALU.mult, [pc_t, k.convw_t], ta_t)
                TT(k, e_, cv[:, cs, :], cv[:, cs, :], ta[:], ALU.add, [ta_t, cv_ts[th]], cv_ts[th])
            yield
        for half in range(2):
            pt, pt_t = psum(k, "f")
            for j in range(4):
                c_ = half * 4 + j
                for i in range(4):
                    MM(k, pt[:, j * 128:(j + 1) * 128], diagW[:, c_, i, :], pc[:, 16 + c_, i:i + 128], [diagW_t, pc_t], pt_t, start=(i == 0), stop=(i == 3))
            ACT(k, cv[:, 16 + half * 4:16 + half * 4 + 4, :], pt[:].rearrange("p (c f) -> p c f", f=128), AF.Silu, pt_t, cv_ts[2])
            yield
        CP(k, "pool", pc[:, :, 0:3], pc[:, :, 128:131], pc_t, pc_t)
        for _ in range(k.opts.get('cwait', 4)):
            yield
        for th in (1, 0):
            ACT(k, cv[:, th * 8:th * 8 + 8, :], cv[:, th * 8:th * 8 + 8, :], AF.Silu, cv_ts[th], cv_ts[th])
            yield
        yield
        TT(k, "pool", sq2[:], cv[:, 0:16, :], cv[:, 0:16, :], ALU.mult, cv_ts[0:2], sq2_t)
        for _ in range(3):
            yield
        for j in range(4):
            pt, pt_t = psum(k, "f")
            MM(k, pt[:], k.onesb[:], sq2[:, j * 4:(j + 1) * 4, :], [sq2_t, k.onesb_t], pt_t)
            ACT(k, rn[:, j * 4:(j + 1) * 4, :], pt[:].rearrange("p (c f) -> p c f", f=128), AF.Ln, [pt_t, k.epsc_t], rn_t, bias=k.epsc[:, 0:1])
        yield
        ACT(k, rn[:], rn[:], AF.Exp, rn_t, rn_t, scale=-0.5)
        yield
        STT(k, qkvT[:, 0:8, :], cv[:, 0:8, :], 128 ** -0.5, rn[:, 0:8, :], ALU.mult, ALU.mult, [cv_ts[0], rn_t], qkvT_t)
        TT(k, "dve", qkvT[:, 8:16, :], cv[:, 8:16, :], rn[:, 8:16, :], ALU.mult, [cv_ts[1], rn_t], qkvT_t)
        CP(k, "act", qkvT[:, 16:24, :], cv[:, 16:24, :], cv_ts[2], qkvT_t)
        yield
        if "qkvT" in k.dbg and ti == k.opts.get("dbg_tile", 0):
            dma_out(k, O["qkvT"], qkvT[:], [qkvT_t])
            dma_out(k, O["gb"], gb[:], [gb_t])
        yield
        for g4 in range(4):
            pt, pt_t = psum(k, "f")
            ptb = pt[:].bitcast(BF16)
            for c4 in range(4):
                TR(k, ptb[:, c4 * 128:(c4 + 1) * 128], qkvT[:, 8 + g4 * 4 + c4, :], k.identb[:], [qkvT_t, k.identb_t], pt_t)
            CP(k, "act" if g4 % 2 == 0 else "dve", kvtok[:, g4 * 4:(g4 + 1) * 4, :], ptb[:, 0:512].rearrange("p (c f) -> p c f", f=128), pt_t, kvtok_t)
        if k.opts.get('stopB', 9) <= 4:
            return

    def units(ti, t):
        virt = isinstance(t, tuple)
        qkvT, qkvT_t = qkvTs[ti % 2]
        kvtok, kvtok_t = kvtoks[ti % 2]
        gb, gb_t = gbs[ti % 2]
        if virt:
            s = t[1]
            dst = O["deltap"] if s == 0 else O["deltas"][(s - 1) * 8:s * 8]
            dma_out(k, dst.rearrange("h k v -> k h v"), Sst[:], [Sst_t] + Sh_t, sem=stsem)
            dma_in(k, Sst[:], I["sdelta"][s * 8:(s + 1) * 8].rearrange("h k v -> k h v"), [Sst_t] + Sh_t, stsem)
            for h in range(NH):
                CP(k, "act", Sbf[par[h]][:, h, :], Sst[:, h, :], Sh_t[h], Sbf_t[par[h]][h])
        yield
        if k.opts.get('stopB', 9) <= 4:
            return
        heads = list(range(NH))

        def stage(mm_fn, ev_fn, chunk=k.opts.get('chunk', 4)):
            for c0 in range(0, NH, chunk):
                banks = {}
                for h in heads[c0:c0 + chunk]:
                    banks[h] = psum(k, "u")
                    mm_fn(h, banks[h][0], banks[h][1])
                for h in heads[c0:c0 + chunk]:
                    ev_fn(h, banks[h][0], banks[h][1])
                yield

        for h in heads:
            EV(k, T_(h, "gTri")[:], k.triu, [k.cm_t, gb_t], Tt(h, "gTri"), scale=gb[:, h:h + 1])
        yield

        def mm(h, pt, pt_t):
            MM(k, pt[:, 0:128], k.onesf[:], T_(h, "gTri")[:], [Tt(h, "gTri"), k.onesf_t], pt_t)

        def ev(h, pt, pt_t):
            ACT(k, T_(h, "absG")[:], pt[:, 0:128], AF.Abs, [pt_t, gb_t], Tt(h, "absG"), bias=gb[:, 48 + h:49 + h])
            ACT(k, T_(h, "ecr")[:], pt[:, 0:128], AF.Exp, pt_t, Tt(h, "ecr"))
            ACT(k, T_(h, "E")[:], T_(h, "absG")[:], AF.Exp, Tt(h, "absG"), Tt(h, "E"), scale=-1.0)
        yield from stage(mm, ev)
        yield
        for h in heads:
            EV(k, T_(h, "kt")[:], kvtok[:, h, :], [kvtok_t, Tt(h, "E")], Tt(h, "kt"), scale=T_(h, "E")[:, 127:128])
            TT(k, "pool", T_(h, "qeT")[:], qkvT[:, h, :], T_(h, "ecr")[:], ALU.mult, [qkvT_t, Tt(h, "ecr")], Tt(h, "qeT"))
            TS(k, "dve", T_(h, "R")[:, 0:128], kvtok[:, 8 + h, :], gb[:, 8 + h:9 + h], ALU.mult, [kvtok_t, gb_t], Tt(h, "R"))
            TS(k, "dve", T_(h, "R")[:, 128:256], kvtok[:, h, :], gb[:, 32 + h:33 + h], ALU.mult, [kvtok_t, gb_t], Tt(h, "R"))
        yield

        def mm(h, pt, pt_t):
            MM(k, pt[:, 0:128], qkvT[:, 8 + h, :], qkvT[:, 8 + h, :], qkvT_t, pt_t)
            MM(k, pt[:, 128:256], qkvT[:, 8 + h, :], qkvT[:, h, :], qkvT_t, pt_t)

        def ev(h, pt, pt_t):
            TT(k, "dve", T_(h, "ELd")[:], pt[:, 0:128], T_(h, "E")[:], ALU.mult, [pt_t, Tt(h, "E")], Tt(h, "ELd"))
            TT(k, "dve", T_(h, "EU")[:], pt[:, 128:256], T_(h, "E")[:], ALU.mult, [pt_t, Tt(h, "E")], Tt(h, "EU"))
            STT(k, T_(h, "Ad")[:], T_(h, "ELd")[:], gb[:, 8 + h:9 + h], k.ldiag, ALU.mult, ALU.mult, [gb_t, Tt(h, "ELd"), k.cm_t], Tt(h, "Ad"))
            STT(k, T_(h, "Ao")[:], T_(h, "ELd")[:], gb[:, 8 + h:9 + h], k.loff, ALU.mult, ALU.mult, [gb_t, Tt(h, "ELd"), k.cm_t], Tt(h, "Ao"))
            TT(k, "pool", T_(h, "qkm")[:], T_(h, "EU")[:], k.uincl, ALU.mult, [Tt(h, "EU"), k.cm_t], Tt(h, "qkm"))
        yield from stage(mm, ev)
        yield

        def mm(h, pt, pt_t):
            TR(k, pt[:].bitcast(BF16)[:, 0:128], T_(h, "Ad")[:], k.identb[:], [Tt(h, "Ad"), k.identb_t], pt_t)

        def ev(h, pt, pt_t):
            ptb = pt[:].bitcast(BF16)[:, 0:128]
            EV(k, T_(h, "XT0")[:], ptb, pt_t, Tt(h, "XT0"))
            TT(k, "dve", T_(h, "DT0")[:], k.identb[:], ptb, ALU.subtract, [pt_t, k.identb_t], Tt(h, "DT0"))
        yield from stage(mm, ev)
        yield
        names = [("Ad", "XT0")] + [("X%d" % (kk % 2), "XT%d" % ((kk + 1) % 2)) for kk in range(4)]
        dts = ["DT0", "DT1", "DT0", "DT1", "DT0"]

        def emit_sq(kk):
            Xc, XTc = names[kk]
            Xn, XTn = names[kk + 1]
            last = kk == 3

            def mm(h, pt, pt_t):
                MM(k, pt[:, 0:128], T_(h, XTc)[:], T_(h, Xc)[:], [Tt(h, XTc), Tt(h, Xc)], pt_t)
                if not last:
                    MM(k, pt[:, 128:256], T_(h, Xc)[:], T_(h, XTc)[:], [Tt(h, XTc), Tt(h, Xc)], pt_t)

            def ev(h, pt, pt_t):
                if last:
                    EV(k, T_(h, Xn)[:], pt[:, 0:128], pt_t, Tt(h, Xn))
                else:
                    e_ = "act" if h % 2 == 0 else "dve"
                    CP(k, e_, T_(h, Xn)[:], pt[:, 0:128], pt_t, Tt(h, Xn))
                    CP(k, e_, T_(h, XTn)[:], pt[:, 128:256], pt_t, Tt(h, XTn))
            yield from stage(mm, ev)

        def emit_dt(kk):
            Xn = names[kk + 1][0]
            DTc_, DTn = dts[kk], dts[kk + 1]

            def mm(h, pt, pt_t):
                MM(k, pt[:, 0:128], T_(h, Xn)[:], T_(h, DTc_)[:], [Tt(h, Xn), Tt(h, DTc_)], pt_t)

            def ev(h, pt, pt_t):
                TT(k, "dve", T_(h, DTn)[:], T_(h, DTc_)[:], pt[:, 0:128], ALU.add, [pt_t, Tt(h, DTc_)], Tt(h, DTn))
            yield from stage(mm, ev)
        for step in (("sq", 0), ("sq", 1), ("dt", 0), ("sq", 2), ("dt", 1), ("sq", 3), ("dt", 2), ("dt", 3)):
            yield from (emit_sq if step[0] == "sq" else emit_dt)(step[1])
        DTc = dts[4]

        def mm(h, pt, pt_t):
            MM(k, pt[:, 0:128], T_(h, "Ao")[:], T_(h, DTc)[:], [Tt(h, "Ao"), Tt(h, DTc)], pt_t)

        def ev(h, pt, pt_t):
            EV(k, T_(h, "nNT")[:], pt[:, 0:128], pt_t, Tt(h, "nNT"), scale=-1.0)
        yield from stage(mm, ev)
        yield
        Xcur = "Xa"
        for it in range(4):
            prev = "Xb" if Xcur == "Xa" else "Xa"

            def mm(h, pt, pt_t):
                MM(k, pt[:, 0:256], T_(h, DTc)[:], T_(h, "R")[:], [Tt(h, DTc), Tt(h, "R")], pt_t, start=True, stop=(it == 0))
                if it > 0:
                    MM(k, pt[:, 0:256], T_(h, "nNT")[:], T_(h, prev)[:], [Tt(h, "nNT"), Tt(h, prev)], pt_t, start=False, stop=True)

            def ev(h, pt, pt_t):
                CP(k, "act" if h % 2 == 0 else "dve", T_(h, Xcur)[:], pt[:, 0:256], pt_t, Tt(h, Xcur))
            yield from stage(mm, ev)
            Xfin = Xcur
            Xcur = prev
            yield

        def mm(h, pt, pt_t):
            TR(k, pt[:].bitcast(BF16)[:, 0:128], T_(h, Xfin)[:, 128:256], k.identb[:], [Tt(h, Xfin), k.identb_t], pt_t)

        def ev(h, pt, pt_t):
            EV(k, T_(h, "nwT")[:], pt[:].bitcast(BF16)[:, 0:128], pt_t, Tt(h, "nwT"), scale=-1.0)
        yield from stage(mm, ev)
        yield

        def mm(h, pt, pt_t):
            MM(k, pt[:, 0:128], T_(h, "nwT")[:], Sbf[par[h]][:, h, :], [Tt(h, "nwT"), Sbf_t[par[h]][h]], pt_t)

        def ev(h, pt, pt_t):
            TT(k, "dve", T_(h, "vnew")[:], T_(h, Xfin)[:, 0:128], pt[:, 0:128], ALU.add, [pt_t, Tt(h, Xfin)], Tt(h, "vnew"))
        yield from stage(mm, ev)
        yield

        def mm(h, pt, pt_t):
            p = par[h]
            MM(k, pt[:, 0:128], T_(h, "qeT")[:], Sbf[p][:, h, :], [Tt(h, "qeT"), Sbf_t[p][h]], pt_t, start=True, stop=False)
            MM(k, pt[:, 0:128], T_(h, "qkm")[:], T_(h, "vnew")[:], [Tt(h, "qkm"), Tt(h, "vnew")], pt_t, start=False, stop=True)
            MM(k, pt[:, 128:256], T_(h, "kt")[:], T_(h, "vnew")[:], [Tt(h, "kt"), Tt(h, "vnew")], pt_t)

        def ev(h, pt, pt_t):
            p = par[h]
            st, st_t = T_(h, "st"), Tt(h, "st")
            STT(k, Sst[:, h, :], Sst[:, h, :], T_(h, "ecr")[:, 127:128], pt[:, 128:256], ALU.mult, ALU.add, [pt_t, Tt(h, "ecr"), Sh_t[h]], Sh_t[h])
            EV(k, Sbf[1 - p][:, h, :], Sst[:, h, :], Sh_t[h], Sbf_t[1 - p][h])
            par[h] = 1 - p
            ACT(k, T_(h, "o2")[:], pt[:, 0:128], AF.Square, pt_t, Tt(h, "o2"))
            k.S.op("dve", lambda e, o=st[:, 0:1], i=T_(h, "o2")[:]: e.tensor_reduce(out=o, in_=i, axis=AX.X, op=ALU.add), [Tt(h, "o2")], [st_t])
            ACT(k, st[:, 1:2], st[:, 0:1], AF.Ln, [st_t, k.epsc_t], st_t, scale=1.0 / 128, bias=k.epsc[:, 0:1])
            ACT(k, st[:, 2:3], st[:, 1:2], AF.Exp, st_t, st_t, scale=-0.5)
            STT(k, T_(h, "yb")[:], pt[:, 0:128], st[:, 2:3], k.bnw, ALU.mult, ALU.mult, [pt_t, st_t, k.small_t], Tt(h, "yb"))
        yield from stage(mm, ev)
        yield

        def mm(h, pt, pt_t):
            TR(k, pt[:].bitcast(BF16)[:, 0:128], T_(h, "yb")[:], k.identb[:], [Tt(h, "yb"), k.identb_t], pt_t)

        def ev(h, pt, pt_t):
            EV(k, ybT[:, h, :], pt[:].bitcast(BF16)[:, 0:128], pt_t, ybT_t)
        yield from stage(mm, ev)
        if not virt:
            dma_out(k, ybd[:, :, t * 128:(t + 1) * 128], ybT[:], [ybT_t], sem=ybsem, writes=[k.ybd_t])
        else:
            s = t[1]
            dma_out(k, ybd[:, :, SEQ + s * ST:SEQ + (s + 1) * ST], ybT[:, :, 0:ST], [ybT_t], sem=ybsem, writes=[k.ybd_t])

    k.pspools = {"u": [[0, 1, 2, 3, 4, 5], 0], "f": [[6, 7], 0]}
    for _ in front(0, vt_list[0]):
        pass
    for ti, t in enumerate(vt_list):
        gf = front(ti + 1, vt_list[ti + 1]) if ti + 1 < len(vt_list) else None
        for _ in units(ti, t):
            if gf is not None:
                if next(gf, "done") == "done":
                    gf = None
        if gf is not None:
            for _ in gf:
                pass
    k.pspools = None
    dst = O["deltap"] if nvirt == 0 else O["deltas"][(nvirt - 1) * 8:nvirt * 8]
    dma_out(k, dst.rearrange("h k v -> k h v"), Sst[:], [Sst_t] + Sh_t, sem=stsem)
    if "ybT" in k.dbg:
        dma_out(k, O["ybT"], k.ybT_d, [k.ybd_t])
class Stg:
    def __init__(self, k, es, nkc, chunk, name):
        self.k = k
        self.nkc = nkc
        self.chunk = chunk
        self.st = [sbt(k, es, "%s%d" % (name, i), [128, nkc, chunk], F32) for i in range(2)]
        self.ss = [k.S.dsem() for _ in range(2)]
        self.j = 0

    def load(self, dst, dst_t, dcol0, src, col0, ncols):
        k = self.k
        wv = src.rearrange("(kc p) n -> p kc n", p=128)
        c = 0
        while c < ncols:
            n = min(self.chunk, ncols - c)
            w, w_t = self.st[self.j % 2]
            dma_in(k, w[:, :, 0:n], wv[:, :, col0 + c:col0 + c + n], w_t, self.ss[self.j % 2])
            CP(k, "pool" if self.j % 2 == 0 else "act", dst[:, :, dcol0 + c:dcol0 + c + n], w[:, :, 0:n], w_t, dst_t)
            c += n
            self.j += 1


def subseq_chunks(d, maxlen):
    L = SEQ // d
    for r in range(d):
        for u0 in range(0, L, maxlen):
            n = min(maxlen, L - u0)
            yield (r * L + u0, r + d * u0, n)


def tslice(tok0, n, d):
    return slice(tok0, tok0 + d * (n - 1) + 1, d)


def cache_copies(k):
    for (W, d) in GROUPS:
        src, dst = k.I["kv%d" % W], k.O["kvs%d" % W]
        for s in range(NS):
            r = 0
            while r < W - ST:
                n = min(256, W - ST - r)
                dma_out(k, dst[s, r:r + n, :], src[s, r + ST:r + ST + n, :], [], eng="pool")
                r += n


def phaseA(k, es):
    nc, S, I, O = k.nc, k.S, k.I, k.O
    hT, hT_t = sbt(k, es, "hT", [128, 8, NTOK], BF16)
    hsem = S.dsem()
    hTd = k.hT_d.rearrange("(kc p) t -> p kc t", p=128)
    for kc in range(8):
        dma_in(k, hT[:, kc, :], hTd[:, kc, :], hT_t, hsem, reads=[k.hTd_t])
    if not k.opts.get("skip_kv"):
        with contextlib.ExitStack() as tes:
            wkvs = [sbt(k, tes, "wkv%d" % i, [128, 8, 1024], BF16) for i in range(2)]
            okv = [sbt(k, tes, "okv%d" % i, [128, 1024], F32) for i in range(2)]
            oks = [S.dsem() for _ in range(2)]
            stg = Stg(k, tes, 8, 256, "kvst")
            j = 0
            def load_kv(g):
                w_, w_t_ = wkvs[g % 2]
                stg.load(w_, w_t_, 0, I["w_in"], OFF_KA + g * 512, 512)
                stg.load(w_, w_t_, 512, I["w_in"], OFF_VA + g * 512, 512)
            load_kv(0)
            for g, (W, d) in enumerate(GROUPS):
                wkv, wkv_t = wkvs[g % 2]
                if g + 1 < len(GROUPS):
                    load_kv(g + 1)
                t0 = 32 - W // 128
                for tt in list(range(t0, 32)) + [-1]:
                    M = 128 if tt >= 0 else NS * ST
                    cs = slice(tt * 128, (tt + 1) * 128) if tt >= 0 else slice(SEQ, NTOK)
                    pa, pb = psum(k), psum(k)
                    for half, pp in ((0, pa), (1, pb)):
                        for kc in range(8):
                            MM(k, pp[0][0:M, :], hT[:, kc, cs], wkv[:, kc, half * 512:(half + 1) * 512], [hT_t, wkv_t], pp[1], start=(kc == 0), stop=(kc == 7))
                    o, o_t = okv[j % 2]
                    CP(k, "act", o[0:M, 0:512], pa[0][0:M, :], pa[1], o_t)
                    CP(k, "dve", o[0:M, 512:1024], pb[0][0:M, :], pb[1], o_t)
                    if tt >= 0:
                        dma_out(k, O["kvp%d" % W][(tt - t0) * 128:(tt - t0 + 1) * 128, :], o[:], [o_t], sem=oks[j % 2])
                    else:
                        for s in range(NS):
                            dma_out(k, O["kvs%d" % W][s, W - ST:W, :], o[s * ST:(s + 1) * ST, :], [o_t], sem=oks[j % 2])
                    j += 1
            S.barrier()
    if k.opts.get("skip_attn"):
        return
    UA, UA_t = sbt(k, es, "UaccA", [128, NTOK], F32)
    UB, UB_t = sbt(k, es, "UaccB", [128, NTOK], F32)
    Us = ((UA, UA_t), (UB, UB_t))
    qT, qT_t = sbt(k, es, "qT", [128, NTOK], BF16)
    kT, kT_t = sbt(k, es, "kT", [128, NTOK], BF16)
    vT, vT_t = sbt(k, es, "vT", [128, SEQ], BF16)
    Vaug, Vaug_t = sbt(k, es, "Vaug", [128, 32, 2, 128], BF16)
    Vsn, Vsn_t = sbt(k, es, "Vsn", [ST, NS, 2, 128], BF16)
    vcs = [sbt(k, es, "vc%d" % i, [128, 2, 128], BF16) for i in range(8)]
    kcTs = [sbt(k, es, "kcT%d" % i, [128, 128], BF16) for i in range(8)]
    cks = [sbt(k, es, "ck%d" % i, [128, 2, 128], F32) for i in range(8)]
    cksem = [S.dsem() for _ in range(8)]
    Ps = [sbt(k, es, "Ps%d" % i, [128, 32], BF16) for i in range(8)]
    NPT = 6
    wqs = [sbt(k, es, "wq%d" % i, [128, 8, 384], BF16) for i in range(2)]
    wza, wza_t = sbt(k, es, "wza", [128, 8, 128], BF16)
    stg = Stg(k, es, 8, 128, "ast")
    Bts = [sbt(k, es, "Bt%d" % i, [128, 2, 256], F32) for i in range(2)]
    BDs = [sbt(k, es, "BD%d" % i, [ST, 2, ST], F32) for i in range(2)]
    bsems = [S.dsem() for _ in range(2)]
    bdsems = [S.dsem() for _ in range(2)]
    PT = [[sbt(k, es, "PT%d_%d" % (hd, p), [128, 256], BF16) for p in range(NPT)] for hd in range(2)]
    tbs = [sbt(k, es, "tb%d" % i, [128, 256], F32) for i in range(8)]
    sz, sz_t = sbt(k, es, "sz", [128, 512], F32)
    rec, rec_t = sbt(k, es, "rec", [128, 512], F32)
    t1, t1_t = sbt(k, es, "t1", [128, 512], F32)
    yats = [sbt(k, es, "yat%d" % i, [128, 512], BF16) for i in range(2)]
    yasem = [S.dsem() for _ in range(2)]
    k.yad_t = Tok("yad")
    MS(k, "pool", Vaug[:, :, 0, 64:128], 1.0, Vaug_t)
    MS(k, "pool", Vaug[:, :, 1, 0:64], 1.0, Vaug_t)
    MS(k, "pool", Vsn[:, :, 0, 64:128], 1.0, Vsn_t)
    MS(k, "pool", Vsn[:, :, 1, 0:64], 1.0, Vsn_t)
    for vc, vc_t in vcs:
        MS(k, "pool", vc[:, 0, 64:128], 1.0, vc_t)
        MS(k, "pool", vc[:, 1, 0:64], 1.0, vc_t)
    tbi = 0
    cki = 0
    yi = 0
    nsp = k.opts.get("nsp", 4)
    pairs = [(sp_, g_) for sp_ in range(nsp) for g_ in range(len(GROUPS))]

    def load_wq(idx):
        sp_, g_ = pairs[idx]
        w_, w_t_ = wqs[idx % 2]
        stg.load(w_, w_t_, 0, I["w_in"], OFF_QA + g_ * 512 + sp_ * 128, 128)
        stg.load(w_, w_t_, 128, I["w_in"], OFF_KA + g_ * 512 + sp_ * 128, 128)
        stg.load(w_, w_t_, 256, I["w_in"], OFF_VA + g_ * 512 + sp_ * 128, 128)
        for hd_ in range(2):
            dma_in(k, Bts[idx % 2][0][:, hd_, :], I["biasT"][g_ * 8 + 2 * sp_ + hd_], Bts[idx % 2][1], bsems[idx % 2])
            dma_in(k, BDs[idx % 2][0][:, hd_, :], I["biasD"][g_ * 8 + 2 * sp_ + hd_], BDs[idx % 2][1], bdsems[idx % 2])
    if pairs:
        load_wq(0)
    pidx = -1
    for sp in range(nsp):
        for g, (W, d) in enumerate(GROUPS):
            pidx += 1
            wq, wq_t = wqs[pidx % 2]
            L = SEQ // d
            nb = L // 128
            Bt, Bt_t = Bts[pidx % 2]
            BD, BD_t = BDs[pidx % 2]
            for c0 in range(0, SEQ, 512):
                for which, dstT, dst_t in ((0, qT, qT_t), (1, kT, kT_t), (2, vT, vT_t)):
                    pt, pt_t = psum(k)
                    for kc in range(8):
                        MM(k, pt[:, 0:512], wq[:, kc, which * 128:(which + 1) * 128], hT[:, kc, c0:c0 + 512], [wq_t, hT_t], pt_t, start=(kc == 0), stop=(kc == 7))
                    ov = dstT[:, 0:SEQ].rearrange("p (r u) -> p r u", r=d)[:, :, c0 // d:(c0 + 512) // d]
                    iv = pt[:, 0:512].rearrange("p (a r) -> p r a", r=d)
                    if which == 0:
                        ACT(k, ov, iv, AF.Copy, pt_t, dst_t, scale=0.125)
                    elif which == 1:
                        CP(k, "dve", ov, iv, pt_t, dst_t)
                    else:
                        EV(k, ov, iv, pt_t, dst_t)
            for which, dstT, dst_t in ((0, qT, qT_t), (1, kT, kT_t)):
                pt, pt_t = psum(k)
                for kc in range(8):
                    MM(k, pt[:, 0:NS * ST], wq[:, kc, which * 128:(which + 1) * 128], hT[:, kc, SEQ:NTOK], [wq_t, hT_t], pt_t, start=(kc == 0), stop=(kc == 7))
                if which == 0:
                    ACT(k, dstT[:, SEQ:NTOK], pt[:, 0:NS * ST], AF.Copy, pt_t, dst_t, scale=0.125)
                else:
                    CP(k, "dve", dstT[:, SEQ:NTOK], pt[:, 0:NS * ST], pt_t, dst_t)
            for n4 in range(8):
                pt, pt_t = psum(k)
                ptb = pt[:].bitcast(BF16)
                for j in range(4):
                    n = n4 * 4 + j
                    TR(k, ptb[:, j * 128:(j + 1) * 128], vT[:, n * 128:(n + 1) * 128], k.identb[:], [vT_t, k.identb_t], pt_t)
                pv = ptb[:, 0:512].rearrange("p (c f) -> p c f", f=128)
                CP(k, "act", Vaug[:, n4 * 4:(n4 + 1) * 4, 0, 0:64], pv[:, :, 0:64], pt_t, Vaug_t)
                CP(k, "dve", Vaug[:, n4 * 4:(n4 + 1) * 4, 1, 64:128], pv[:, :, 64:128], pt_t, Vaug_t)
            pt, pt_t = psum(k)
            for s in range(NS):
                for kc in range(8):
                    MM(k, pt[0:ST, s * 128:(s + 1) * 128], hT[:, kc, SEQ + s * ST:SEQ + (s + 1) * ST], wq[:, kc, 256:384], [wq_t, hT_t], pt_t, start=(kc == 0), stop=(kc == 7))
            pv = pt[0:ST, :].rearrange("p (c f) -> p c f", f=128)
            CP(k, "act", Vsn[:, :, 0, 0:64], pv[:, :, 0:64], pt_t, Vsn_t)
            CP(k, "dve", Vsn[:, :, 1, 64:128], pv[:, :, 64:128], pt_t, Vsn_t)
            if pidx + 1 < len(pairs):
                load_wq(pidx + 1)
            def unit_info(kb, hd):
                first = (kb % nb == 0)
                lastb = (kb % nb == nb - 1)
                r, u0 = (kb * 128) // L, (kb * 128) % L
                return first, (128 if lastb else 256), tslice(r + d * u0, 128, d)

            def emit_qk(grp):
                nonlocal tbi
                banks = {}
                for (kb, hd) in grp:
                    first, N, cols = unit_info(kb, hd)
                    hs = slice(64 * hd, 64 * hd + 64)
                    banks[(kb, hd)] = psum(k)
                    pt, pt_t = banks[(kb, hd)]
                    MM(k, pt[:, 0:N], kT[hs, kb * 128:(kb + 1) * 128], qT[hs, kb * 128:kb * 128 + N], [kT_t, qT_t], pt_t)
                tb_of = {}
                for (kb, hd) in grp:
                    first, N, cols = unit_info(kb, hd)
                    pt, pt_t = banks[(kb, hd)]
                    tb_of[(kb, hd)] = tbs[tbi % len(tbs)]
                    tbi += 1
                    tb, tb_t = tb_of[(kb, hd)]
                    TT(k, "dve", tb[:, 0:N], pt[:, 0:N], Bt[:, hd, 0:N], ALU.add, [pt_t, Bt_t], tb_t)
                for (kb, hd) in grp:
                    first, N, cols = unit_info(kb, hd)
                    tb, tb_t = tb_of[(kb, hd)]
                    P, P_t = PT[hd][kb % NPT]
                    ACT(k, P[:, 0:N], tb[:, 0:N], AF.Exp, tb_t, P_t)

            def emit_pv(grp):
                banks = {}
                for (kb, hd) in grp:
                    first, N, cols = unit_info(kb, hd)
                    P, P_t = PT[hd][kb % NPT]
                    banks[(kb, hd)] = psum(k)
                    pu, pu_t = banks[(kb, hd)]
                    if not first:
                        Pp, Pp_t = PT[hd][(kb - 1) % NPT]
                        MM(k, pu[:, 0:128], Vaug[:, kb - 1, hd, :], Pp[:, 128:256], [Vaug_t, Pp_t], pu_t, start=True, stop=False)
                    MM(k, pu[:, 0:128], Vaug[:, kb, hd, :], P[:, 0:128], [Vaug_t, P_t], pu_t, start=first, stop=True)
                for j, (kb, hd) in enumerate(grp):
                    first, N, cols = unit_info(kb, hd)
                    U, U_t = Us[hd]
                    pu, pu_t = banks[(kb, hd)]
                    if g == 0:
                        CP(k, "act" if j % 2 == 0 else "dve", U[:, cols], pu[:, 0:128], pu_t, U_t)
                    else:
                        TT(k, "dve", U[:, cols], U[:, cols], pu[:, 0:128], ALU.add, [pu_t, U_t], U_t)
            grps = [[(kb, hd) for kb in (2 * n, 2 * n + 1) for hd in range(2)] for n in range(16)]
            emit_qk(grps[0])
            for n in range(16):
                if n + 1 < 16:
                    emit_qk(grps[n + 1])
                emit_pv(grps[n])
            subs = [(s, None) for s in range(NS)] if d == 1 else [(s, i) for s in range(NS) for i in range(ST)]
            if k.opts.get("skip_sattn"):
                subs = []
            NBATCH = 4

            def sub_load(j, s, i):
                ck, ck_t = cks[j % len(cks)]
                rows = I["kv%d" % W][s, (0 if i is None else i):W:d, :].rearrange("r (t c) -> r t c", t=2)[:, :, 2 * sp * 64:2 * sp * 64 + 128]
                dma_in(k, ck[:], rows, ck_t, cksem[j % len(cks)])
            for j0 in range(0, min(NBATCH, len(subs))):
                sub_load(cki + j0, *subs[j0])
            for b0 in range(0, len(subs), NBATCH):
                batch = subs[b0:b0 + NBATCH]
                idx = [cki + b0 + j for j in range(len(batch))]
                for j, (s, i) in enumerate(subs[b0 + NBATCH:b0 + 2 * NBATCH]):
                    sub_load(cki + b0 + NBATCH + j, s, i)
                tr = {}
                for j, (s, i) in zip(idx, batch):
                    ck, ck_t = cks[j % len(cks)]
                    tr[j] = psum(k)
                    TR(k, tr[j][0][:, 0:128], ck[:, 0, :], k.ident, [ck_t, k.cm_t], tr[j][1])
                for j, (s, i) in zip(idx, batch):
                    ck, ck_t = cks[j % len(cks)]
                    vc, vc_t = vcs[j % len(vcs)]
                    kcT, kcT_t = kcTs[j % len(kcTs)]
                    CP(k, "act", kcT[:], tr[j][0][:, 0:128], tr[j][1], kcT_t)
                    CP(k, "dve", vc[:, 0, 0:64], ck[:, 1, 0:64], ck_t, vc_t)
                    CP(k, "dve", vc[:, 1, 64:128], ck[:, 1, 64:128], ck_t, vc_t)
                qk = {}
                for j, (s, i) in zip(idx, batch):
                    kcT, kcT_t = kcTs[j % len(kcTs)]
                    q0 = SEQ + s * ST + (0 if i is None else i)
                    nq = ST if i is None else 1
                    qc = slice(q0, q0 + nq)
                    kc_new = slice(SEQ + s * ST, SEQ + (s + 1) * ST)
                    qk[j] = psum(k)
                    pt, pt_t = qk[j]
                    for hd in range(2):
                        hs = slice(64 * hd, 64 * hd + 64)
                        MM(k, pt[:, 16 * hd:16 * hd + nq], kcT[hs, :], qT[hs, qc], [kcT_t, qT_t], pt_t)
                        MM(k, pt[0:ST, 16 * hd + 8:16 * hd + 8 + nq], kT[hs, kc_new], qT[hs, qc], [kT_t, qT_t], pt_t)
                for j, (s, i) in zip(idx, batch):
                    nq = ST if i is None else 1
                    pt, pt_t = qk[j]
                    tb, tb_t = tbs[j % len(tbs)]
                    ptv = pt[:, 0:32].rearrange("p (h c) -> p h c", h=2)
                    tbv = tb[:, 0:32].rearrange("p (h c) -> p h c", h=2)
                    TT(k, "dve", tbv[:, :, 0:nq], ptv[:, :, 0:nq], Bt[:, :, 128:128 + nq], ALU.add, [pt_t, Bt_t], tb_t)
                    Bc = Bt[0:ST, :, 0:ST] if i is None else BD[:, :, i:i + 1]
                    TT(k, "dve", tbv[0:ST, :, 8:8 + nq], ptv[0:ST, :, 8:8 + nq], Bc, ALU.add, [pt_t, Bt_t, BD_t], tb_t)
                for j, (s, i) in zip(idx, batch):
                    nq = ST if i is None else 1
                    tb, tb_t = tbs[j % len(tbs)]
                    P, P_t = Ps[j % len(Ps)]
                    tbv = tb[:, 0:32].rearrange("p (h c) -> p h c", h=2)
                    Pv = P[:, 0:32].rearrange("p (h c) -> p h c", h=2)
                    ACT(k, Pv[:, :, 0:nq], tbv[:, :, 0:nq], AF.Exp, tb_t, P_t)
                    ACT(k, Pv[0:ST, :, 8:8 + nq], tbv[0:ST, :, 8:8 + nq], AF.Exp, tb_t, P_t)
                pv = {}
                for j, (s, i) in zip(idx, batch):
                    nq = ST if i is None else 1
                    vc, vc_t = vcs[j % len(vcs)]
                    P, P_t = Ps[j % len(Ps)]
                    pv[j] = psum(k)
                    pu, pu_t = pv[j]
                    for hd in range(2):
                        MM(k, pu[:, 16 * hd:16 * hd + nq], vc[:, hd, :], P[:, 16 * hd:16 * hd + nq], [vc_t, P_t], pu_t, start=True, stop=False)
                        MM(k, pu[:, 16 * hd:16 * hd + nq], Vsn[:, s, hd, :], P[0:ST, 16 * hd + 8:16 * hd + 8 + nq], [Vsn_t, P_t], pu_t, start=False, stop=True)
                for j, (s, i) in zip(idx, batch):
                    q0 = SEQ + s * ST + (0 if i is None else i)
                    nq = ST if i is None else 1
                    qc = slice(q0, q0 + nq)
                    pu, pu_t = pv[j]
                    for hd in range(2):
                        U, U_t = Us[hd]
                        if g == 0:
                            CP(k, "act", U[:, qc], pu[:, 16 * hd:16 * hd + nq], pu_t, U_t)
                        else:
                            TT(k, "dve", U[:, qc], U[:, qc], pu[:, 16 * hd:16 * hd + nq], ALU.add, [pu_t, U_t], U_t)
            cki += len(subs)
        stg.load(wza, wza_t, 0, I["w_in"], OFF_ZA + sp * 128, 128)
        for c0 in range(0, NTOK, 512):
            n = min(512, NTOK - c0)
            pt, pt_t = psum(k)
            for kc in range(8):
                MM(k, pt[:, 0:n], wza[:, kc, :], hT[:, kc, c0:c0 + n], [wza_t, hT_t], pt_t, start=(kc == 0), stop=(kc == 7))
            ACT(k, sz[:, 0:n], pt[:, 0:n], AF.Silu, pt_t, sz_t)
            ACT(k, rec[0:64, 0:n], UA[64:128, c0:c0 + n], AF.Ln, UA_t, rec_t)
            ACT(k, rec[64:128, 0:n], UB[0:64, c0:c0 + n], AF.Ln, UB_t, rec_t)
            ACT(k, rec[:, 0:n], rec[:, 0:n], AF.Exp, rec_t, rec_t, scale=-1.0)
            TT(k, "pool", t1[0:64, 0:n], UA[0:64, c0:c0 + n], rec[0:64, 0:n], ALU.mult, [UA_t, rec_t], t1_t)
            TT(k, "pool", t1[64:128, 0:n], UB[64:128, c0:c0 + n], rec[64:128, 0:n], ALU.mult, [UB_t, rec_t], t1_t)
            yat, yat_t = yats[yi % 2]
            TT(k, "dve", yat[:, 0:n], t1[:, 0:n], sz[:, 0:n], ALU.mult, [t1_t, sz_t], yat_t)
            dma_out(k, k.yaT_d[sp * 128:(sp + 1) * 128, c0:c0 + n], yat[:, 0:n], [yat_t], sem=yasem[yi % 2], writes=[k.yad_t])
            yi += 1
    if "yaT" in k.dbg:
        S.barrier(("sp",))
        dma_out(k, O["yaT"], k.yaT_d, [k.yad_t])
def phaseF(k, es):
    nc, S, I, O = k.nc, k.S, k.I, k.O
    TF = 256
    wg, wg_t = sbt(k, es, "wg", [128, 8, 2048], BF16)
    wzb, wzb_t = sbt(k, es, "wzb", [128, 8, 1024], BF16)
    wa, wa_t = sbt(k, es, "wa", [128, 4, 1024], BF16)
    wb, wb_t = sbt(k, es, "wb", [128, 8, 1024], BF16)
    wo, wo_t = sbt(k, es, "wo", [128, 8, 1024], BF16)
    lng, lng_t = sbt(k, es, "lng", [128, 2, D], F32)
    gate_rep, gate_t = sbt(k, es, "gate_rep", [128, D], F32)
    gate_s, gates_t = sbt(k, es, "gate_s", [NS * ST, D], F32)
    s0 = S.dsem()
    dma_in(k, lng[:, 0, :], I["ln_g"].partition_broadcast(128), lng_t, s0)
    dma_in(k, lng[:, 1, :], I["ln_b"].partition_broadcast(128), lng_t, s0)
    with contextlib.ExitStack() as tes:
        stg = Stg(k, tes, 8, 256, "fst")
        stg.load(wg, wg_t, 0, I["w_in"], OFF_GA, 2048)
        stg.load(wzb, wzb_t, 0, I["w_in"], OFF_ZB, 1024)
        stg.load(wb, wb_t, 0, I["w_b"], 0, 1024)
        stg.load(wo, wo_t, 0, I["w_out"], 0, 1024)
        stg4 = Stg(k, tes, 4, 256, "fst4")
        stg4.load(wa, wa_t, 0, I["w_a"], 0, 1024)
        gtmp, gtmp_t = sbt(k, tes, "gtmp", [128, 128], F32)
        for fc in range(8):
            TS(k, "dve", gtmp[:], k.onesf[:], k.cond[:, 16 + fc, 0:1], ALU.mult, [k.onesf_t, k.cond_t], gtmp_t)
            pt, pt_t = psum(k)
            MM(k, pt[:, 0:128], gtmp[:], k.ident, [gtmp_t, k.cm_t], pt_t)
            CP(k, "act", gate_rep[:, fc * 128:(fc + 1) * 128], pt[:, 0:128], pt_t, gate_t)
            for s in range(NS):
                TS(k, "dve", gtmp[:, s * ST:(s + 1) * ST], k.onesf[:, 0:ST], k.cond[:, 16 + fc, 1 + s:2 + s], ALU.mult, [k.onesf_t, k.cond_t], gtmp_t)
            pt, pt_t = psum(k)
            MM(k, pt[0:NS * ST, 0:128], gtmp[:, 0:NS * ST], k.ident, [gtmp_t, k.cm_t], pt_t)
            CP(k, "act", gate_s[:, fc * 128:(fc + 1) * 128], pt[0:NS * ST, 0:128], pt_t, gates_t)
        S.barrier()
    hts = [sbt(k, es, "fhT%d" % i, [128, 8, TF], BF16) for i in range(2)]
    yas = [sbt(k, es, "fya%d" % i, [128, 4, TF], BF16) for i in range(2)]
    ybs = [sbt(k, es, "fyb%d" % i, [128, 8, TF], BF16) for i in range(2)]
    lsem = [[S.dsem() for _ in range(3)] for _ in range(2)]
    xts = [sbt(k, es, "fx%d" % i, [128, D], F32) for i in range(2)]
    xsem = [S.dsem() for _ in range(2)]
    ybg, ybg_t = sbt(k, es, "ybg", [128, 8, TF], BF16)
    mTs = [sbt(k, es, "mT%d" % i, [128, 8, TF], BF16) for i in range(2)]
    nhalf, nhalf_t = sbt(k, es, "nhalf", [128, 1], F32)
    MS(k, "dve", nhalf[:], -0.5, nhalf_t)
    sg = [sbt(k, es, "sg%d" % i, [128, TF], F32) for i in range(4)]
    tt_, tt_t = sbt(k, es, "tt", [128, D], F32)
    ys = [sbt(k, es, "fy%d" % i, [128, D], F32) for i in range(2)]
    ysem = [S.dsem() for _ in range(2)]
    stt, stt_t = sbt(k, es, "fstat", [128, 24], F32)
    hTd = k.hT_d.rearrange("(kc p) t -> p kc t", p=128)
    yad = k.yaT_d.rearrange("(kc p) t -> p kc t", p=128)
    ybd = k.ybT_d.rearrange("(kc p) t -> p kc t", p=128)
    xv = I["x"].rearrange("(t p) d -> t p d", p=128)
    ntile = k.opts.get("nftiles", SEQ // TF)
    tiles = [(i * TF, TF) for i in range(ntile)] + [(SEQ, NS * ST)]

    def loads(ti):
        c0, n = tiles[ti]
        b = ti % 2
        dma_in(k, hts[b][0][:, :, 0:n], hTd[:, :, c0:c0 + n], hts[b][1], lsem[b][0], reads=[k.hTd_t])
        dma_in(k, yas[b][0][:, :, 0:n], yad[:, :, c0:c0 + n], yas[b][1], lsem[b][1], reads=[k.yad_t])
        dma_in(k, ybs[b][0][:, :, 0:n], ybd[:, :, c0:c0 + n], ybs[b][1], lsem[b][2], reads=[k.ybd_t])

    xi = 0
    yi = 0
    sgi = 0
    loads(0)
    def s12(ti):
        nonlocal sgi
        c0, n = tiles[ti]
        mT, mT_t = mTs[ti % 2]
        b = ti % 2
        if ti + 1 < len(tiles):
            loads(ti + 1)
        hTt, hTt_t = hts[b]
        ya, ya_t = yas[b]
        yb, yb_t = ybs[b]
        for hb in range(8):
            pt, pt_t = psum(k)
            for kc in range(8):
                MM(k, pt[:, 0:n], wzb[:, kc, hb * 128:(hb + 1) * 128], hTt[:, kc, 0:n], [wzb_t, hTt_t], pt_t, start=(kc == 0), stop=(kc == 7))
            s_, s_t = sg[sgi % 4]
            sgi += 1
            ACT(k, s_[:, 0:n], pt[:, 0:n], AF.Silu, pt_t, s_t)
            TT(k, "dve", ybg[:, hb, 0:n], yb[:, hb, 0:n], s_[:, 0:n], ALU.mult, [yb_t, s_t], ybg_t)
        yield
        for ncb in range(8):
            if ncb == 4:
                yield
            cs = slice(ncb * 128, (ncb + 1) * 128)
            pa, pb, pga, pgb = psum(k), psum(k), psum(k), psum(k)
            for kc in range(4):
                MM(k, pa[0][:, 0:n], wa[:, kc, cs], ya[:, kc, 0:n], [wa_t, ya_t], pa[1], start=(kc == 0), stop=(kc == 3))
            for kc in range(8):
                MM(k, pb[0][:, 0:n], wb[:, kc, cs], ybg[:, kc, 0:n], [wb_t, ybg_t], pb[1], start=(kc == 0), stop=(kc == 7))
            for kc in range(8):
                MM(k, pga[0][:, 0:n], wg[:, kc, cs], hTt[:, kc, 0:n], [wg_t, hTt_t], pga[1], start=(kc == 0), stop=(kc == 7))
            for kc in range(8):
                MM(k, pgb[0][:, 0:n], wg[:, kc, 1024 + ncb * 128:1024 + (ncb + 1) * 128], hTt[:, kc, 0:n], [wg_t, hTt_t], pgb[1], start=(kc == 0), stop=(kc == 7))
            sa, sa_t = sg[sgi % 4]
            sb_, sb_t = sg[(sgi + 1) % 4]
            sgi += 2
            ACT(k, sa[:, 0:n], pga[0][:, 0:n], AF.Sigmoid, pga[1], sa_t)
            ACT(k, sb_[:, 0:n], pgb[0][:, 0:n], AF.Sigmoid, pgb[1], sb_t)
            TT(k, "dve", sa[:, 0:n], pa[0][:, 0:n], sa[:, 0:n], ALU.mult, [pa[1], sa_t], sa_t)
            TT(k, "dve", sb_[:, 0:n], pb[0][:, 0:n], sb_[:, 0:n], ALU.mult, [pb[1], sb_t], sb_t)
            TT(k, "dve", mT[:, ncb, 0:n], sa[:, 0:n], sb_[:, 0:n], ALU.add, [sa_t, sb_t], mT_t)

    def s3(ti):
        nonlocal xi, yi
        c0, n = tiles[ti]
        mT, mT_t = mTs[ti % 2]
        for j0 in range(0, n, 128):
            M = min(128, n - j0)
            samp = c0 >= SEQ
            xt, xt_t = xts[xi % 2]
            if samp:
                dma_in(k, xt[0:M, :], I["xs"], xt_t, xsem[xi % 2])
            else:
                dma_in(k, xt[:], xv[(c0 + j0) // 128], xt_t, xsem[xi % 2])
            xi += 1
            p0, p1 = psum(k), psum(k)
            for half, pp in ((0, p0), (1, p1)):
                for kc in range(8):
                    MM(k, pp[0][0:M, :], mT[:, kc, j0:j0 + M], wo[:, kc, half * 512:(half + 1) * 512], [mT_t, wo_t], pp[1], start=(kc == 0), stop=(kc == 7))
            gr, gr_t = (gate_s, gates_t) if samp else (gate_rep, gate_t)
            TT(k, "dve", tt_[0:M, 0:512], p0[0][0:M, :], gr[0:M, 0:512], ALU.mult, [p0[1], gr_t], tt_t)
            TT(k, "dve", tt_[0:M, 512:1024], p1[0][0:M, :], gr[0:M, 512:1024], ALU.mult, [p1[1], gr_t], tt_t)
            STT(k, tt_[0:M, :], xt[0:M, :], ALPHA, tt_[0:M, :], ALU.mult, ALU.add, [xt_t, tt_t], tt_t)
            k.S.op("dve", lambda e, o=stt[0:M, 0:6], i_=tt_[0:M, 0:512]: e.bn_stats(out=o, in_=i_), [tt_t], [stt_t])
            k.S.op("dve", lambda e, o=stt[0:M, 6:12], i_=tt_[0:M, 512:1024]: e.bn_stats(out=o, in_=i_), [tt_t], [stt_t])
            k.S.op("dve", lambda e, o=stt[0:M, 12:14], i_=stt[0:M, 0:12]: e.bn_aggr(out=o, in_=i_), [stt_t], [stt_t])
            TS(k, "dve", stt[0:M, 14:15], stt[0:M, 13:14], 1e-5, ALU.add, stt_t, stt_t)
            TT(k, "pool", stt[0:M, 15:16], stt[0:M, 14:15], nhalf[0:M, :], ALU.pow, [stt_t, nhalf_t], stt_t)
            y, y_t = ys[yi % 2]
            STT(k, stt[0:M, 16:17], stt[0:M, 12:13], -1.0, stt[0:M, 15:16], ALU.mult, ALU.mult, [stt_t], stt_t)
            ACT(k, y[0:M, :], tt_[0:M, :], AF.Identity, [tt_t, stt_t], y_t, scale=stt[0:M, 15:16], bias=stt[0:M, 16:17])
            TT(k, "pool", y[0:M, :], y[0:M, :], lng[0:M, 0, :], ALU.mult, [y_t, lng_t], y_t)
            TT(k, "pool", y[0:M, :], y[0:M, :], lng[0:M, 1, :], ALU.add, [y_t, lng_t], y_t)
            if samp:
                dma_out(k, O["ys"], y[0:M, :], [y_t], sem=ysem[yi % 2])
            else:
                dma_out(k, O["y"][c0 + j0:c0 + j0 + 128, :], y[:], [y_t], sem=ysem[yi % 2])
            yi += 1
            yield

    for _ in s12(0):
        pass
    for ti in range(len(tiles)):
        g12 = s12(ti + 1) if ti + 1 < len(tiles) else iter(())
        g3 = s3(ti)
        next(g12, None)
        next(g3, None)
        next(g12, None)
        for _ in g3:
            pass
        for _ in g12:
            pass


def t5_causal_buckets(dist):
    n_buckets, max_dist = 32, 2048
    max_exact = n_buckets // 2
    dist = np.asarray(dist, dtype=np.int64)
    ratio = np.maximum(dist, max_exact) / max_exact
    large = max_exact + (np.log(ratio) / math.log(max_dist / max_exact) * (n_buckets - max_exact)).astype(np.int64)
    return np.where(dist < max_exact, dist, np.minimum(large, n_buckets - 1)).astype(np.int32)


def host_consts():
    p = np.arange(128)[:, None]
    f = np.arange(128)[None, :]
    blk = (p // 32) == (f // 32)
    cm = np.stack([(p == f), (p <= f), (f < p) & blk, (f >= p), (f < p) & ~blk]).astype(np.float32)
    return cm


def bias_layout(rel_bias):
    p = np.arange(128)[:, None]
    f = np.arange(256)[None, :]
    j = np.where(f < 128, f - p, f - p)
    valid = np.where(f < 128, j >= 0, j <= 128)
    idx = np.where(valid, j, 129).astype(np.int64)
    out = np.empty((24, 128, 256), np.float32)
    for gi, (window, dil) in enumerate(GROUPS):
        buckets = t5_causal_buckets(dil * np.arange(window // dil + 1))
        for hh in range(8):
            h = gi * 8 + hh
            ext = np.concatenate([rel_bias[buckets, h], np.array([NEG, NEG], np.float32)]).astype(np.float32)
            out[h] = ext[np.minimum(idx, 129)]
    return out


def bias_diag_layout(rel_bias):
    out = np.empty((24, ST, ST), np.float32)
    eye = np.eye(ST, dtype=bool)
    for gi, (window, dil) in enumerate(GROUPS):
        b0 = t5_causal_buckets(np.zeros(1))[0]
        for hh in range(8):
            h = gi * 8 + hh
            ext = np.array([rel_bias[b0, h], NEG], np.float32)
            out[h] = ext[np.where(eye, 0, 1)]
    return out


def make_in_maps(inp):
    f32 = lambda a: np.ascontiguousarray(np.asarray(a, dtype=np.float32))
    cm = host_consts()
    biasT = bias_layout(np.asarray(inp["rel_bias"], np.float32))
    shared = {
        "w_cond": f32(inp["w_cond"][0]),
        "bcondT": f32(np.asarray(inp["b_cond"][0]).reshape(24, 128).T),
        "w_in": f32(inp["w_in"][0]),
        "biasT": biasT,
        "biasD": bias_diag_layout(np.asarray(inp["rel_bias"], np.float32)),
        "convwT": f32(np.asarray(inp["conv_w"][0]).reshape(4, 24, 128).transpose(2, 1, 0)),
        "a_log": f32(inp["a_log"]).reshape(1, 8),
        "dt_bias": f32(inp["dt_bias"]).reshape(1, 8),
        "b_norm_w": f32(inp["b_norm_w"]).reshape(1, 128),
        "w_a": f32(inp["w_branch_a"][0]),
        "w_b": f32(inp["w_branch_b"][0]),
        "w_out": f32(inp["w_out"][0]),
        "ln_g": f32(inp["ln_g"]).reshape(1, D),
        "ln_b": f32(inp["ln_b"]).reshape(1, D),
        "cmats": cm,
    }
    maps = []
    for b in range(8):
        sl = slice(NS * b, NS * b + NS)
        c5 = np.concatenate([np.asarray(inp["c_prompt"][b:b + 1]), np.asarray(inp["c_sample"][sl])], axis=0)
        m = dict(shared)
        m["x"] = f32(inp["x_prompt"][b])
        m["xs"] = f32(np.asarray(inp["x_sample"][sl]).reshape(NS * ST, D))
        m["cT"] = f32(c5.T.reshape(8, 128, 1 + NS).transpose(1, 0, 2))
        m["kv128"] = f32(np.asarray(inp["cache_kv_w128"][0, sl]).reshape(NS, 128, 1024))
        m["kv512"] = f32(np.asarray(inp["cache_kv_w512"][0, sl]).reshape(NS, 512, 1024))
        m["kv2048"] = f32(np.asarray(inp["cache_kv_w2048"][0, sl]).reshape(NS, 2048, 1024))
        m["sconv"] = f32(np.asarray(inp["state_conv"][0, sl]).reshape(NS * 3, 3072))
        m["sdelta"] = f32(np.asarray(inp["state_delta"][0, sl]).reshape(NS * 8, 128, 128))
        maps.append(m)
    return maps


_NC_CACHE = {}


def kernel(**inputs):
    if "nc" not in _NC_CACHE:
        _NC_CACHE["nc"] = build()
    nc = _NC_CACHE["nc"]
    maps = make_in_maps(inputs)
    res = run_bass_kernel_spmd(nc, maps, core_ids=list(range(8))).results
    g = lambda name: [np.asarray(r[name], dtype=np.float32) for r in res]
    y = np.stack(g("y"))
    ys = np.concatenate([a.reshape(NS, ST, D) for a in g("ys")], axis=0)
    outs = [y, ys]
    for w in (128, 512, 2048):
        outs.append(np.stack([a.reshape(w, 2, 8, 64) for a in g("kvp%d" % w)])[None])
    outs.append(np.stack(g("convp"))[None])
    outs.append(np.stack(g("deltap"))[None])
    for w in (128, 512, 2048):
        outs.append(np.concatenate([a.reshape(NS, w, 2, 8, 64) for a in g("kvs%d" % w)], axis=0)[None])
    outs.append(np.concatenate(g("convs"), axis=0)[None])
    outs.append(np.concatenate([a.reshape(NS, 8, 128, 128) for a in g("deltas")], axis=0)[None])
    return tuple(outs)
```
